# Optimizing a Trainium2 kernel written in Bass

```python
import jax
import jax.numpy as jnp
from jax import lax
import numpy as np

D_MODEL = 1024
BATCH = 4
SEQ = 4096
DEPTH = 4

CTX_LEN = 256
GRID_W = 64
D_MIX = 1024

RG_WIDTH = 384
RG_BLOCKS = 6
RG_BLOCK = RG_WIDTH // RG_BLOCKS
RG_C = 8.0
CONV_K = 4

NA_HEADS = 6
NA_HEAD_DIM = 64
NA_WIDTH = NA_HEADS * NA_HEAD_DIM
NA_ROWS = 8
NA_COLS = 16

GDN_HEADS = 4
GDN_DK = 64
GDN_DV = 64
GDN_QK = GDN_HEADS * GDN_DK
GDN_WIDTH = GDN_HEADS * GDN_DV
GDN_CHUNK = 64
ROPE_BASE = 10000.0

CONV_CH = RG_WIDTH + 2 * GDN_QK + GDN_WIDTH
CONV_SPLIT = (RG_WIDTH, RG_WIDTH + GDN_QK, RG_WIDTH + 2 * GDN_QK)
REST_SIZES = (RG_WIDTH, NA_WIDTH, NA_WIDTH, NA_WIDTH, NA_WIDTH, GDN_WIDTH, 2 * GDN_HEADS, 2 * GDN_HEADS)
REST_SPLIT = tuple(int(v) for v in np.cumsum(REST_SIZES)[:-1])
D_IN = CONV_CH + sum(REST_SIZES)

DEEPNORM_ALPHA = (2.0 * DEPTH) ** 0.25
DEEPNORM_BETA = (8.0 * DEPTH) ** -0.25
LN_EPS = 1e-5
NORM_EPS = 1e-6
F32 = jnp.float32

kernel_name = 'hybrid_rglru_natten_gdn_prefix_dit'


def layer_norm(x, g, b):
    xf = x.astype(F32)
    mu = jnp.mean(xf, -1, keepdims=True)
    var = jnp.mean(jnp.square(xf - mu), -1, keepdims=True)
    return ((xf - mu) * lax.rsqrt(var + LN_EPS)).astype(x.dtype) * g + b


def split_heads(t, n_heads):
    return t.reshape(t.shape[:-1] + (n_heads, t.shape[-1] // n_heads))


def flip_seq(t, d):
    return t[:, ::-1] if d else t


def depthwise_conv_centred(u, w):
    k = w.shape[0]
    return lax.conv_general_dilated(u, w[:, None, :].astype(u.dtype), window_strides=(1,),
                                    padding=[(k // 2, k - 1 - k // 2)],
                                    dimension_numbers=('NWC', 'WIO', 'NWC'),
                                    feature_group_count=u.shape[-1])


def combined_projection(u, w_in, conv_w):
    p = u @ w_in
    conv = depthwise_conv_centred(p[..., :CONV_CH], conv_w)
    xa, qg, kg, vg = jnp.split(conv, CONV_SPLIT, axis=-1)
    za, qn, kn, vn, zn, zg, b_raw, a_raw = jnp.split(p[..., CONV_CH:], REST_SPLIT, axis=-1)
    return (xa, qg, kg, vg, za, qn, kn, vn, zn, zg, b_raw, a_raw)


def _linear_combine(left, right):
    a1, b1 = left
    a2, b2 = right
    return a1 * a2, a2 * b1 + b2


def block_diag_linear(u, w, b):
    ub = u.reshape(u.shape[:-1] + (RG_BLOCKS, RG_BLOCK))
    return jnp.einsum('blnd,nde->blne', ub, w).reshape(u.shape) + b


def rglru_scan(u, w_a, b_a, w_x, b_x, lam, h0):
    r = jax.nn.sigmoid(block_diag_linear(u, w_a, b_a))
    i = jax.nn.sigmoid(block_diag_linear(u, w_x, b_x))
    log_a = RG_C * r * jax.nn.log_sigmoid(lam)
    a = jnp.exp(log_a)
    b = jnp.sqrt(-jnp.expm1(2.0 * log_a)) * (i * u)
    b = b.at[:, 0].add(a[:, 0] * h0)
    _, h = lax.associative_scan(_linear_combine, (a, b), axis=1)
    return h


def rglru_bidir(u_lat, u_ctx, w_a, b_a, w_x, b_x, lam, ctx_out):
    dt = u_lat.dtype
    ul, uc = u_lat.astype(F32), u_ctx.astype(F32)
    h0 = jnp.zeros(uc.shape[:1] + uc.shape[2:], F32)
    lat, ctx = [], []
    for d in range(2):
        prm = (w_a[d], b_a[d], w_x[d], b_x[d], lam[d].astype(F32))
        hc = rglru_scan(flip_seq(uc, d), *prm, h0)
        hl = rglru_scan(flip_seq(ul, d), *prm, hc[:, -1])
        lat.append(flip_seq(hl, d))
        ctx.append(flip_seq(hc, d))
    y_ctx = (ctx[0] + ctx[1]).astype(dt) if ctx_out else None
    return (lat[0] + lat[1]).astype(dt), y_ctx


def neighbourhood_attend(q, k, v, k_ctx, v_ctx, rpb):
    bsz, seq, nh, hd = q.shape
    rows = seq // GRID_W
    wr = min(NA_ROWS, rows)
    q = q * (hd ** -0.5)
    qg = q.reshape(bsz, rows, GRID_W, nh, hd)
    kg = k.reshape(bsz, rows, GRID_W, nh, hd)
    vg = v.reshape(bsz, rows, GRID_W, nh, hd)
    row_start = jnp.clip(jnp.arange(rows) - wr // 2, 0, rows - wr)
    col_start = jnp.clip(jnp.arange(GRID_W) - NA_COLS // 2, 0, GRID_W - NA_COLS)
    col_idx = col_start[:, None] + jnp.arange(NA_COLS)[None, :]
    col_off = col_idx - jnp.arange(GRID_W)[:, None] + (NA_COLS - 1)
    rpb_col = rpb[:, :, col_off]
    n_loc = wr * NA_COLS

    def row_block(args):
        r, q_row = args
        r0 = row_start[r]
        k_rows = lax.dynamic_slice_in_dim(kg, r0, wr, axis=1)
        v_rows = lax.dynamic_slice_in_dim(vg, r0, wr, axis=1)
        k_win = k_rows[:, :, col_idx]
        v_win = v_rows[:, :, col_idx]
        row_off = r0 + jnp.arange(wr) - r + (NA_ROWS - 1)
        bias = jnp.transpose(rpb_col[:, row_off], (2, 0, 1, 3))[None]
        s_loc = jnp.einsum('bjhd,bwjmhd->bjhwm', q_row, k_win) + bias
        s_ctx = jnp.einsum('bjhd,bchd->bjhc', q_row, k_ctx)
        s = jnp.concatenate([s_loc.reshape(bsz, GRID_W, nh, n_loc), s_ctx], -1).astype(F32)
        p = jax.nn.softmax(s, axis=-1).astype(v.dtype)
        p_loc = p[..., :n_loc].reshape(bsz, GRID_W, nh, wr, NA_COLS)
        return (jnp.einsum('bjhwm,bwjmhd->bjhd', p_loc, v_win)
                + jnp.einsum('bjhc,bchd->bjhd', p[..., n_loc:], v_ctx))

    o = lax.map(row_block, (jnp.arange(rows), jnp.moveaxis(qg, 1, 0)))
    return jnp.moveaxis(o, 0, 1).reshape(bsz, seq, nh, hd)


def context_attend(q, k, v):
    s = jnp.einsum('bqhd,bkhd->bhqk', q * (q.shape[-1] ** -0.5), k).astype(F32)
    p = jax.nn.softmax(s, axis=-1).astype(v.dtype)
    return jnp.einsum('bhqk,bkhd->bqhd', p, v)


def axial_rope(t, rows_pos, cols_pos):
    half = t.shape[-1] // 2
    nf = half // 2
    inv_freq = ROPE_BASE ** (-jnp.arange(nf, dtype=F32) / nf)

    def rot(tp, pos):
        ang = pos.astype(F32)[:, None] * inv_freq[None, :]
        cos = jnp.cos(ang)[None, :, None, :]
        sin = jnp.sin(ang)[None, :, None, :]
        t1, t2 = tp[..., :nf], tp[..., nf:]
        return jnp.concatenate([t1 * cos - t2 * sin, t2 * cos + t1 * sin], -1)

    return jnp.concatenate([rot(t[..., :half], rows_pos), rot(t[..., half:], cols_pos)], -1)


def l2_normalise(t):
    return t * lax.rsqrt(jnp.sum(jnp.square(t), -1, keepdims=True) + NORM_EPS)


def gdn_qkv(qg, kg, vg, rope):
    q = l2_normalise(split_heads(jax.nn.silu(qg), GDN_HEADS).astype(F32))
    k = l2_normalise(split_heads(jax.nn.silu(kg), GDN_HEADS).astype(F32))
    v = split_heads(jax.nn.silu(vg), GDN_HEADS).astype(F32)
    if rope is not None:
        q = axial_rope(q, *rope)
        k = axial_rope(k, *rope)
    return q * (GDN_DK ** -0.5), k, v


def gdn_gates(b_raw, a_raw, a_log, dt_bias, d):
    sl = slice(d * GDN_HEADS, (d + 1) * GDN_HEADS)
    beta = jax.nn.sigmoid(b_raw[..., sl].astype(F32))
    g = -jnp.exp(a_log[d].astype(F32)) * jax.nn.softplus(a_raw[..., sl].astype(F32) + dt_bias[d].astype(F32))
    return beta, g


def gated_delta_chunked(q, k, v, beta, g, s0):
    bsz, seq, nh, _ = q.shape
    dv = v.shape[-1]
    n = seq // GDN_CHUNK

    def chunks(t):
        t = t.reshape((bsz, n, GDN_CHUNK) + t.shape[2:])
        return jnp.moveaxis(t, (1, 2), (0, 3))

    qc, kc, vc, bc, gc = (chunks(t) for t in (q, k, v, beta, g))
    gcum = jnp.cumsum(gc, axis=-1)
    lower = jnp.tril(jnp.ones((GDN_CHUNK, GDN_CHUNK), bool))
    strict = jnp.tril(jnp.ones((GDN_CHUNK, GDN_CHUNK), bool), -1)
    decay = jnp.exp(jnp.where(lower, gcum[..., :, None] - gcum[..., None, :], -jnp.inf))
    kb = kc * bc[..., None]
    a_mat = (jnp.where(strict, jnp.einsum('nbhid,nbhjd->nbhij', kb, kc) * decay, 0.0)
             + jnp.eye(GDN_CHUNK, dtype=F32))
    u = lax.linalg.triangular_solve(a_mat, vc * bc[..., None], left_side=True, lower=True, unit_diagonal=True)
    w = lax.linalg.triangular_solve(a_mat, kb * jnp.exp(gcum)[..., None], left_side=True, lower=True,
                                    unit_diagonal=True)
    qk = jnp.einsum('nbhid,nbhjd->nbhij', qc, kc) * decay

    def step(s, inp):
        q_i, k_i, u_i, w_i, qk_i, g_i = inp
        v_new = u_i - jnp.einsum('bhck,bhkv->bhcv', w_i, s)
        o = (jnp.einsum('bhck,bhkv->bhcv', q_i * jnp.exp(g_i)[..., None], s)
             + jnp.einsum('bhij,bhjv->bhiv', qk_i, v_new))
        g_last = g_i[..., -1:]
        s = (s * jnp.exp(g_last)[..., None]
             + jnp.einsum('bhck,bhcv->bhkv', k_i * jnp.exp(g_last - g_i)[..., None], v_new))
        return s, o

    s_fin, o = lax.scan(step, s0, (qc, kc, u, w, qk, gcum))
    o = jnp.moveaxis(o, (0, 3), (1, 2)).reshape(bsz, seq, nh, dv)
    return s_fin, o


def gdn_bidir(lat, ctx, a_log, dt_bias, ctx_out):
    q_l, k_l, v_l, b_l, a_l = lat
    q_c, k_c, v_c, b_c, a_c = ctx
    s0 = jnp.zeros((q_c.shape[0], GDN_HEADS, GDN_DK, GDN_DV), F32)
    outs_l, outs_c = [], []
    for d in range(2):
        beta_c, g_c = gdn_gates(b_c, a_c, a_log, dt_bias, d)
        beta_l, g_l = gdn_gates(b_l, a_l, a_log, dt_bias, d)
        s_c, o_c = gated_delta_chunked(*[flip_seq(t, d) for t in (q_c, k_c, v_c, beta_c, g_c)], s0)
        _, o_l = gated_delta_chunked(*[flip_seq(t, d) for t in (q_l, k_l, v_l, beta_l, g_l)], s_c)
        outs_l.append(flip_seq(o_l, d))
        outs_c.append(flip_seq(o_c, d))
    o_ctx = outs_c[0] + outs_c[1] if ctx_out else None
    return outs_l[0] + outs_l[1], o_ctx


def gdn_output(o, norm_w, z):
    o = o * lax.rsqrt(jnp.mean(jnp.square(o), -1, keepdims=True) + NORM_EPS) * norm_w.astype(F32)
    o = o * jax.nn.silu(split_heads(z, GDN_HEADS).astype(F32))
    return o.reshape(o.shape[:2] + (GDN_WIDTH,)).astype(z.dtype)


def merge_heads(t):
    return t.reshape(t.shape[:2] + (-1,))


def trunk_layer(x, xc, mod, mod_c, rope, w_in, conv_w, rg_wa, rg_ba, rg_wx, rg_bx, rg_lam, na_rpb,
                gdn_alog, gdn_dtb, gdn_nw, w_out, ln_g, ln_b, ctx_out):
    shift, scale, gate = jnp.split(mod, 3, axis=-1)
    shift_c, scale_c, gate_c = jnp.split(mod_c, 3, axis=-1)
    (a_l, qg_l, kg_l, vg_l, za_l, qn_l, kn_l, vn_l, zn_l, zg_l, br_l, ar_l) = combined_projection(
        x * (1 + scale[:, None]) + shift[:, None], w_in, conv_w)
    (a_c, qg_c, kg_c, vg_c, za_c, qn_c, kn_c, vn_c, zn_c, zg_c, br_c, ar_c) = combined_projection(
        xc * (1 + scale_c) + shift_c, w_in, conv_w)

    h_l, h_c = rglru_bidir(a_l, a_c, rg_wa, rg_ba, rg_wx, rg_bx, rg_lam, ctx_out)
    kb_c, vb_c = split_heads(kn_c, NA_HEADS), split_heads(vn_c, NA_HEADS)
    nb_l = neighbourhood_attend(split_heads(qn_l, NA_HEADS), split_heads(kn_l, NA_HEADS),
                                split_heads(vn_l, NA_HEADS), kb_c, vb_c, na_rpb)
    q_l, k_l, v_l = gdn_qkv(qg_l, kg_l, vg_l, rope)
    q_c, k_c, v_c = gdn_qkv(qg_c, kg_c, vg_c, None)
    o_l, o_c = gdn_bidir((q_l, k_l, v_l, br_l, ar_l), (q_c, k_c, v_c, br_c, ar_c), gdn_alog, gdn_dtb, ctx_out)

    y_l = jnp.concatenate([h_l * jax.nn.silu(za_l),
                           merge_heads(nb_l) * jax.nn.silu(zn_l),
                           gdn_output(o_l, gdn_nw, zg_l)], -1) @ w_out
    x_new = layer_norm(DEEPNORM_ALPHA * x + gate[:, None] * y_l, ln_g, ln_b)
    if not ctx_out:
        return x_new, None
    nb_c = context_attend(split_heads(qn_c, NA_HEADS), kb_c, vb_c)
    y_c = jnp.concatenate([h_c * jax.nn.silu(za_c),
                           merge_heads(nb_c) * jax.nn.silu(zn_c),
                           gdn_output(o_c, gdn_nw, zg_c)], -1) @ w_out
    xc_new = layer_norm(DEEPNORM_ALPHA * xc + gate_c * y_c, ln_g, ln_b)
    return x_new, xc_new


def setup_inputs(seed: int = 0) -> dict:
    key = jax.random.key(seed)
    ks = jax.random.split(key, 20)

    def nrm(k, shape, s):
        return jax.random.normal(k, shape, F32) * s

    x = nrm(ks[0], (BATCH, SEQ, D_MODEL), 1.0)
    c = nrm(ks[1], (BATCH, D_MODEL), 1.0)
    ctx = nrm(ks[2], (BATCH, CTX_LEN, D_MODEL), 1.0)
    c_ctx = nrm(ks[3], (D_MODEL,), 1.0)
    w_mod = nrm(ks[4], (DEPTH, D_MODEL, 3 * D_MODEL), 0.5 * D_MODEL ** -0.5)
    b_mod = nrm(ks[5], (DEPTH, 3 * D_MODEL), 0.02)
    w_in = nrm(ks[6], (DEPTH, D_MODEL, D_IN), D_MODEL ** -0.5)
    conv_w = nrm(ks[7], (DEPTH, CONV_K, CONV_CH), CONV_K ** -0.5)
    rg_wa = nrm(ks[8], (DEPTH, 2, RG_BLOCKS, RG_BLOCK, RG_BLOCK), RG_BLOCK ** -0.5)
    rg_ba = nrm(ks[9], (DEPTH, 2, RG_WIDTH), 0.02)
    rg_wx = nrm(ks[10], (DEPTH, 2, RG_BLOCKS, RG_BLOCK, RG_BLOCK), RG_BLOCK ** -0.5)
    rg_bx = nrm(ks[11], (DEPTH, 2, RG_WIDTH), 0.02)
    a_pow = jax.random.uniform(ks[12], (DEPTH, 2, RG_WIDTH), F32, 0.9, 0.999)
    s_lam = a_pow ** (1.0 / RG_C)
    rg_lam = jnp.log(s_lam) - jnp.log1p(-s_lam)
    na_rpb = nrm(ks[13], (DEPTH, NA_HEADS, 2 * NA_ROWS - 1, 2 * NA_COLS - 1), 0.1)
    gdn_alog = jnp.log(jax.random.uniform(ks[14], (DEPTH, 2, GDN_HEADS), F32, 1.0, 16.0))
    dt0 = jnp.exp(jax.random.uniform(ks[15], (DEPTH, 2, GDN_HEADS), F32, np.log(1e-3), np.log(1e-1)))
    gdn_dtb = dt0 + jnp.log(-jnp.expm1(-dt0))
    gdn_nw = 1.0 + nrm(ks[16], (DEPTH, GDN_DV), 0.02)
    w_out = nrm(ks[17], (DEPTH, D_MIX, D_MODEL), D_MIX ** -0.5 * DEEPNORM_BETA)
    ln_g = 1.0 + nrm(ks[18], (DEPTH, D_MODEL), 0.02)
    ln_b = nrm(ks[19], (DEPTH, D_MODEL), 0.02)
    return {'x': x, 'c': c, 'ctx': ctx, 'c_ctx': c_ctx, 'w_mod': w_mod, 'b_mod': b_mod, 'w_in': w_in,
            'conv_w': conv_w, 'rg_wa': rg_wa, 'rg_ba': rg_ba, 'rg_wx': rg_wx, 'rg_bx': rg_bx,
            'rg_lam': rg_lam, 'na_rpb': na_rpb, 'gdn_alog': gdn_alog, 'gdn_dtb': gdn_dtb,
            'gdn_nw': gdn_nw, 'w_out': w_out, 'ln_g': ln_g, 'ln_b': ln_b}


def reference(x, c, ctx, c_ctx, w_mod, b_mod, w_in, conv_w, rg_wa, rg_ba, rg_wx, rg_bx, rg_lam, na_rpb,
              gdn_alog, gdn_dtb, gdn_nw, w_out, ln_g, ln_b):
    pos = jnp.arange(x.shape[1])
    rope = (pos // GRID_W, pos % GRID_W)
    sc = jax.nn.silu(c)
    scc = jax.nn.silu(c_ctx)
    xc = ctx
    for l in range(DEPTH):
        mod = sc @ w_mod[l] + b_mod[l]
        mod_c = scc @ w_mod[l] + b_mod[l]
        x, xc = trunk_layer(x, xc, mod, mod_c, rope, w_in[l], conv_w[l], rg_wa[l], rg_ba[l], rg_wx[l],
                            rg_bx[l], rg_lam[l], na_rpb[l], gdn_alog[l], gdn_dtb[l], gdn_nw[l], w_out[l],
                            ln_g[l], ln_b[l], l < DEPTH - 1)
    return x
```

```python
from contextlib import ExitStack
import numpy as np
import concourse.bass as bass
import concourse.mybir as mybir
from concourse.bass_utils import run_bass_kernel_spmd

F32 = mybir.dt.float32
BF16 = mybir.dt.bfloat16
AF = mybir.ActivationFunctionType
ALU = mybir.AluOpType
AX = mybir.AxisListType

ENGS = ['pe', 'act', 'dve', 'pool', 'sp']
N_DMA_SEMS = 24


_UNIQ = [0]


def _sbt(nc, name, shape, dt):
    _UNIQ[0] += 1
    return nc.sbuf_tensor("%s_u%d" % (name, _UNIQ[0]), list(shape), dt)


class Builder:
    def __init__(self, nc):
        self.nc = nc
        self.stack = ExitStack()
        self.ops = {e: [] for e in ENGS}
        self.sem = {}
        self.cnt = {}
        self.waited = {e: {} for e in ENGS}
        self.last_w = {}
        self.readers = {}
        for e in ENGS:
            self.sem[e] = self.stack.enter_context(nc.semaphore('sem_' + e))
            self.cnt[e] = 0
        for k in range(N_DMA_SEMS):
            key = ('dma', k)
            self.sem[key] = self.stack.enter_context(nc.semaphore('sem_dma%d' % k))
            self.cnt[key] = 0
        self.dma_rr = 0
        self.n_ops = 0
        self.pending = {e: [] for e in ENGS}

    def sb(self, name, shape, dtype):
        return self.stack.enter_context(_sbt(self.nc, name, list(shape), dtype))

    def ps(self, name, shape, dtype):
        return self.stack.enter_context(self.nc.psum_tensor(name, list(shape), dtype))

    def _deps(self, eng, reads, writes):
        toks = []
        for r in reads:
            t = self.last_w.get(r)
            if t is not None:
                toks.append((t, False))
        for w in writes:
            t = self.last_w.get(w)
            if t is not None:
                toks.append((t, False))
            for sk, (val, peng) in self.readers.get(w, {}).items():
                toks.append(((sk, val, peng), True))
        waits = []
        for (sk, val, peng), is_war in toks:
            if peng == eng:
                if eng == 'pe' or is_war:
                    continue
            if self.waited[eng].get(sk, 0) >= val:
                continue
            self.waited[eng][sk] = val
            waits.append((sk, val))
        best = {}
        for sk, val in waits:
            best[sk] = max(best.get(sk, 0), val)
        return list(best.items())

    def _commit(self, tok, reads, writes):
        for w in writes:
            self.last_w[w] = tok
            self.readers[w] = {}
        for r in reads:
            d = self.readers.setdefault(r, {})
            sk, val, peng = tok
            if d.get(sk, (0, None))[0] < val:
                d[sk] = (val, peng)

    def barrier(self):
        for e in ENGS:
            for k, v in self.cnt.items():
                if v > 0 and self.waited[e].get(k, 0) < v:
                    self.waited[e][k] = v
                    self.pending[e].append((k, v))

    def _take_pending(self, eng, waits):
        if self.pending[eng]:
            best = dict(waits)
            for k, v in self.pending[eng]:
                best[k] = max(best.get(k, 0), v)
            self.pending[eng] = []
            return list(best.items())
        return waits

    def op(self, eng, fn, reads=(), writes=()):
        waits = self._take_pending(eng, self._deps(eng, reads, writes))
        self.cnt[eng] += 1
        tok = (eng, self.cnt[eng], eng)
        self.ops[eng].append((waits, fn, eng, 1))
        self._commit(tok, reads, writes)
        self.n_ops += 1

    def dma(self, q, out, in_, reads=(), writes=(), **kw):
        k = self.dma_rr
        self.dma_rr = (self.dma_rr + 1) % N_DMA_SEMS
        key = ('dma', k)
        waits = self._deps(q, reads, writes)
        if self.cnt[key] > 0 and self.waited[q].get(key, 0) < self.cnt[key]:
            self.waited[q][key] = self.cnt[key]
            waits = [w for w in waits if w[0] != key] + [(key, self.cnt[key])]
        waits = self._take_pending(q, waits)
        self.cnt[key] += 16
        tok = (key, self.cnt[key], 'dma')
        self.ops[q].append((waits, lambda e: e.dma_start(out=out, in_=in_, **kw), key, 16))
        self._commit(tok, reads, writes)
        self.n_ops += 1

    def finish(self):
        nc = self.nc
        fin = [(k, v) for k, v in self.cnt.items() if v > 0]
        ops = self.ops
        sem = self.sem

        def replay(name):
            def run(e):
                for waits, fn, sk, inc in ops[name]:
                    for wk, wv in waits:
                        e.wait_ge(sem[wk], wv)
                    ins = fn(e)
                    ins.then_inc(sem[sk], inc)
                if name == 'sp':
                    for k, v in fin:
                        e.wait_ge(sem[k], v)
            return run

        with nc.Block() as block:
            block.tensor(replay('pe'))
            block.scalar(replay('act'))
            block.vector(replay('dve'))
            block.gpsimd(replay('pool'))
            block.sync(replay('sp'))
        self.stack.close()


DEPTH = 4
D = 1024
KC = 8
T = 4352
NT = 34
GCOLS = 1672
ALPHA = (2.0 * DEPTH) ** 0.25
LN_EPS = 1e-5
NORM_EPS = 1e-6
BIG = 30000.0
BLOCKS = [(0, 256)] + [(256 + 512 * i, 512) for i in range(8)]
PADW = 4360
C_ID, C_ONE, C_BONE, C_RM, C_UF, C_UB, C_SF, C_SB, C_MF, C_MB = [128 * i for i in range(10)]
NCONST = 1280
ONLY = {'rg', 'na', 'gdn'}


def pcol(t):
    return t + 2 if t < 256 else t + 6


class E:
    def __init__(self, B):
        self.B = B

    def mm(self, out, lhsT, rhs, st, sp, r, w):
        self.B.op('pe', lambda e: e.matmul(out, lhsT=lhsT, rhs=rhs, start=st, stop=sp), r, w)

    def tr(self, out, in_, ident, r, w):
        self.B.op('pe', lambda e: e.transpose(out=out, in_=in_, identity=ident), r, w)

    def act(self, out, in_, func, r, w, bias=None, scale=None, accum=None):
        kw = {}
        if bias is not None:
            kw['bias'] = bias
        if scale is not None:
            kw['scale'] = scale
        if accum is not None:
            kw['accum_out'] = accum
        self.B.op('act', lambda e: e.activation(out=out, in_=in_, func=func, **kw), r, w)

    def tt(self, eng, out, in0, in1, op, r, w):
        self.B.op(eng, lambda e: e.tensor_tensor(out=out, in0=in0, in1=in1, op=op), r, w)

    def ts(self, eng, out, in0, s1, s2, op0, op1, r, w):
        if op1 is None:
            self.B.op(eng, lambda e: e.tensor_scalar(out=out, in0=in0, scalar1=s1, scalar2=None, op0=op0), r, w)
        else:
            self.B.op(eng, lambda e: e.tensor_scalar(out=out, in0=in0, scalar1=s1, scalar2=s2, op0=op0, op1=op1), r, w)

    def stt(self, out, in0, sc, in1, op0, op1, r, w):
        self.B.op('dve', lambda e: e.scalar_tensor_tensor(out=out, in0=in0, scalar=sc, in1=in1, op0=op0, op1=op1), r, w)

    def cp(self, eng, out, in_, r, w):
        if eng == 'act':
            self.B.op('act', lambda e: e.activation(out=out, in_=in_, func=AF.Copy), r, w)
        else:
            self.B.op(eng, lambda e: e.tensor_copy(out=out, in_=in_), r, w)

    def red(self, out, in_, op, r, w, negate=False):
        self.B.op('dve', lambda e: e.tensor_reduce(out=out, in_=in_, axis=AX.X, op=op, negate=negate), r, w)

    def recip(self, out, in_, r, w):
        self.B.op('dve', lambda e: e.reciprocal(out=out, in_=in_), r, w)

    def memset(self, eng, ap, val, w):
        self.B.op(eng, lambda e: e.memset(ap, val), (), w)

    def scan(self, out, d0, d1, init, r, w):
        self.B.op('dve', lambda e: e.tensor_tensor_scan(out=out, data0=d0, data1=d1, initial=init,
                                                        op0=ALU.mult, op1=ALU.add), r, w)


def rev(ap):
    return ap[:, ::-1]


class Prog:
    def __init__(self, G, steps, x_ext_out, final, debug=False):
        self.G = G
        self.steps = steps
        self.final = final
        nc = self.nc = bass.Bass("TRN2", target_bir_lowering=False)
        self.B = B = Builder(nc)
        self.e = E(B)
        self.debug = debug
        self.dbg_outs = {}
        L = DEPTH

        def din(name, shape, dt=F32):
            return nc.dram_tensor(name, list(shape), dt, kind="ExternalInput").ap()

        def dout(name, shape, dt=F32):
            return nc.dram_tensor(name, list(shape), dt, kind="ExternalOutput").ap()

        def dint(name, shape, dt=F32):
            return nc.dram_tensor(name, list(shape), dt, kind="Internal").ap()

        self.dout = dout
        self.x_in = din("x_in", [T, D])
        self.cc = din("cc", [128, KC, 2])
        self.consts_d = din("consts", [128, NCONST])
        self.w_mod = din("w_mod", [L, D, 3 * D])
        self.bmod_rep = din("bmod_rep", [L, 128, 3 * D])
        self.w_in = din("w_in", [L, D, G * GCOLS])
        self.convw = din("convw", [L, G, 128, 5, 4])
        self.rgw = din("rgw", [L, G, 128, 2, 2, 192])
        self.rgb = din("rgb", [L, G, 128, 2, 2, 3])
        self.nab = din("nab", [L, G, 3, 128, 5, 640])
        self.galog = din("galog", [L, G, 128, NT, 4])
        self.gdtb = din("gdtb", [L, G, 128, NT, 4])
        self.gnw = din("gnw", [L, 128, 128])
        self.rope_cos = din("rope_cos", [128, 4096])
        self.rope_sin = din("rope_sin", [128, 4096])
        self.w_out = din("w_out", [L, D, D])
        self.lng_rep = din("lng_rep", [L, 128, D])
        self.lnb_rep = din("lnb_rep", [L, 128, D])
        has_front = any(s[0] == 'front' for s in steps)
        has_back = any(s[0] == 'back' for s in steps)
        self.uT_d = dint("uT_d", [128, KC, T], BF16)
        if steps[0][0] == 'back':
            self.yc_in = din("yc_in", [D, T], BF16)
        else:
            self.yc_in = None
        if has_front and x_ext_out:
            self.yc_out = dout("yc_out", [512 * G, T], BF16)
        else:
            self.yc_out = None
        self.yc_int = dint("yc_int", [D, T], BF16) if not x_ext_out else None
        self.x_ext_out = x_ext_out
        if x_ext_out and has_back and not final:
            self.x_out = dout("x_out", [T, D])
        else:
            self.x_out = None
        self.x_scr = [dint("x_scr0", [T, D]), dint("x_scr1", [T, D])] if not x_ext_out else None
        self.out_d = dout("out", [4096, D]) if final else None

        self.banks = B.ps("banks", [128, 8, 512], F32)
        self.cst = B.sb("cst", [128, NCONST], F32)
        self.identb = B.sb("identb", [128, 128], BF16)
        self.screp = B.sb("screp", [128, 2, KC, 128], F32)
        self.modcol = B.sb("modcol", [128, 16, 2], F32)
        self.gate_bc = B.sb("gate_bc", [128, 2, D], F32)
        self.build()

    def c(self, off, n=128, p0=0, p1=128):
        return self.cst[p0:p1, off:off + n]

    def bk(self, i):
        return ('bk', i)

    def dump(self, name, ap, shape, reads, dt=F32):
        if not self.debug:
            return
        d = self.nc.dram_tensor("dbg_" + name, list(shape), dt, kind="ExternalOutput").ap()
        self.dbg_outs[name] = d
        self.B.dma('pool', d, ap, reads=reads, writes=[('dbg', name)])

    def build(self):
        B, e = self.B, self.e
        B.dma('sp', self.cst[:], self.consts_d, writes=['cst'])
        e.cp('dve', self.identb[:], self.c(C_ID), ['cst'], ['identb'])
        with ExitStack() as ph:
            cct = ph.enter_context(_sbt(self.nc, "cct", [128, KC, 2], F32))
            sct = ph.enter_context(_sbt(self.nc, "sct", [128, KC, 2], F32))
            B.dma('sp', cct[:], self.cc, writes=['cct'])
            e.act(sct[:], cct[:], AF.Silu, ['cct'], ['sct'])
            for j in range(2):
                for kc in range(KC):
                    e.ts('dve', self.screp[:, j, kc, :], self.c(C_ONE), sct[:, kc, j:j + 1], None, ALU.mult, None,
                         ['cst', 'sct'], ['screp'])
            B.barrier()
        x_cur = self.x_in
        nxt = 0
        for kind, l in self.steps:
            last = (l == DEPTH - 1)
            if kind == 'front':
                yc = self.yc_out if self.x_ext_out else self.yc_int
                self.mod_phase(l)
                self.u_phase(l, x_cur)
                for g in range(self.G):
                    row0 = 512 * g
                    if 'rg' in ONLY:
                        self.rg_pass(l, g, yc, row0)
                    if 'na' in ONLY:
                        self.na_pass(l, g, yc, row0 + 192, ctx_out=not last)
                    if 'gdn' in ONLY:
                        self.gdn_pass(l, g, yc, row0 + 384)
            else:
                if self.yc_in is not None and (kind, l) == self.steps[0]:
                    yc = self.yc_in
                    self.mod_phase(l, gate_only=True)
                else:
                    yc = self.yc_int
                if last:
                    self.back_phase(l, x_cur, yc, self.out_d, True)
                else:
                    if self.x_ext_out:
                        xn = self.x_out
                    else:
                        xn = self.x_scr[nxt]
                        nxt ^= 1
                    self.back_phase(l, x_cur, yc, xn, False)
                    x_cur = xn
        B.finish()

    def mod_phase(self, l, gate_only=False):
        B, e, nc = self.B, self.e, self.nc
        with ExitStack() as ph:
            wm = [ph.enter_context(_sbt(nc, "wm%d" % i, [128, 3 * D], F32)) for i in range(2)]
            modbc = ph.enter_context(_sbt(nc, "modbc", [128, 3 * D], F32))
            bmod = ph.enter_context(_sbt(nc, "bmod", [128, 3 * D], F32))
            junk = ph.enter_context(_sbt(nc, "junk", [128, 128], F32))
            B.dma('sp', bmod[:], self.bmod_rep[l], writes=['bmod'])
            for j in range(2):
                for kc in range(KC):
                    b = kc % 2
                    B.dma('sp', wm[b][:], self.w_mod[l, kc * 128:(kc + 1) * 128, :], writes=[('wm', b)])
                    for nb in range(6):
                        e.mm(self.banks[:, nb, :], self.screp[:, j, kc, :], wm[b][:, nb * 512:(nb + 1) * 512],
                             kc == 0, kc == KC - 1, [('wm', b), 'screp'], [self.bk(nb)])
                for nb in range(6):
                    e.tt('dve', modbc[:, nb * 512:(nb + 1) * 512], self.banks[:, nb, :],
                         bmod[:, nb * 512:(nb + 1) * 512], ALU.add, ['bmod'], ['modbc', self.bk(nb)])
                if not gate_only:
                    for ch in range(16):
                        e.tt('dve', junk[:], modbc[:, ch * 128:(ch + 1) * 128], self.c(C_ID), ALU.mult,
                             ['modbc', 'cst'], ['junk'])
                        e.red(self.modcol[:, ch, j:j + 1], junk[:], ALU.add, ['junk'], ['modcol'])
                e.cp('pool', self.gate_bc[:, j, :], modbc[:, 2 * D:3 * D], ['modbc'], ['gate_bc'])
            if not gate_only:
                e.ts('dve', self.modcol[:, 8:16, :], self.modcol[:, 8:16, :], 1.0, None, ALU.add, None,
                     ['modcol'], ['modcol'])
            B.barrier()

    def u_phase(self, l, x_cur):
        B, e, nc = self.B, self.e, self.nc
        with ExitStack() as ph:
            xt = [ph.enter_context(_sbt(nc, "u_xt%d" % i, [128, D], F32)) for i in range(2)]
            ut = [ph.enter_context(_sbt(nc, "u_ut%d" % i, [128, KC, 128], BF16)) for i in range(2)]
            for ti in range(NT):
                j = 1 if ti < 2 else 0
                b = ti % 2
                B.dma('sp', xt[b][:], x_cur[ti * 128:(ti + 1) * 128, :], reads=['X'], writes=[('u_xt', b)])
                for kc in range(KC):
                    bank = 2 * b + kc // 4
                    e.tr(self.banks[:, bank, (kc % 4) * 128:(kc % 4 + 1) * 128], xt[b][:, kc * 128:(kc + 1) * 128],
                         self.c(C_ID), [('u_xt', b), 'cst'], [self.bk(bank)])
                for kc in range(KC):
                    bank = 2 * b + kc // 4
                    src = self.banks[:, bank, (kc % 4) * 128:(kc % 4 + 1) * 128]
                    if kc % 2 == 0:
                        e.ts('dve', ut[b][:, kc, :], src, self.modcol[:, 8 + kc, j:j + 1], self.modcol[:, kc, j:j + 1],
                             ALU.mult, ALU.add, ['modcol'], [('u_ut', b), self.bk(bank)])
                    else:
                        e.act(ut[b][:, kc, :], src, AF.Identity, ['modcol'], [('u_ut', b), self.bk(bank)],
                              bias=self.modcol[:, kc, j:j + 1], scale=self.modcol[:, 8 + kc, j:j + 1])
                B.dma('pool', self.uT_d[:, :, ti * 128:(ti + 1) * 128], ut[b][:], reads=[('u_ut', b)], writes=['uT'])
            B.barrier()

    def load_w(self, ph, l, g, col0, ncols, tag):
        B, e, nc = self.B, self.e, self.nc
        wsb = ph.enter_context(_sbt(nc, "wsb_" + tag, [128, KC, ncols], BF16))
        stg = [ph.enter_context(_sbt(nc, "wstg%d_%s" % (i, tag), [128, ncols], F32)) for i in range(2)]
        c0 = g * GCOLS + col0
        for kc in range(KC):
            b = kc % 2
            B.dma('sp', stg[b][:], self.w_in[l, kc * 128:(kc + 1) * 128, c0:c0 + ncols], writes=[('wstg', b)])
            e.cp('pool', wsb[:, kc, :], stg[b][:], [('wstg', b)], ['wsb'])
        return wsb

    def proj(self, ph, wsb, fm_tiles, tm_ranges, fm_sink, tm_sink):
        B, e, nc = self.B, self.e, self.nc
        ub = [ph.enter_context(_sbt(nc, "ub%d" % i, [128, KC, 512], BF16)) for i in range(2)]
        rr = 0
        for bi, (s, n) in enumerate(BLOCKS):
            b = bi % 2
            B.dma('sp', ub[b][:, :, 0:n], self.uT_d[:, :, s:s + n], reads=['uT'], writes=[('ub', b)])
            for idx, (c0, M) in enumerate(fm_tiles):
                bank = rr % 4
                rr += 1
                for kc in range(KC):
                    e.mm(self.banks[0:M, bank, 0:n], wsb[:, kc, c0:c0 + M], ub[b][:, kc, 0:n], kc == 0, kc == KC - 1,
                         ['wsb', ('ub', b)], [self.bk(bank)])
                fm_sink(idx, s, n, self.banks[0:M, bank, 0:n], self.bk(bank))
            for tl in range(n // 128):
                ti = s // 128 + tl
                for idx, (c0, W) in enumerate(tm_ranges):
                    bank = rr % 4
                    rr += 1
                    for kc in range(KC):
                        e.mm(self.banks[:, bank, 0:W], ub[b][:, kc, tl * 128:(tl + 1) * 128], wsb[:, kc, c0:c0 + W],
                             kc == 0, kc == KC - 1, ['wsb', ('ub', b)], [self.bk(bank)])
                    tm_sink(idx, ti, self.banks[:, bank, 0:W], self.bk(bank))

    def conv(self, P, out, cw, tile, np_, r, w):
        e = self.e
        for (s, n) in [(0, 256), (256, 4096)]:
            base = pcol(s) - 2
            o = out[0:np_, s:s + n]
            e.ts('dve', o, P[0:np_, base:base + n], cw[0:np_, tile, 0:1], None, ALU.mult, None, r, w)
            for j in range(1, 4):
                e.stt(o, P[0:np_, base + j:base + j + n], cw[0:np_, tile, j:j + 1], o, ALU.mult, ALU.add, r, w)

    def zero_pads(self, P, key):
        e = self.e
        for a, b in [(0, 2), (258, 262), (4358, 4360)]:
            e.memset('pool', P[:, a:b], 0.0, [key])

    def rg_pass(self, l, g, yc, row0):
        B, e, nc = self.B, self.e, self.nc
        with ExitStack() as p0:
            xa = [p0.enter_context(_sbt(nc, "rg_xa%d" % i, [128, T], F32)) for i in range(2)]
            zs = [p0.enter_context(_sbt(nc, "rg_zs%d" % i, [128, T], BF16)) for i in range(2)]
            cw = p0.enter_context(_sbt(nc, "rg_cw", [128, 5, 4], F32))
            B.dma('sp', cw[:], self.convw[l, g], writes=['cw'])
            NP = [128, 64]
            with ExitStack() as ph:
                wsb = self.load_w(ph, l, g, 0, 384, "rg")
                P = [ph.enter_context(_sbt(nc, "rg_P%d" % i, [128, PADW], F32)) for i in range(2)]
                for i in range(2):
                    self.zero_pads(P[i], ('rgP', i))

                def fm_sink(idx, s, n, ps, bkey):
                    if idx < 2:
                        e.cp('act', P[idx][0:NP[idx], pcol(s):pcol(s) + n], ps, [], [bkey, ('rgP', idx)])
                    else:
                        i = idx - 2
                        e.act(zs[i][0:NP[i], s:s + n], ps, AF.Silu, [], [bkey, ('rg_zs', i)])

                self.proj(ph, wsb, [(0, 128), (128, 64), (192, 128), (320, 64)], [], fm_sink, None)
                for i in range(2):
                    self.conv(P[i], xa[i], cw, i, NP[i], [('rgP', i), 'cw'], [('rg_xa', i)])
                B.barrier()
            if self.debug:
                self.dump("rg_xa0_%d_%d" % (l, g), xa[0][:], [128, T], [('rg_xa', 0)])
            with ExitStack() as ph:
                hb = [ph.enter_context(_sbt(nc, "rg_hb%d" % i, [128, T], F32)) for i in range(2)]
                w = ph.enter_context(_sbt(nc, "rg_w", [128, 2, 2, 192], F32))
                bb = ph.enter_context(_sbt(nc, "rg_b", [128, 2, 2, 3], F32))
                c8 = ph.enter_context(_sbt(nc, "rg_c8", [128, 2, 2], F32))
                tmp = {}
                for nm in ['r', 'i', 'a', 's']:
                    for i in range(2):
                        tmp[(nm, i)] = ph.enter_context(_sbt(nc, "rg_t%s%d" % (nm, i), [128, 512], F32))
                hf = [[ph.enter_context(_sbt(nc, "rg_hf%d_%d" % (i, k), [128, 512], F32)) for k in range(2)]
                      for i in range(2)]
                yb = [[ph.enter_context(_sbt(nc, "rg_yb%d_%d" % (i, k), [128, 512], BF16)) for k in range(2)]
                      for i in range(2)]
                B.dma('sp', w[:], self.rgw[l, g], writes=['rg_w'])
                B.dma('sp', bb[:], self.rgb[l, g], writes=['rg_b'])
                e.act(c8[:], bb[:, :, :, 2], AF.Exp, ['rg_b'], ['rg_c8'], scale=-1.0)
                e.act(c8[:], c8[:], AF.Ln, ['rg_c8'], ['rg_c8'], bias=1.0)
                e.ts('dve', c8[:], c8[:], -8.0, None, ALU.mult, None, ['rg_c8'], ['rg_c8'])
                wcols = [(0, 128), (128, 192)]
                for d in (1, 0):
                    order = BLOCKS if d == 0 else [BLOCKS[0]] + BLOCKS[:0:-1]
                    prev = [None, None]
                    for bi, (s, n) in enumerate(order):
                        for i in range(2):
                            np_ = NP[i]
                            tr_, ti_, ta_, ts_ = (tmp[(nm, i)][0:np_, 0:n] for nm in ['r', 'i', 'a', 's'])
                            kr, ki, ka, ks = (('rg_t', nm, i) for nm in ['r', 'i', 'a', 's'])
                            xin = xa[i][0:np_, s:s + n]
                            b0, b1 = 4 + 2 * i, 5 + 2 * i
                            c0, c1 = wcols[i]
                            e.mm(self.banks[0:np_, b0, 0:n], w[0:np_, d, 0, c0:c1], xin, True, True,
                                 ['rg_w', ('rg_xa', i)], [self.bk(b0)])
                            e.mm(self.banks[0:np_, b1, 0:n], w[0:np_, d, 1, c0:c1], xin, True, True,
                                 ['rg_w', ('rg_xa', i)], [self.bk(b1)])
                            e.act(tr_, self.banks[0:np_, b0, 0:n], AF.Sigmoid, ['rg_b'], [kr, self.bk(b0)],
                                  bias=bb[0:np_, i, d, 0:1])
                            e.act(ti_, self.banks[0:np_, b1, 0:n], AF.Sigmoid, ['rg_b'], [ki, self.bk(b1)],
                                  bias=bb[0:np_, i, d, 1:2])
                            e.act(ta_, tr_, AF.Exp, [kr, 'rg_c8'], [ka], scale=c8[0:np_, i, d:d + 1])
                            e.tt('pool', ts_, ta_, ta_, ALU.mult, [ka], [ks])
                            e.act(ts_, ts_, AF.Sqrt, [ks], [ks], bias=1.0, scale=-1.0)
                            e.tt('dve', ti_, ts_, ti_, ALU.mult, [ks, ki], [ki])
                            e.tt('pool', ti_, ti_, xin, ALU.mult, [ki, ('rg_xa', i)], [ki])
                            init = prev[i] if prev[i] is not None else 0.0
                            if d == 1:
                                dst = hb[i][0:np_, s:s + n]
                                e.scan(rev(dst), rev(ta_), rev(ti_), init, [ka, ki, ('rg_hb', i)], [('rg_hb', i)])
                                prev[i] = hb[i][0:np_, s:s + 1]
                            else:
                                k = bi % 2
                                dst = hf[i][k][0:np_, 0:n]
                                e.scan(dst, ta_, ti_, init, [ka, ki, ('rg_hf', i, 1 - k)], [('rg_hf', i, k)])
                                prev[i] = hf[i][k][0:np_, n - 1:n]
                                yt = yb[i][k][0:np_, 0:n]
                                e.tt('pool', ts_, dst, hb[i][0:np_, s:s + n], ALU.add, [('rg_hf', i, k), ('rg_hb', i)], [ks])
                                e.tt('dve', yt, ts_, zs[i][0:np_, s:s + n], ALU.mult, [ks, ('rg_zs', i)], [('rg_yb', i, k)])
                                r0 = row0 + 128 * i
                                B.dma('pool', yc[r0:r0 + np_, s:s + n], yt, reads=[('rg_yb', i, k)], writes=['YC'])
                B.barrier()

    def na_pass(self, l, g, yc, row0, ctx_out):
        B, e, nc = self.B, self.e, self.nc
        NP = [128, 64]
        with ExitStack() as p0:
            qT = [p0.enter_context(_sbt(nc, "na_q%d" % i, [128, T], BF16)) for i in range(2)]
            kT = [p0.enter_context(_sbt(nc, "na_k%d" % i, [128, T], BF16)) for i in range(2)]
            zs = [p0.enter_context(_sbt(nc, "na_zs%d" % i, [128, T], BF16)) for i in range(2)]
            v_tm = p0.enter_context(_sbt(nc, "na_v", [128, NT, 192], BF16))
            with ExitStack() as ph:
                wsb = self.load_w(ph, l, g, 384, 768, "na")

                def fm_sink(idx, s, n, ps, bkey):
                    i = idx % 2
                    if idx < 2:
                        e.act(qT[i][0:NP[i], s:s + n], ps, AF.Copy, [], [bkey, 'na_q'], scale=0.125)
                    elif idx < 4:
                        e.cp('dve', kT[i][0:NP[i], s:s + n], ps, [], [bkey, 'na_k'])
                    else:
                        e.act(zs[i][0:NP[i], s:s + n], ps, AF.Silu, [], [bkey, 'na_zs'])

                def tm_sink(idx, ti, ps, bkey):
                    e.cp('dve', v_tm[:, ti, :], ps, [], [bkey, 'na_v'])

                self.proj(ph, wsb, [(0, 128), (128, 64), (192, 128), (320, 64), (384, 128), (512, 64)],
                          [(576, 192)], fm_sink, tm_sink)
                B.barrier()
            with ExitStack() as ph:
                nab = ph.enter_context(_sbt(nc, "na_bias", [128, 5, 640], F32))
                S = [ph.enter_context(_sbt(nc, "na_S%d" % i, [128, 896], F32)) for i in range(2)]
                Pn = [ph.enter_context(_sbt(nc, "na_Pn%d" % i, [128, 896], BF16)) for i in range(2)]
                PT = [ph.enter_context(_sbt(nc, "na_PT%d" % i, [128, 896], BF16)) for i in range(2)]
                st = [ph.enter_context(_sbt(nc, "na_st%d" % i, [128, 4], F32)) for i in range(2)]
                yb = [ph.enter_context(_sbt(nc, "na_yb%d" % i, [64, 128], BF16)) for i in range(2)]
                it = 0
                for h in range(3):
                    ti_ = 0 if h < 2 else 1
                    hr0 = 64 * (h % 2)
                    q_ = qT[ti_]
                    k_ = kT[ti_]
                    z_ = zs[ti_]
                    B.dma('sp', nab[:], self.nab[l, g, h], writes=['na_bias'])
                    qts = [(qt, True) for qt in range(32)]
                    if ctx_out:
                        qts += [(0, False), (1, False)]
                    for qt, lat in qts:
                        p = it % 2
                        it += 1
                        if lat:
                            cls = {0: 0, 1: 1, 30: 3, 31: 4}.get(qt, 2)
                            kt0 = min(max(qt - 2, 0), 27)
                            qcol = (2 + qt) * 128
                            kcol = (2 + kt0) * 128
                            ktiles = [2 + kt0 + c for c in range(5)] + [0, 1]
                            nk = 896
                        else:
                            qcol = qt * 128
                            ktiles = [0, 1]
                            nk = 256
                        qa = q_[hr0:hr0 + 64, qcol:qcol + 128]
                        b0, b1 = 2 * p, 2 * p + 1
                        kS, kP, kPT, kst = ('na_S', p), ('na_Pn', p), ('na_PT', p), ('na_st', p)
                        if lat:
                            e.mm(self.banks[:, b0, 0:512], qa, k_[hr0:hr0 + 64, kcol:kcol + 512], True, True,
                                 ['na_q', 'na_k'], [self.bk(b0)])
                            e.mm(self.banks[:, b1, 0:128], qa, k_[hr0:hr0 + 64, kcol + 512:kcol + 640], True, True,
                                 ['na_q', 'na_k'], [self.bk(b1)])
                            e.mm(self.banks[:, b1, 128:384], qa, k_[hr0:hr0 + 64, 0:256], True, True,
                                 ['na_q', 'na_k'], [self.bk(b1)])
                            e.tt('dve', S[p][:, 0:512], self.banks[:, b0, :], nab[:, cls, 0:512], ALU.add,
                                 ['na_bias'], [kS, self.bk(b0)])
                            e.tt('dve', S[p][:, 512:640], self.banks[:, b1, 0:128], nab[:, cls, 512:640], ALU.add,
                                 ['na_bias'], [kS, self.bk(b1)])
                            e.cp('act', S[p][:, 640:896], self.banks[:, b1, 128:384], [], [kS, self.bk(b1)])
                        else:
                            e.mm(self.banks[:, b0, 0:256], qa, k_[hr0:hr0 + 64, 0:256], True, True,
                                 ['na_q', 'na_k'], [self.bk(b0)])
                            e.cp('act', S[p][:, 0:256], self.banks[:, b0, 0:256], [], [kS, self.bk(b0)])
                        e.red(st[p][:, 0:1], S[p][:, 0:nk], ALU.max, [kS], [kst], negate=True)
                        e.act(S[p][:, 0:nk], S[p][:, 0:nk], AF.Exp, [kS, kst], [kS, (kst, 'sum')], bias=st[p][:, 0:1],
                              scale=1.0, accum=st[p][:, 1:2])
                        e.recip(st[p][:, 2:3], st[p][:, 1:2], [(kst, 'sum')], [(kst, 'ri')])
                        e.ts('pool', Pn[p][:, 0:nk], S[p][:, 0:nk], st[p][:, 2:3], None, ALU.mult, None,
                             [kS, (kst, 'ri')], [kP])
                        pbf = self.banks[:, 4 + p, :].bitcast(BF16)
                        nch = nk // 128
                        for c in range(nch):
                            e.tr(pbf[:, c * 128:(c + 1) * 128], Pn[p][:, c * 128:(c + 1) * 128], self.identb[:],
                                 [kP, 'identb'], [self.bk(4 + p)])
                        e.cp('act', PT[p][:, 0:nk], pbf[:, 0:nk], [], [kPT, self.bk(4 + p)])
                        for c in range(nch):
                            e.mm(self.banks[0:64, 6 + p, 0:128], v_tm[:, ktiles[c], h * 64:(h + 1) * 64], PT[p][:, c * 128:(c + 1) * 128],
                                 c == 0, c == nch - 1, ['na_v', kPT], [self.bk(6 + p)])
                        e.tt('dve', yb[p][:], self.banks[0:64, 6 + p, 0:128], z_[hr0:hr0 + 64, qcol:qcol + 128], ALU.mult,
                             ['na_zs'], [('na_yb', p), self.bk(6 + p)])
                        r0 = row0 + 64 * h
                        B.dma('pool', yc[r0:r0 + 64, qcol:qcol + 128], yb[p][:], reads=[('na_yb', p)], writes=['YC'])
                B.barrier()

    def gdn_pass(self, l, g, yc, row0):
        B, e, nc = self.B, self.e, self.nc
        bnk = self.banks
        with ExitStack() as p0:
            qT = p0.enter_context(_sbt(nc, "gd_q", [128, T], F32))
            kT = p0.enter_context(_sbt(nc, "gd_k", [128, T], F32))
            zg = p0.enter_context(_sbt(nc, "gd_zg", [128, NT, 128], BF16))
            ba = p0.enter_context(_sbt(nc, "gd_ba", [128, NT, 8], F32))
            vT = p0.enter_context(_sbt(nc, "gd_v", [128, T], F32))
            cw = p0.enter_context(_sbt(nc, "gd_cw", [128, 5, 4], F32))
            with ExitStack() as p1:
                B.dma('sp', cw[:], self.convw[l, g], writes=['cw'])
                with ExitStack() as ph:
                    wsb = self.load_w(ph, l, g, 1152, 520, "gd")
                    P = [ph.enter_context(_sbt(nc, "gd_P%d" % i, [128, PADW], F32)) for i in range(3)]
                    for i in range(3):
                        self.zero_pads(P[i], ('gdP', i))

                    def fm_sink(idx, s, n, ps, bkey):
                        e.cp('act', P[idx][:, pcol(s):pcol(s) + n], ps, [], [bkey, ('gdP', idx)])

                    def tm_sink(idx, ti, ps, bkey):
                        e.act(zg[:, ti, :], ps[:, 0:128], AF.Silu, [], [bkey, 'gd_zg'])
                        e.cp('dve', ba[:, ti, :], ps[:, 128:136], [], [bkey, 'gd_ba'])

                    self.proj(ph, wsb, [(0, 128), (128, 128), (256, 128)], [(384, 136)], fm_sink, tm_sink)
                    for i, dst in enumerate([qT, kT, vT]):
                        self.conv(P[i], dst, cw, 2 + i, 128, [('gdP', i), 'cw'], [('gd_c', i)])
                    B.barrier()
                kv = p1.enter_context(_sbt(nc, "gd_kv", [128, NT, 256], F32))
                with ExitStack() as ph:
                    sq = [ph.enter_context(_sbt(nc, "gd_sq%d" % i, [128, 512], F32)) for i in range(2)]
                    nr = [ph.enter_context(_sbt(nc, "gd_nr%d" % i, [128, 512], F32)) for i in range(2)]
                    t1 = [ph.enter_context(_sbt(nc, "gd_t1%d" % i, [128, 512], F32)) for i in range(2)]
                    cs = [ph.enter_context(_sbt(nc, "gd_cs%d" % i, [128, 512], F32)) for i in range(2)]
                    sn = [ph.enter_context(_sbt(nc, "gd_sn%d" % i, [128, 512], F32)) for i in range(2)]
                    for bi, (s, n) in enumerate(BLOCKS):
                        lat = s >= 256
                        pb = bi % 2
                        if lat:
                            B.dma('sp', cs[pb][:], self.rope_cos[:, s - 256:s - 256 + 512], writes=[('gd_cs', pb)])
                            B.dma('sp', sn[pb][:], self.rope_sin[:, s - 256:s - 256 + 512], writes=[('gd_sn', pb)])
                        for wi, Xt in enumerate([qT, kT]):
                            X = Xt[:, s:s + n]
                            kx = ('gd_c', wi)
                            p = wi
                            b0, b1 = 4 + 2 * p, 5 + 2 * p
                            e.act(X, X, AF.Silu, [kx], [kx])
                            e.tt('pool', sq[p][:, 0:n], X, X, ALU.mult, [kx], [('gd_sq', p)])
                            e.mm(bnk[:, b0, 0:n], self.c(C_BONE), sq[p][:, 0:n], True, True, [('gd_sq', p), 'cst'],
                                 [self.bk(b0)])
                            e.act(nr[p][:, 0:n], bnk[:, b0, 0:n], AF.Sqrt, [], [('gd_nr', p), self.bk(b0)], bias=NORM_EPS)
                            e.recip(nr[p][:, 0:n], nr[p][:, 0:n], [('gd_nr', p)], [('gd_nr', p)])
                            if wi == 0:
                                e.stt(X, X, 0.125, nr[p][:, 0:n], ALU.mult, ALU.mult, [kx, ('gd_nr', p)], [kx])
                            else:
                                e.tt('dve', X, X, nr[p][:, 0:n], ALU.mult, [kx, ('gd_nr', p)], [kx])
                            if lat:
                                e.mm(bnk[:, b1, 0:n], self.c(C_RM), X, True, True, [kx, 'cst'], [self.bk(b1)])
                                e.tt('pool', t1[p][:, 0:n], X, cs[pb][:, 0:n], ALU.mult, [kx, ('gd_cs', pb)], [('gd_t1', p)])
                                e.tt('dve', X, bnk[:, b1, 0:n], sn[pb][:, 0:n], ALU.mult, [('gd_sn', pb)],
                                     [kx, self.bk(b1)])
                                e.tt('pool', X, X, t1[p][:, 0:n], ALU.add, [kx, ('gd_t1', p)], [kx])
                        e.act(vT[:, s:s + n], vT[:, s:s + n], AF.Silu, [('gd_c', 2)], [('gd_c', 2)])
                    for ti in range(NT):
                        b0 = ti % 4
                        e.tr(bnk[:, b0, 0:128], kT[:, ti * 128:(ti + 1) * 128], self.c(C_ID), [('gd_c', 1), 'cst'],
                             [self.bk(b0)])
                        e.tr(bnk[:, b0, 128:256], vT[:, ti * 128:(ti + 1) * 128], self.c(C_ID), [('gd_c', 2), 'cst'],
                             [self.bk(b0)])
                        e.cp('act' if ti % 2 else 'dve', kv[:, ti, :], bnk[:, b0, 0:256], [], ['gd_kv', self.bk(b0)])
                    B.barrier()
                if self.debug:
                    self.dump("gd_q_%d_%d" % (l, g), qT[:], [128, T], [('gd_c', 0)])
                    self.dump("gd_k_%d_%d" % (l, g), kT[:], [128, T], [('gd_c', 1)])
                    self.dump("gd_kv_%d_%d" % (l, g), kv[:], [128, NT, 256], ['gd_kv'])
                with ExitStack() as ph:
                    def sbt(name, shape, dt=F32):
                        return ph.enter_context(_sbt(nc, name, shape, dt))
                    o_tm = [sbt("gd_o0", [128, NT, 128]), vT[:].rearrange("p (t c) -> p t c", c=128)]
                    al = sbt("gd_al", [128, NT, 4])
                    dtb = sbt("gd_dtb", [128, NT, 4])
                    g_all = sbt("gd_g", [128, NT, 4])
                    beta = sbt("gd_beta", [128, NT, 4])
                    nbeta = sbt("gd_nbeta", [128, NT, 4])
                    tg = sbt("gd_tg", [128, NT, 4])
                    B.dma('sp', al[:], self.galog[l, g], writes=['gd_al'])
                    B.dma('sp', dtb[:], self.gdtb[l, g], writes=['gd_dtb'])
                    e.act(beta[:], ba[:, :, 0:4], AF.Sigmoid, ['gd_ba'], ['gd_beta'])
                    e.ts('dve', nbeta[:], beta[:], -1.0, None, ALU.mult, None, ['gd_beta'], ['gd_nbeta'])
                    e.tt('dve', tg[:], ba[:, :, 4:8], dtb[:], ALU.add, ['gd_ba', 'gd_dtb'], ['gd_tg'])
                    e.act(tg[:], tg[:], AF.Exp, ['gd_tg'], ['gd_tg'])
                    e.act(tg[:], tg[:], AF.Ln, ['gd_tg'], ['gd_tg'], bias=1.0)
                    e.act(al[:], al[:], AF.Exp, ['gd_al'], ['gd_al'])
                    e.stt(g_all[:], tg[:], -1.0, al[:], ALU.mult, ALU.mult, ['gd_tg', 'gd_al'], ['gd_g'])
                    if self.debug:
                        self.dump("gd_g_%d_%d" % (l, g), g_all[:], [128, NT, 4], ['gd_g'])
                        self.dump("gd_beta_%d_%d" % (l, g), beta[:], [128, NT, 4], ['gd_beta'])
                    R = 3
                    chains = [(d, h) for d in range(2) for h in range(2)]
                    T2t = {c: [sbt("gd_T2t%d%d_%d" % (c[0], c[1], r), [128, 128]) for r in range(R)] for c in chains}
                    QKt = {c: [sbt("gd_QKt%d%d_%d" % (c[0], c[1], r), [128, 128]) for r in range(R)] for c in chains}
                    vec = {c: [sbt("gd_vec%d%d_%d" % (c[0], c[1], r), [128, 4]) for r in range(R)] for c in chains}
                    Sst = {c: sbt("gd_S%d%d" % c, [128, 64]) for c in chains}
                    Rp = {c: sbt("gd_Rp%d%d" % c, [128, 64]) for c in chains}
                    qs = {c: sbt("gd_qs%d%d" % c, [128, 64]) for c in chains}
                    vnw = {c: sbt("gd_vn%d%d" % c, [128, 64]) for c in chains}
                    kh = {c: sbt("gd_kh%d%d" % c, [128, 64]) for c in chains}
                    Gm = [sbt("gd_G%d" % i, [128, 128]) for i in range(2)]
                    nG = [sbt("gd_nG%d" % i, [128, 128]) for i in range(2)]
                    Et = [sbt("gd_Et%d" % i, [128, 128]) for i in range(2)]
                    tv = [sbt("gd_tv%d" % i, [128, 2]) for i in range(2)]
                    YT = [[sbt("gd_YT%d_%d" % (i, k), [128, 256]) for k in range(2)] for i in range(2)]
                    Yt = [[sbt("gd_Yt%d_%d" % (i, k), [128, 128]) for k in range(2)] for i in range(2)]
                    for c in chains:
                        e.memset('pool', Sst[c][:], 0.0, [('gd_S', c)])
                    ident = self.c(C_ID)
                    ones = self.c(C_ONE)
                    orders = {0: list(range(NT)), 1: [1, 0] + list(range(NT - 1, 1, -1))}
                    pc = [0]

                    def prep(c, ti, slot):
                        d, h = c
                        col = 2 * d + h
                        pp = pc[0] % 2
                        pc[0] += 1
                        U = self.c(C_UF if d == 0 else C_UB)
                        Sm = self.c(C_SF if d == 0 else C_SB)
                        Mk = self.c(C_MF if d == 0 else C_MB)
                        hr0 = 64 * h
                        tc0 = ti * 128
                        gcol = g_all[:, ti, col:col + 1]
                        kG, knG, kEt, ktv = ('gd_G', pp), ('gd_nG', pp), ('gd_Et', pp), ('gd_tv', pp)
                        kslot = ('gd_slot', c, slot)
                        bD, bK, bI, bJ = 0, 1, 2, 3
                        e.ts('pool', Gm[pp][:], U, gcol, None, ALU.mult, None, ['cst', 'gd_g'], [kG])
                        e.ts('pool', nG[pp][:], U, gcol, -1.0, ALU.mult, ALU.mult, ['cst', 'gd_g'], [knG])
                        e.mm(bnk[:, bD, 0:128], ones, Gm[pp][:], True, False, ['cst', kG], [self.bk(bD)])
                        e.mm(bnk[:, bD, 0:128], nG[pp][:], ones, False, False, ['cst', knG], [self.bk(bD)])
                        e.mm(bnk[:, bD, 0:128], ident, Mk, False, True, ['cst'], [self.bk(bD)])
                        e.mm(bnk[:, bD, 128:129], U, gcol, True, True, ['cst', 'gd_g'], [self.bk(bD)])
                        e.mm(bnk[:, bD, 129:130], ones, gcol, True, True, ['cst', 'gd_g'], [self.bk(bD)])
                        e.act(Et[pp][:], bnk[:, bD, 0:128], AF.Exp, [], [kEt, self.bk(bD)])
                        e.cp('dve', tv[pp][:], bnk[:, bD, 128:130], [], [ktv, self.bk(bD)])
                        vc = vec[c][slot]
                        e.act(vc[:, 0:1], tv[pp][:, 0:1], AF.Exp, [ktv], [(kslot, 'v0')])
                        e.ts('pool', vc[:, 1:2], vc[:, 0:1], -1.0, None, ALU.mult, None, [(kslot, 'v0')], [(kslot, 'v1')])
                        e.act(vc[:, 2:3], tv[pp][:, 0:1], AF.Exp, [ktv], [(kslot, 'v2')], bias=tv[pp][:, 1:2], scale=-1.0)
                        e.act(vc[:, 3:4], tv[pp][:, 1:2], AF.Exp, [ktv], [(kslot, 'v3')])
                        ka = kT[hr0:hr0 + 64, tc0:tc0 + 128]
                        qa = qT[hr0:hr0 + 64, tc0:tc0 + 128]
                        e.mm(bnk[:, bK, 0:128], ka, ka, True, True, [('gd_c', 1)], [self.bk(bK)])
                        e.mm(bnk[:, bK, 128:256], ka, qa, True, True, [('gd_c', 1), ('gd_c', 0)], [self.bk(bK)])
                        Y0 = YT[pp][0]
                        kYT = [('gd_YT', pp, 0), ('gd_YT', pp, 1)]
                        kYt = [('gd_Yt', pp, 0), ('gd_Yt', pp, 1)]
                        e.stt(Y0[:, 0:128], bnk[:, bK, 0:128], nbeta[:, ti, col:col + 1], Et[pp][:], ALU.mult, ALU.mult,
                              ['gd_nbeta', kEt], [kYT[0], self.bk(bK)])
                        e.tt('dve', QKt[c][slot][:], bnk[:, bK, 128:256], Et[pp][:], ALU.mult, [kEt],
                             [(kslot, 'QK'), self.bk(bK)])
                        e.tt('pool', Y0[:, 0:128], Y0[:, 0:128], Sm, ALU.mult, [kYT[0], 'cst'], [kYT[0]])
                        e.tt('pool', Y0[:, 128:256], Y0[:, 0:128], ident, ALU.add, [kYT[0], 'cst'], [kYT[0]])
                        e.tr(bnk[:, bJ, 0:128], Y0[:, 0:128], ident, [kYT[0], 'cst'], [self.bk(bJ)])
                        e.cp('act', Yt[pp][0][:], bnk[:, bJ, 0:128], [], [kYt[0], self.bk(bJ)])
                        e.mm(bnk[:, bI, 0:128], Yt[pp][0][:], Y0[:, 0:128], True, True, [kYt[0], kYT[0]], [self.bk(bI)])
                        e.mm(bnk[:, bJ, 0:128], Y0[:, 0:128], Yt[pp][0][:], True, True, [kYt[0], kYT[0]], [self.bk(bJ)])
                        e.cp('act', YT[pp][1][:, 0:128], bnk[:, bI, 0:128], [], [kYT[1], self.bk(bI)])
                        e.cp('dve', Yt[pp][1][:], bnk[:, bJ, 0:128], [], [kYt[1], self.bk(bJ)])
                        e.cp('pool', YT[pp][1][:, 128:256], Y0[:, 128:256], [kYT[0]], [kYT[1]])
                        for q in range(1, 7):
                            a = q % 2
                            nb_ = 1 - a
                            cur, curt = YT[pp][a], Yt[pp][a]
                            nx, nxt_ = YT[pp][nb_], Yt[pp][nb_]
                            if q < 6:
                                e.mm(bnk[:, bI, 0:256], curt[:], cur[:, 0:256], True, True, [kYt[a], kYT[a]], [self.bk(bI)])
                                e.mm(bnk[:, bJ, 0:128], cur[:, 0:128], curt[:], True, True, [kYt[a], kYT[a]], [self.bk(bJ)])
                                e.cp('act', nx[:, 0:128], bnk[:, bI, 0:128], [], [kYT[nb_], self.bk(bI)])
                                e.tt('dve', nx[:, 128:256], bnk[:, bI, 128:256], cur[:, 128:256], ALU.add, [kYT[a]],
                                     [kYT[nb_], self.bk(bI)])
                                e.cp('dve', nxt_[:], bnk[:, bJ, 0:128], [], [kYt[nb_], self.bk(bJ)])
                            else:
                                e.mm(bnk[:, bI, 0:128], curt[:], cur[:, 128:256], True, True, [kYt[a], kYT[a]], [self.bk(bI)])
                                e.tt('dve', T2t[c][slot][:], bnk[:, bI, 0:128], cur[:, 128:256], ALU.add, [kYT[a]],
                                     [(kslot, 'T'), self.bk(bI)])

                    def seq_stage(stage, c, ti, slot):
                        d, h = c
                        col = 2 * d + h
                        hr0 = 64 * h
                        tc0 = ti * 128
                        ci = chains.index(c)
                        bC = 4 + ci
                        kslot = ('gd_slot', c, slot)
                        vc = vec[c][slot]
                        kS = ('gd_S', c)
                        if stage == 0:
                            e.mm(bnk[:, bC, 0:64], kT[hr0:hr0 + 64, tc0:tc0 + 128], Sst[c][hr0:hr0 + 64, :], True, True,
                                 [('gd_c', 1), kS], [self.bk(bC)])
                            e.mm(bnk[:, bC, 64:128], qT[hr0:hr0 + 64, tc0:tc0 + 128], Sst[c][hr0:hr0 + 64, :], True, True,
                                 [('gd_c', 0), kS], [self.bk(bC)])
                            e.ts('pool', kh[c][:], kv[:, ti, hr0:hr0 + 64], vc[:, 2:3], None, ALU.mult, None,
                                 ['gd_kv', (kslot, 'v2')], [('gd_kh', c)])
                        elif stage == 1:
                            e.stt(Rp[c][:], bnk[:, bC, 0:64], vc[:, 1:2], kv[:, ti, 128 + hr0:128 + hr0 + 64], ALU.mult, ALU.add,
                                  [(kslot, 'v1'), 'gd_kv'], [('gd_Rp', c), self.bk(bC)])
                            e.act(qs[c][:], bnk[:, bC, 64:128], AF.Identity, [(kslot, 'v0')], [('gd_qs', c), self.bk(bC)],
                                  scale=vc[:, 0:1])
                        elif stage == 2:
                            e.mm(bnk[:, bC, 128:192], T2t[c][slot][:], Rp[c][:], True, True, [(kslot, 'T'), ('gd_Rp', c)],
                                 [self.bk(bC)])
                        elif stage == 3:
                            e.act(vnw[c][:], bnk[:, bC, 128:192], AF.Identity, ['gd_beta'], [('gd_vn', c), self.bk(bC)],
                                  scale=beta[:, ti, col:col + 1])
                        elif stage == 4:
                            e.mm(bnk[:, bC, 192:256], QKt[c][slot][:], vnw[c][:], True, True, [(kslot, 'QK'), ('gd_vn', c)],
                                 [self.bk(bC)])
                            e.mm(bnk[hr0:hr0 + 64, bC, 256:320], kh[c][:], vnw[c][:], True, True,
                                 [('gd_kh', c), ('gd_vn', c)], [self.bk(bC)])
                        else:
                            e.tt('dve', o_tm[d][:, ti, hr0:hr0 + 64], bnk[:, bC, 192:256], qs[c][:], ALU.add,
                                 [('gd_qs', c)], [('gd_o', d), self.bk(bC)])
                            e.stt(Sst[c][hr0:hr0 + 64, :], Sst[c][hr0:hr0 + 64, :], vc[hr0:hr0 + 64, 3:4],
                                  bnk[hr0:hr0 + 64, bC, 256:320], ALU.mult, ALU.add, [kS, (kslot, 'v3')], [kS, self.bk(bC)])

                    for c in chains:
                        prep(c, orders[c[0]][0], 0)
                    for si in range(NT):
                        if si + 1 < NT:
                            for c in chains:
                                prep(c, orders[c[0]][si + 1], (si + 1) % R)
                        for stage in range(6):
                            for c in chains:
                                seq_stage(stage, c, orders[c[0]][si], si % R)
                    if self.debug:
                        self.dump("gd_o0_%d_%d" % (l, g), o_tm[0][:], [128, NT, 128], [('gd_o', 0)])
                        self.dump("gd_o1_%d_%d" % (l, g), o_tm[1][:], [128, NT, 128], [('gd_o', 1)])
                    nw = sbt("gd_nw", [128, 128])
                    ss = sbt("gd_ss", [128, NT * 2])
                    yb = [sbt("gd_yb%d" % i, [128, 512], BF16) for i in range(2)]
                    B.dma('sp', nw[:], self.gnw[l], writes=['gd_nw'])
                    O = o_tm[0]
                    O1 = o_tm[1]
                    Of = O[:].rearrange("p t c -> p (t c)")
                    O1f = O1[:].rearrange("p t c -> p (t c)")
                    e.tt('pool', Of, Of, O1f, ALU.add, [('gd_o', 0), ('gd_o', 1)], [('gd_o', 0)])
                    e.tt('dve', O1f, Of, Of, ALU.mult, [('gd_o', 0)], [('gd_o', 1)])
                    e.red(ss[:], O1[:].rearrange("p t (h c) -> p (t h) c", c=64), ALU.add, [('gd_o', 1)], ['gd_ss'])
                    e.ts('dve', ss[:], ss[:], 1.0 / 64.0, NORM_EPS, ALU.mult, ALU.add, ['gd_ss'], ['gd_ss'])
                    e.act(ss[:], ss[:], AF.Sqrt, ['gd_ss'], ['gd_ss'])
                    e.recip(ss[:], ss[:], ['gd_ss'], ['gd_ss'])
                    O3 = O[:].rearrange("p t (h c) -> p (t h) c", c=64)
                    e.tt('dve', O3, O3, ss[:].unsqueeze(2).to_broadcast([128, NT * 2, 64]), ALU.mult,
                         [('gd_o', 0), 'gd_ss'], [('gd_o', 0)])
                    e.tt('pool', O[:], O[:], nw[:].unsqueeze(1).to_broadcast([128, NT, 128]), ALU.mult,
                         [('gd_o', 0), 'gd_nw'], [('gd_o', 0)])
                    e.tt('dve', O[:], O[:], zg[:], ALU.mult, [('gd_o', 0), 'gd_zg'], [('gd_o', 0)])
                    groups = [[0, 1]] + [list(range(2 + 4 * i, 6 + 4 * i)) for i in range(8)]
                    for gi, tl in enumerate(groups):
                        pb = gi % 2
                        bO = gi % 4
                        for k, ti in enumerate(tl):
                            e.tr(bnk[:, bO, k * 128:(k + 1) * 128], O[:, ti, :], ident, [('gd_o', 0), 'cst'], [self.bk(bO)])
                        n = 128 * len(tl)
                        e.cp('act' if gi % 2 else 'dve', yb[pb][:, 0:n], bnk[:, bO, 0:n], [], [('gd_yb', pb), self.bk(bO)])
                        B.dma('pool', yc[row0:row0 + 128, tl[0] * 128:tl[0] * 128 + n], yb[pb][:, 0:n],
                              reads=[('gd_yb', pb)], writes=['YC'])
                    B.barrier()

    def back_phase(self, l, x_cur, yc, x_next, last):
        B, e, nc = self.B, self.e, self.nc
        bnk = self.banks
        with ExitStack() as ph:
            def sbt(name, shape, dt=F32):
                return ph.enter_context(_sbt(nc, name, shape, dt))
            wo = sbt("bk_wo", [128, KC, D], BF16)
            stg = [sbt("bk_stg%d" % i, [128, D]) for i in range(2)]
            ycb = [sbt("bk_yc%d" % i, [128, KC, 128], BF16) for i in range(2)]
            xt = [sbt("bk_xt%d" % i, [128, D]) for i in range(2)]
            tb = [sbt("bk_tb%d" % i, [128, D]) for i in range(2)]
            zb = [sbt("bk_zb%d" % i, [128, D]) for i in range(2)]
            lng = sbt("bk_lng", [128, D])
            lnb = sbt("bk_lnb", [128, D])
            st6 = [sbt("bk_st6%d" % i, [128, 2, 6]) for i in range(2)]
            mv = [sbt("bk_mv%d" % i, [128, 4]) for i in range(2)]
            B.dma('sp', lng[:], self.lng_rep[l], writes=['bk_lng'])
            B.dma('sp', lnb[:], self.lnb_rep[l], writes=['bk_lnb'])
            for kc in range(KC):
                b = kc % 2
                B.dma('sp', stg[b][:], self.w_out[l, kc * 128:(kc + 1) * 128, :], writes=[('bk_stg', b)])
                e.cp('pool', wo[:, kc, :], stg[b][:], [('bk_stg', b)], ['bk_wo'])
            ycv = yc.rearrange("(c p) t -> p c t", p=128)
            tiles = list(range(2, NT)) if last else list(range(NT))
            for n_, ti in enumerate(tiles):
                j = 1 if ti < 2 else 0
                b = n_ % 2
                b0 = 2 * b
                B.dma('sp', ycb[b][:], ycv[:, :, ti * 128:(ti + 1) * 128], reads=['YC'], writes=[('bk_yc', b)])
                B.dma('sp', xt[b][:], x_cur[ti * 128:(ti + 1) * 128, :], reads=['X'], writes=[('bk_xt', b)])
                for nb in range(2):
                    for c in range(KC):
                        e.mm(bnk[:, b0 + nb, :], ycb[b][:, c, :], wo[:, c, nb * 512:(nb + 1) * 512], c == 0, c == KC - 1,
                             [('bk_yc', b), 'bk_wo'], [self.bk(b0 + nb)])
                for nb in range(2):
                    e.tt('dve', tb[b][:, nb * 512:(nb + 1) * 512], bnk[:, b0 + nb, :],
                         self.gate_bc[:, j, nb * 512:(nb + 1) * 512], ALU.mult, ['gate_bc'],
                         [('bk_tb', b), self.bk(b0 + nb)])
                e.stt(zb[b][:], xt[b][:], ALPHA, tb[b][:], ALU.mult, ALU.add, [('bk_xt', b), ('bk_tb', b)], [('bk_zb', b)])
                for nb in range(2):
                    B.op('dve', (lambda o_, i_: (lambda en: en.bn_stats(out=o_, in_=i_)))(
                        st6[b][:, nb, :], zb[b][:, nb * 512:(nb + 1) * 512]), [('bk_zb', b)], [('bk_st6', b)])
                B.op('dve', (lambda o_, i_: (lambda en: en.bn_aggr(out=o_, in_=i_)))(
                    mv[b][:, 0:2], st6[b][:].rearrange("p a s -> p (a s)")), [('bk_st6', b)], [('bk_mv', b)])
                e.act(mv[b][:, 2:3], mv[b][:, 1:2], AF.Sqrt, [('bk_mv', b)], [('bk_sd', b)], bias=LN_EPS)
                e.recip(mv[b][:, 3:4], mv[b][:, 2:3], [('bk_sd', b)], [('bk_rs', b)])
                e.ts('dve', zb[b][:], zb[b][:], mv[b][:, 0:1], mv[b][:, 3:4], ALU.subtract, ALU.mult,
                     [('bk_zb', b), ('bk_mv', b), ('bk_rs', b)], [('bk_zb', b)])
                e.tt('pool', zb[b][:], zb[b][:], lng[:], ALU.mult, [('bk_zb', b), 'bk_lng'], [('bk_zb', b)])
                e.tt('pool', zb[b][:], zb[b][:], lnb[:], ALU.add, [('bk_zb', b), 'bk_lnb'], [('bk_zb', b)])
                if last:
                    dst = x_next[(ti - 2) * 128:(ti - 1) * 128, :]
                else:
                    dst = x_next[ti * 128:(ti + 1) * 128, :]
                B.dma('pool', dst, zb[b][:], reads=[('bk_zb', b)], writes=['Xn'])
            B.barrier()
            B.last_w['X'] = B.last_w.get('Xn')


def group_cols(g):
    R0 = 1152
    rgx = np.arange(192 * g, 192 * g + 192)
    gq = 384 + np.arange(128 * g, 128 * g + 128)
    gk = 640 + np.arange(128 * g, 128 * g + 128)
    gv = 896 + np.arange(128 * g, 128 * g + 128)
    rgg = R0 + np.arange(192 * g, 192 * g + 192)
    naq = R0 + 384 + np.arange(192 * g, 192 * g + 192)
    nak = R0 + 768 + np.arange(192 * g, 192 * g + 192)
    nav = R0 + 1152 + np.arange(192 * g, 192 * g + 192)
    nag = R0 + 1536 + np.arange(192 * g, 192 * g + 192)
    gg = R0 + 1920 + np.arange(128 * g, 128 * g + 128)
    bf = R0 + 2176 + np.arange(2 * g, 2 * g + 2)
    bb = R0 + 2180 + np.arange(2 * g, 2 * g + 2)
    af = R0 + 2184 + np.arange(2 * g, 2 * g + 2)
    ab = R0 + 2188 + np.arange(2 * g, 2 * g + 2)
    cols = np.concatenate([rgx, rgg, naq, nak, nag, nav, gq, gk, gv, gg, bf, bb, af, ab])
    assert cols.shape[0] == GCOLS
    conv_ch = [rgx[:128], rgx[128:], gq, gk, gv]
    return cols, conv_ch


def make_consts():
    c = np.zeros((128, NCONST), np.float32)
    i = np.arange(128)
    c[:, C_ID:C_ID + 128] = np.eye(128)
    c[:, C_ONE:C_ONE + 128] = 1.0
    c[:64, C_BONE:C_BONE + 64] = 1.0
    c[64:, C_BONE + 64:C_BONE + 128] = 1.0
    R = np.zeros((128, 128), np.float32)
    for h in range(2):
        for half in range(2):
            o = 64 * h + 32 * half
            for t in range(16):
                R[o + t, o + t + 16] = -1.0
                R[o + t + 16, o + t] = 1.0
    c[:, C_RM:C_RM + 128] = R.T
    k = i[:, None]
    m = i[None, :]
    c[:, C_UF:C_UF + 128] = (k <= m)
    c[:, C_UB:C_UB + 128] = (k >= m)
    c[:, C_SF:C_SF + 128] = (m > k)
    c[:, C_SB:C_SB + 128] = (m < k)
    c[:, C_MF:C_MF + 128] = np.where(m >= k, 0.0, -BIG)
    c[:, C_MB:C_MB + 128] = np.where(m <= k, 0.0, -BIG)
    return c


def make_rope():
    p = np.arange(128)
    d = p % 64
    half = d // 32
    fi = d % 16
    inv_freq = (np.float32(10000.0) ** (-np.arange(16, dtype=np.float32) / np.float32(16))).astype(np.float32)
    t = np.arange(4096)
    row = (t // 64).astype(np.float32)
    col = (t % 64).astype(np.float32)
    pos = np.where(half[:, None] == 0, row[None, :], col[None, :]).astype(np.float32)
    ang = (pos * inv_freq[fi][:, None]).astype(np.float32)
    return np.cos(ang).astype(np.float32), np.sin(ang).astype(np.float32)


NA_CLASSES = [(0, 0), (1, 0), (2, 0), (30, 27), (31, 27)]


def make_nab(rpb_h):
    out = np.full((128, 5, 640), -BIG, np.float32)
    q = np.arange(128)
    key = np.arange(640)
    for ci, (qt, kt0) in enumerate(NA_CLASSES):
        r = 2 * qt + q // 64
        j = q % 64
        kr = 2 * kt0 + key // 64
        kcn = key % 64
        r0 = np.clip(r - 4, 0, 56)
        c0 = np.clip(j - 8, 0, 48)
        okr = (kr[None, :] >= r0[:, None]) & (kr[None, :] < r0[:, None] + 8)
        okc = (kcn[None, :] >= c0[:, None]) & (kcn[None, :] < c0[:, None] + 16)
        ro = np.clip(kr[None, :] - r[:, None] + 7, 0, 14)
        co = np.clip(kcn[None, :] - j[:, None] + 15, 0, 30)
        vals = rpb_h[ro, co]
        out[:, ci, :] = np.where(okr & okc, vals, np.float32(-BIG))
    return out


def prep_shared(inp, G):
    L = DEPTH
    sh = {}
    sh['consts'] = make_consts()
    cs, sn = make_rope()
    sh['rope_cos'], sh['rope_sin'] = cs, sn
    sh['w_mod'] = np.ascontiguousarray(inp['w_mod'])
    sh['bmod_rep'] = np.ascontiguousarray(np.broadcast_to(inp['b_mod'][:, None, :], (L, 128, 3 * D)))
    sh['gnw'] = np.ascontiguousarray(np.broadcast_to(np.tile(inp['gdn_nw'], (1, 2))[:, None, :], (L, 128, 128)))
    sh['lng_rep'] = np.ascontiguousarray(np.broadcast_to(inp['ln_g'][:, None, :], (L, 128, D)))
    sh['lnb_rep'] = np.ascontiguousarray(np.broadcast_to(inp['ln_b'][:, None, :], (L, 128, D)))
    rows = []
    for g in range(2):
        rows += list(range(192 * g, 192 * g + 192))
        rows += list(range(384 + 192 * g, 384 + 192 * g + 192))
        rows += list(range(768 + 128 * g, 768 + 128 * g + 128))
    sh['w_out'] = np.ascontiguousarray(inp['w_out'][:, np.array(rows), :])
    return sh


def prep_group(inp, g):
    L = DEPTH
    cols, conv_ch = group_cols(g)
    o = {}
    o['w_in'] = inp['w_in'][:, :, cols]
    cw = np.zeros((L, 128, 5, 4), np.float32)
    for ti, ch in enumerate(conv_ch):
        cw[:, :len(ch), ti, :] = np.transpose(inp['conv_w'][:, :, ch], (0, 2, 1))
    o['convw'] = cw
    rgw = np.zeros((L, 128, 2, 2, 192), np.float32)
    rgb = np.zeros((L, 128, 2, 2, 3), np.float32)
    for d in range(2):
        for ai, (wn, bn) in enumerate([('rg_wa', 'rg_ba'), ('rg_wx', 'rg_bx')]):
            W = inp[wn][:, d]
            rgw[:, 0:64, d, ai, 0:64] = W[:, 3 * g]
            rgw[:, 64:128, d, ai, 64:128] = W[:, 3 * g + 1]
            rgw[:, 0:64, d, ai, 128:192] = W[:, 3 * g + 2]
            bvec = inp[bn][:, d, 192 * g:192 * g + 192]
            rgb[:, :, 0, d, ai] = bvec[:, 0:128]
            rgb[:, 0:64, 1, d, ai] = bvec[:, 128:192]
        lam = inp['rg_lam'][:, d, 192 * g:192 * g + 192]
        rgb[:, :, 0, d, 2] = lam[:, 0:128]
        rgb[:, 0:64, 1, d, 2] = lam[:, 128:192]
    o['rgw'] = rgw
    o['rgb'] = rgb
    nab = np.zeros((L, 3, 128, 5, 640), np.float32)
    for l in range(L):
        for h in range(3):
            nab[l, h] = make_nab(inp['na_rpb'][l, 3 * g + h])
    o['nab'] = nab
    al = np.zeros((L, 128, NT, 4), np.float32)
    dt = np.zeros((L, 128, NT, 4), np.float32)
    for d in range(2):
        for h in range(2):
            al[:, :, :, 2 * d + h] = inp['gdn_alog'][:, d, 2 * g + h][:, None, None]
            dt[:, :, :, 2 * d + h] = inp['gdn_dtb'][:, d, 2 * g + h][:, None, None]
    o['galog'] = al
    o['gdtb'] = dt
    return o


def core_inputs(inp, sh, groups, b, x_rows):
    m = dict(sh)
    gs = [prep_group(inp, g) for g in groups]
    m['w_in'] = np.ascontiguousarray(np.concatenate([q['w_in'] for q in gs], axis=2))
    for k in ['convw', 'rgw', 'rgb', 'nab', 'galog', 'gdtb']:
        m[k] = np.ascontiguousarray(np.stack([q[k] for q in gs], axis=1))
    m['x_in'] = np.ascontiguousarray(x_rows)
    cc = np.stack([inp['c'][b], inp['c_ctx']], axis=-1)
    m['cc'] = np.ascontiguousarray(cc.reshape(KC, 128, 2).transpose(1, 0, 2))
    return m


MODE = 'B'
_PROG_CACHE = {}


def _prog(key, **kw):
    if key not in _PROG_CACHE:
        _PROG_CACHE[key] = Prog(**kw)
    return _PROG_CACHE[key]


def kernel_unfused(inp):
    nb = inp['x'].shape[0]
    sh = prep_shared(inp, 1)
    ncore = 2 * nb
    x_rows = [np.concatenate([inp['ctx'][b], inp['x'][b]], 0) for b in range(nb)]
    base = [core_inputs(inp, sh, [c % 2], c // 2, x_rows[c // 2]) for c in range(ncore)]
    yc_full = None
    out = None
    for k in range(DEPTH + 1):
        if k == 0:
            P = _prog(('A', 0), G=1, steps=[('front', 0)], x_ext_out=True, final=False)
        elif k < DEPTH:
            P = _prog(('A', k), G=1, steps=[('back', k - 1), ('front', k)], x_ext_out=True, final=False)
        else:
            P = _prog(('A', k), G=1, steps=[('back', DEPTH - 1)], x_ext_out=True, final=True)
        maps = []
        for c in range(ncore):
            m = dict(base[c])
            m['x_in'] = x_rows[c // 2]
            if k > 0:
                m['yc_in'] = yc_full[c // 2]
            maps.append(m)
        res = run_bass_kernel_spmd(P.nc, maps, core_ids=list(range(ncore)))
        rs = res.results
        if k < DEPTH:
            yc_full = [np.ascontiguousarray(np.concatenate([np.asarray(rs[2 * b]['yc_out']),
                                                            np.asarray(rs[2 * b + 1]['yc_out'])], 0))
                       for b in range(nb)]
        if 0 < k < DEPTH:
            x_rows = [np.asarray(rs[2 * b]['x_out']) for b in range(nb)]
        if k == DEPTH:
            out = np.stack([np.asarray(rs[2 * b]['out']) for b in range(nb)], 0)
    return out.astype(np.float32)


def kernel_fused(inp):
    nb = inp['x'].shape[0]
    sh = prep_shared(inp, 2)
    steps = []
    for l in range(DEPTH):
        steps += [('front', l), ('back', l)]
    P = _prog(('B',), G=2, steps=steps, x_ext_out=False, final=True)
    maps = []
    for c in range(8):
        b = c % nb
        x_rows = np.concatenate([inp['ctx'][b], inp['x'][b]], 0)
        maps.append(core_inputs(inp, sh, [0, 1], b, x_rows))
    res = run_bass_kernel_spmd(P.nc, maps, core_ids=list(range(8)))
    out = np.stack([np.asarray(res.results[b]['out']) for b in range(nb)], 0)
    return out.astype(np.float32)


def kernel(**inputs):
    inp = {k: np.asarray(v) for k, v in inputs.items()}
    if MODE == 'A':
        return kernel_unfused(inp)
    return kernel_fused(inp)
```

```python
from contextlib import ExitStack
import numpy as np
import concourse.bass as bass
import concourse.mybir as mybir
from concourse.bass_utils import run_bass_kernel_spmd

F32 = mybir.dt.float32
BF16 = mybir.dt.bfloat16
AF = mybir.ActivationFunctionType
ALU = mybir.AluOpType
AX = mybir.AxisListType

ENGS = ['pe', 'act', 'dve', 'pool', 'sp']
N_DMA_SEMS = 24


_UNIQ = [0]


def _sbt(nc, name, shape, dt):
    _UNIQ[0] += 1
    return nc.sbuf_tensor("%s_u%d" % (name, _UNIQ[0]), list(shape), dt)


class Builder:
    def __init__(self, nc):
        self.nc = nc
        self.stack = ExitStack()
        self.ops = {e: [] for e in ENGS}
        self.sem = {}
        self.cnt = {}
        self.waited = {e: {} for e in ENGS}
        self.last_w = {}
        self.readers = {}
        for e in ENGS:
            self.sem[e] = self.stack.enter_context(nc.semaphore('sem_' + e))
            self.cnt[e] = 0
        for k in range(N_DMA_SEMS):
            key = ('dma', k)
            self.sem[key] = self.stack.enter_context(nc.semaphore('sem_dma%d' % k))
            self.cnt[key] = 0
        self.dma_rr = 0
        self.n_ops = 0
        self.pending = {e: [] for e in ENGS}

    def sb(self, name, shape, dtype):
        return self.stack.enter_context(_sbt(self.nc, name, list(shape), dtype))

    def ps(self, name, shape, dtype):
        return self.stack.enter_context(self.nc.psum_tensor(name, list(shape), dtype))

    def _deps(self, eng, reads, writes):
        toks = []
        for r in reads:
            t = self.last_w.get(r)
            if t is not None:
                toks.append((t, False))
        for w in writes:
            t = self.last_w.get(w)
            if t is not None:
                toks.append((t, False))
            for sk, (val, peng) in self.readers.get(w, {}).items():
                toks.append(((sk, val, peng), True))
        waits = []
        for (sk, val, peng), is_war in toks:
            if peng == eng:
                if eng == 'pe' or is_war:
                    continue
            if self.waited[eng].get(sk, 0) >= val:
                continue
            self.waited[eng][sk] = val
            waits.append((sk, val))
        best = {}
        for sk, val in waits:
            best[sk] = max(best.get(sk, 0), val)
        return list(best.items())

    def _commit(self, tok, reads, writes):
        for w in writes:
            self.last_w[w] = tok
            self.readers[w] = {}
        for r in reads:
            d = self.readers.setdefault(r, {})
            sk, val, peng = tok
            if d.get(sk, (0, None))[0] < val:
                d[sk] = (val, peng)

    def barrier(self):
        for e in ENGS:
            for k, v in self.cnt.items():
                if v > 0 and self.waited[e].get(k, 0) < v:
                    self.waited[e][k] = v
                    self.pending[e].append((k, v))

    def _take_pending(self, eng, waits):
        if self.pending[eng]:
            best = dict(waits)
            for k, v in self.pending[eng]:
                best[k] = max(best.get(k, 0), v)
            self.pending[eng] = []
            return list(best.items())
        return waits

    def op(self, eng, fn, reads=(), writes=()):
        waits = self._take_pending(eng, self._deps(eng, reads, writes))
        self.cnt[eng] += 1
        tok = (eng, self.cnt[eng], eng)
        self.ops[eng].append((waits, fn, eng, 1))
        self._commit(tok, reads, writes)
        self.n_ops += 1

    def dma(self, q, out, in_, reads=(), writes=(), fn=None, **kw):
        k = self.dma_rr
        self.dma_rr = (self.dma_rr + 1) % N_DMA_SEMS
        key = ('dma', k)
        waits = self._deps(q, reads, writes)
        if self.cnt[key] > 0 and self.waited[q].get(key, 0) < self.cnt[key]:
            self.waited[q][key] = self.cnt[key]
            waits = [w for w in waits if w[0] != key] + [(key, self.cnt[key])]
        waits = self._take_pending(q, waits)
        self.cnt[key] += 16
        tok = (key, self.cnt[key], 'dma')
        if fn is None:
            fn = lambda e: e.dma_start(out=out, in_=in_, **kw)
        self.ops[q].append((waits, fn, key, 16))
        self._commit(tok, reads, writes)
        self.n_ops += 1

    def finish(self):
        nc = self.nc
        fin = [(k, v) for k, v in self.cnt.items() if v > 0]
        ops = self.ops
        sem = self.sem

        def replay(name):
            def run(e):
                for waits, fn, sk, inc in ops[name]:
                    for wk, wv in waits:
                        e.wait_ge(sem[wk], wv)
                    ins = fn(e)
                    ins.then_inc(sem[sk], inc)
                if name == 'sp':
                    for k, v in fin:
                        e.wait_ge(sem[k], v)
            return run

        with nc.Block() as block:
            block.tensor(replay('pe'))
            block.scalar(replay('act'))
            block.vector(replay('dve'))
            block.gpsimd(replay('pool'))
            block.sync(replay('sp'))
        self.stack.close()


DEPTH = 4
D = 1024
KC = 8
T = 4352
NT = 34
GCOLS = 1672
ALPHA = (2.0 * DEPTH) ** 0.25
LN_EPS = 1e-5
NORM_EPS = 1e-6
BIG = 30000.0
BLOCKS = [(0, 256)] + [(256 + 512 * i, 512) for i in range(8)]
PADW = 4360
C_ID, C_ONE, C_BONE, C_RM, C_UF, C_UB, C_SF, C_SB, C_MF, C_MB = [128 * i for i in range(10)]
NCONST = 1280
ONLY = {'rg', 'na', 'gdn'}


def pcol(t):
    return t + 2 if t < 256 else t + 6


class E:
    def __init__(self, B):
        self.B = B

    def mm(self, out, lhsT, rhs, st, sp, r, w):
        self.B.op('pe', lambda e: e.matmul(out, lhsT=lhsT, rhs=rhs, start=st, stop=sp), r, w)

    def tr(self, out, in_, ident, r, w):
        self.B.op('pe', lambda e: e.transpose(out=out, in_=in_, identity=ident), r, w)

    def act(self, out, in_, func, r, w, bias=None, scale=None, accum=None):
        kw = {}
        if bias is not None:
            kw['bias'] = bias
        if scale is not None:
            kw['scale'] = scale
        if accum is not None:
            kw['accum_out'] = accum
        self.B.op('act', lambda e: e.activation(out=out, in_=in_, func=func, **kw), r, w)

    def tt(self, eng, out, in0, in1, op, r, w):
        self.B.op(eng, lambda e: e.tensor_tensor(out=out, in0=in0, in1=in1, op=op), r, w)

    def ts(self, eng, out, in0, s1, s2, op0, op1, r, w):
        if op1 is None:
            self.B.op(eng, lambda e: e.tensor_scalar(out=out, in0=in0, scalar1=s1, scalar2=None, op0=op0), r, w)
        else:
            self.B.op(eng, lambda e: e.tensor_scalar(out=out, in0=in0, scalar1=s1, scalar2=s2, op0=op0, op1=op1), r, w)

    def stt(self, out, in0, sc, in1, op0, op1, r, w):
        self.B.op('dve', lambda e: e.scalar_tensor_tensor(out=out, in0=in0, scalar=sc, in1=in1, op0=op0, op1=op1), r, w)

    def cp(self, eng, out, in_, r, w):
        if eng == 'act':
            self.B.op('act', lambda e: e.activation(out=out, in_=in_, func=AF.Copy), r, w)
        else:
            self.B.op(eng, lambda e: e.tensor_copy(out=out, in_=in_), r, w)

    def red(self, out, in_, op, r, w, negate=False):
        self.B.op('dve', lambda e: e.tensor_reduce(out=out, in_=in_, axis=AX.X, op=op, negate=negate), r, w)

    def recip(self, out, in_, r, w):
        self.B.op('dve', lambda e: e.reciprocal(out=out, in_=in_), r, w)

    def memset(self, eng, ap, val, w):
        self.B.op(eng, lambda e: e.memset(ap, val), (), w)

    def scan(self, out, d0, d1, init, r, w):
        self.B.op('dve', lambda e: e.tensor_tensor_scan(out=out, data0=d0, data1=d1, initial=init,
                                                        op0=ALU.mult, op1=ALU.add), r, w)


def rev(ap):
    return ap[:, ::-1]


class Prog:
    def __init__(self, G, steps, x_ext_out, final, debug=False):
        self.G = G
        self.steps = steps
        self.final = final
        nc = self.nc = bass.Bass("TRN2", target_bir_lowering=False)
        self.B = B = Builder(nc)
        self.e = E(B)
        self.debug = debug
        self.dbg_outs = {}
        L = DEPTH

        def din(name, shape, dt=F32):
            return nc.dram_tensor(name, list(shape), dt, kind="ExternalInput").ap()

        def dout(name, shape, dt=F32):
            return nc.dram_tensor(name, list(shape), dt, kind="ExternalOutput").ap()

        def dint(name, shape, dt=F32):
            return nc.dram_tensor(name, list(shape), dt, kind="Internal").ap()

        self.dout = dout
        self.x_in = din("x_in", [T, D])
        self.cc = din("cc", [128, KC, 2])
        self.consts_d = din("consts", [128, NCONST])
        self.w_mod = din("w_mod", [L, D, 3 * D])
        self.bmod_rep = din("bmod_rep", [L, 128, 3 * D])
        self.w_in = din("w_in", [L, D, G * GCOLS])
        self.convw = din("convw", [L, G, 128, 5, 4])
        self.rgw = din("rgw", [L, G, 128, 2, 2, 192])
        self.rgb = din("rgb", [L, G, 128, 2, 2, 3])
        self.nab = din("nab", [L, G, 3, 128, 5, 640])
        self.galog = din("galog", [L, G, 128, NT, 4])
        self.gdtb = din("gdtb", [L, G, 128, NT, 4])
        self.gnw = din("gnw", [L, 128, 128])
        self.rope_cos = din("rope_cos", [128, 4096])
        self.rope_sin = din("rope_sin", [128, 4096])
        self.w_out = din("w_out", [L, D, D])
        self.lng_rep = din("lng_rep", [L, 128, D])
        self.lnb_rep = din("lnb_rep", [L, 128, D])
        has_front = any(s[0] == 'front' for s in steps)
        has_back = any(s[0] == 'back' for s in steps)
        self.uT_d = dint("uT_d", [128, KC, T], BF16)
        if steps[0][0] == 'back':
            self.yc_in = din("yc_in", [D, T], BF16)
        else:
            self.yc_in = None
        if has_front and x_ext_out:
            self.yc_out = dout("yc_out", [512 * G, T], BF16)
        else:
            self.yc_out = None
        self.yc_int = dint("yc_int", [D, T], BF16) if not x_ext_out else None
        self.x_ext_out = x_ext_out
        if x_ext_out and has_back and not final:
            self.x_out = dout("x_out", [T, D])
        else:
            self.x_out = None
        self.x_scr = [dint("x_scr0", [T, D]), dint("x_scr1", [T, D])] if not x_ext_out else None
        self.out_d = dout("out", [4096, D]) if final else None

        self.banks = B.ps("banks", [128, 8, 512], F32)
        self.cst = B.sb("cst", [128, NCONST], F32)
        self.identb = B.sb("identb", [128, 128], BF16)
        self.screp = B.sb("screp", [128, 2, KC, 128], F32)
        self.modcol = B.sb("modcol", [128, 16, 2], F32)
        self.gate_bc = B.sb("gate_bc", [128, 2, D], F32)
        self.build()

    def c(self, off, n=128, p0=0, p1=128):
        return self.cst[p0:p1, off:off + n]

    def bk(self, i):
        return ('bk', i)

    def dump(self, name, ap, shape, reads, dt=F32):
        if not self.debug:
            return
        d = self.nc.dram_tensor("dbg_" + name, list(shape), dt, kind="ExternalOutput").ap()
        self.dbg_outs[name] = d
        self.B.dma('pool', d, ap, reads=reads, writes=[('dbg', name)])

    def build(self):
        B, e = self.B, self.e
        B.dma('sp', self.cst[:], self.consts_d, writes=['cst'])
        e.cp('dve', self.identb[:], self.c(C_ID), ['cst'], ['identb'])
        with ExitStack() as ph:
            cct = ph.enter_context(_sbt(self.nc, "cct", [128, KC, 2], F32))
            sct = ph.enter_context(_sbt(self.nc, "sct", [128, KC, 2], F32))
            B.dma('sp', cct[:], self.cc, writes=['cct'])
            e.act(sct[:], cct[:], AF.Silu, ['cct'], ['sct'])
            for j in range(2):
                for kc in range(KC):
                    e.ts('dve', self.screp[:, j, kc, :], self.c(C_ONE), sct[:, kc, j:j + 1], None, ALU.mult, None,
                         ['cst', 'sct'], ['screp'])
            B.barrier()
        x_cur = self.x_in
        nxt = 0
        for kind, l in self.steps:
            last = (l == DEPTH - 1)
            if kind == 'front':
                yc = self.yc_out if self.x_ext_out else self.yc_int
                self.mod_phase(l)
                self.u_phase(l, x_cur)
                for g in range(self.G):
                    row0 = 512 * g
                    if 'rg' in ONLY:
                        self.rg_pass(l, g, yc, row0)
                    if 'na' in ONLY:
                        self.na_pass(l, g, yc, row0 + 192, ctx_out=not last)
                    if 'gdn' in ONLY:
                        self.gdn_pass(l, g, yc, row0 + 384)
            else:
                if self.yc_in is not None and (kind, l) == self.steps[0]:
                    yc = self.yc_in
                    self.mod_phase(l, gate_only=True)
                else:
                    yc = self.yc_int
                if last:
                    self.back_phase(l, x_cur, yc, self.out_d, True)
                else:
                    if self.x_ext_out:
                        xn = self.x_out
                    else:
                        xn = self.x_scr[nxt]
                        nxt ^= 1
                    self.back_phase(l, x_cur, yc, xn, False)
                    x_cur = xn
        B.finish()

    def mod_phase(self, l, gate_only=False):
        B, e, nc = self.B, self.e, self.nc
        with ExitStack() as ph:
            wm = [ph.enter_context(_sbt(nc, "wm%d" % i, [128, 3 * D], F32)) for i in range(2)]
            modbc = ph.enter_context(_sbt(nc, "modbc", [128, 3 * D], F32))
            bmod = ph.enter_context(_sbt(nc, "bmod", [128, 3 * D], F32))
            junk = ph.enter_context(_sbt(nc, "junk", [128, 128], F32))
            B.dma('sp', bmod[:], self.bmod_rep[l], writes=['bmod'])
            for j in range(2):
                for kc in range(KC):
                    b = kc % 2
                    B.dma('sp', wm[b][:], self.w_mod[l, kc * 128:(kc + 1) * 128, :], writes=[('wm', b)])
                    for nb in range(6):
                        e.mm(self.banks[:, nb, :], self.screp[:, j, kc, :], wm[b][:, nb * 512:(nb + 1) * 512],
                             kc == 0, kc == KC - 1, [('wm', b), 'screp'], [self.bk(nb)])
                for nb in range(6):
                    e.tt('dve', modbc[:, nb * 512:(nb + 1) * 512], self.banks[:, nb, :],
                         bmod[:, nb * 512:(nb + 1) * 512], ALU.add, ['bmod'], ['modbc', self.bk(nb)])
                if not gate_only:
                    for ch in range(16):
                        e.tt('dve', junk[:], modbc[:, ch * 128:(ch + 1) * 128], self.c(C_ID), ALU.mult,
                             ['modbc', 'cst'], ['junk'])
                        e.red(self.modcol[:, ch, j:j + 1], junk[:], ALU.add, ['junk'], ['modcol'])
                e.cp('pool', self.gate_bc[:, j, :], modbc[:, 2 * D:3 * D], ['modbc'], ['gate_bc'])
            if not gate_only:
                e.ts('dve', self.modcol[:, 8:16, :], self.modcol[:, 8:16, :], 1.0, None, ALU.add, None,
                     ['modcol'], ['modcol'])
            B.barrier()

    def u_phase(self, l, x_cur):
        B, e, nc = self.B, self.e, self.nc
        with ExitStack() as ph:
            xt = [ph.enter_context(_sbt(nc, "u_xt%d" % i, [128, D], F32)) for i in range(2)]
            ut = [ph.enter_context(_sbt(nc, "u_ut%d" % i, [128, KC, 128], BF16)) for i in range(2)]
            for ti in range(NT):
                j = 1 if ti < 2 else 0
                b = ti % 2
                B.dma('sp', xt[b][:], x_cur[ti * 128:(ti + 1) * 128, :], reads=['X'], writes=[('u_xt', b)])
                for kc in range(KC):
                    bank = 2 * b + kc // 4
                    e.tr(self.banks[:, bank, (kc % 4) * 128:(kc % 4 + 1) * 128], xt[b][:, kc * 128:(kc + 1) * 128],
                         self.c(C_ID), [('u_xt', b), 'cst'], [self.bk(bank)])
                for kc in range(KC):
                    bank = 2 * b + kc // 4
                    src = self.banks[:, bank, (kc % 4) * 128:(kc % 4 + 1) * 128]
                    if kc % 2 == 0:
                        e.ts('dve', ut[b][:, kc, :], src, self.modcol[:, 8 + kc, j:j + 1], self.modcol[:, kc, j:j + 1],
                             ALU.mult, ALU.add, ['modcol'], [('u_ut', b), self.bk(bank)])
                    else:
                        e.act(ut[b][:, kc, :], src, AF.Identity, ['modcol'], [('u_ut', b), self.bk(bank)],
                              bias=self.modcol[:, kc, j:j + 1], scale=self.modcol[:, 8 + kc, j:j + 1])
                B.dma('pool', self.uT_d[:, :, ti * 128:(ti + 1) * 128], ut[b][:], reads=[('u_ut', b)], writes=['uT'])
            B.barrier()

    def load_w(self, ph, l, g, col0, ncols, tag):
        B, e, nc = self.B, self.e, self.nc
        wsb = ph.enter_context(_sbt(nc, "wsb_" + tag, [128, KC, ncols], BF16))
        stg = [ph.enter_context(_sbt(nc, "wstg%d_%s" % (i, tag), [128, ncols], F32)) for i in range(2)]
        c0 = g * GCOLS + col0
        for kc in range(KC):
            b = kc % 2
            B.dma('sp', stg[b][:], self.w_in[l, kc * 128:(kc + 1) * 128, c0:c0 + ncols], writes=[('wstg', b)])
            e.cp('pool', wsb[:, kc, :], stg[b][:], [('wstg', b)], ['wsb'])
        return wsb

    def proj(self, ph, wsb, fm_tiles, tm_ranges, fm_sink, tm_sink):
        B, e, nc = self.B, self.e, self.nc
        ub = [ph.enter_context(_sbt(nc, "ub%d" % i, [128, KC, 512], BF16)) for i in range(2)]
        rr = 0
        for bi, (s, n) in enumerate(BLOCKS):
            b = bi % 2
            B.dma('sp', ub[b][:, :, 0:n], self.uT_d[:, :, s:s + n], reads=['uT'], writes=[('ub', b)])
            for idx, (c0, M) in enumerate(fm_tiles):
                bank = rr % 4
                rr += 1
                for kc in range(KC):
                    e.mm(self.banks[0:M, bank, 0:n], wsb[:, kc, c0:c0 + M], ub[b][:, kc, 0:n], kc == 0, kc == KC - 1,
                         ['wsb', ('ub', b)], [self.bk(bank)])
                fm_sink(idx, s, n, self.banks[0:M, bank, 0:n], self.bk(bank))
            for tl in range(n // 128):
                ti = s // 128 + tl
                for idx, (c0, W) in enumerate(tm_ranges):
                    bank = rr % 4
                    rr += 1
                    for kc in range(KC):
                        e.mm(self.banks[:, bank, 0:W], ub[b][:, kc, tl * 128:(tl + 1) * 128], wsb[:, kc, c0:c0 + W],
                             kc == 0, kc == KC - 1, ['wsb', ('ub', b)], [self.bk(bank)])
                    tm_sink(idx, ti, self.banks[:, bank, 0:W], self.bk(bank))

    def conv(self, P, out, cw, tile, np_, r, w):
        e = self.e
        for (s, n) in [(0, 256), (256, 4096)]:
            base = pcol(s) - 2
            o = out[0:np_, s:s + n]
            e.ts('dve', o, P[0:np_, base:base + n], cw[0:np_, tile, 0:1], None, ALU.mult, None, r, w)
            for j in range(1, 4):
                e.stt(o, P[0:np_, base + j:base + j + n], cw[0:np_, tile, j:j + 1], o, ALU.mult, ALU.add, r, w)

    def zero_pads(self, P, key):
        e = self.e
        for a, b in [(0, 2), (258, 262), (4358, 4360)]:
            e.memset('pool', P[:, a:b], 0.0, [key])

    def rg_pass(self, l, g, yc, row0):
        B, e, nc = self.B, self.e, self.nc
        with ExitStack() as p0:
            xa = [p0.enter_context(_sbt(nc, "rg_xa%d" % i, [128, T], F32)) for i in range(2)]
            zs = [p0.enter_context(_sbt(nc, "rg_zs%d" % i, [128, T], BF16)) for i in range(2)]
            cw = p0.enter_context(_sbt(nc, "rg_cw", [128, 5, 4], F32))
            B.dma('sp', cw[:], self.convw[l, g], writes=['cw'])
            NP = [128, 64]
            with ExitStack() as ph:
                wsb = self.load_w(ph, l, g, 0, 384, "rg")
                P = [ph.enter_context(_sbt(nc, "rg_P%d" % i, [128, PADW], F32)) for i in range(2)]
                for i in range(2):
                    self.zero_pads(P[i], ('rgP', i))

                def fm_sink(idx, s, n, ps, bkey):
                    if idx < 2:
                        e.cp('act', P[idx][0:NP[idx], pcol(s):pcol(s) + n], ps, [], [bkey, ('rgP', idx)])
                    else:
                        i = idx - 2
                        e.act(zs[i][0:NP[i], s:s + n], ps, AF.Silu, [], [bkey, ('rg_zs', i)])

                self.proj(ph, wsb, [(0, 128), (128, 64), (192, 128), (320, 64)], [], fm_sink, None)
                for i in range(2):
                    self.conv(P[i], xa[i], cw, i, NP[i], [('rgP', i), 'cw'], [('rg_xa', i)])
                B.barrier()
            if self.debug:
                self.dump("rg_xa0_%d_%d" % (l, g), xa[0][:], [128, T], [('rg_xa', 0)])
            with ExitStack() as ph:
                hb = [ph.enter_context(_sbt(nc, "rg_hb%d" % i, [128, T], F32)) for i in range(2)]
                w = ph.enter_context(_sbt(nc, "rg_w", [128, 2, 2, 192], F32))
                bb = ph.enter_context(_sbt(nc, "rg_b", [128, 2, 2, 3], F32))
                c8 = ph.enter_context(_sbt(nc, "rg_c8", [128, 2, 2], F32))
                tmp = {}
                for nm in ['r', 'i', 'a', 's']:
                    for i in range(2):
                        tmp[(nm, i)] = ph.enter_context(_sbt(nc, "rg_t%s%d" % (nm, i), [128, 512], F32))
                hf = [[ph.enter_context(_sbt(nc, "rg_hf%d_%d" % (i, k), [128, 512], F32)) for k in range(2)]
                      for i in range(2)]
                yb = [[ph.enter_context(_sbt(nc, "rg_yb%d_%d" % (i, k), [128, 512], BF16)) for k in range(2)]
                      for i in range(2)]
                B.dma('sp', w[:], self.rgw[l, g], writes=['rg_w'])
                B.dma('sp', bb[:], self.rgb[l, g], writes=['rg_b'])
                e.act(c8[:], bb[:, :, :, 2], AF.Exp, ['rg_b'], ['rg_c8'], scale=-1.0)
                e.act(c8[:], c8[:], AF.Ln, ['rg_c8'], ['rg_c8'], bias=1.0)
                e.ts('dve', c8[:], c8[:], -8.0, None, ALU.mult, None, ['rg_c8'], ['rg_c8'])
                wcols = [(0, 128), (128, 192)]
                for d in (1, 0):
                    order = BLOCKS if d == 0 else [BLOCKS[0]] + BLOCKS[:0:-1]
                    prev = [None, None]
                    for bi, (s, n) in enumerate(order):
                        for i in range(2):
                            np_ = NP[i]
                            tr_, ti_, ta_, ts_ = (tmp[(nm, i)][0:np_, 0:n] for nm in ['r', 'i', 'a', 's'])
                            kr, ki, ka, ks = (('rg_t', nm, i) for nm in ['r', 'i', 'a', 's'])
                            xin = xa[i][0:np_, s:s + n]
                            b0, b1 = 4 + 2 * i, 5 + 2 * i
                            c0, c1 = wcols[i]
                            e.mm(self.banks[0:np_, b0, 0:n], w[0:np_, d, 0, c0:c1], xin, True, True,
                                 ['rg_w', ('rg_xa', i)], [self.bk(b0)])
                            e.mm(self.banks[0:np_, b1, 0:n], w[0:np_, d, 1, c0:c1], xin, True, True,
                                 ['rg_w', ('rg_xa', i)], [self.bk(b1)])
                            e.act(tr_, self.banks[0:np_, b0, 0:n], AF.Sigmoid, ['rg_b'], [kr, self.bk(b0)],
                                  bias=bb[0:np_, i, d, 0:1])
                            e.act(ti_, self.banks[0:np_, b1, 0:n], AF.Sigmoid, ['rg_b'], [ki, self.bk(b1)],
                                  bias=bb[0:np_, i, d, 1:2])
                            e.act(ta_, tr_, AF.Exp, [kr, 'rg_c8'], [ka], scale=c8[0:np_, i, d:d + 1])
                            e.tt('pool', ts_, ta_, ta_, ALU.mult, [ka], [ks])
                            e.act(ts_, ts_, AF.Sqrt, [ks], [ks], bias=1.0, scale=-1.0)
                            e.tt('dve', ti_, ts_, ti_, ALU.mult, [ks, ki], [ki])
                            e.tt('pool', ti_, ti_, xin, ALU.mult, [ki, ('rg_xa', i)], [ki])
                            init = prev[i] if prev[i] is not None else 0.0
                            if d == 1:
                                dst = hb[i][0:np_, s:s + n]
                                e.scan(rev(dst), rev(ta_), rev(ti_), init, [ka, ki, ('rg_hb', i)], [('rg_hb', i)])
                                prev[i] = hb[i][0:np_, s:s + 1]
                            else:
                                k = bi % 2
                                dst = hf[i][k][0:np_, 0:n]
                                e.scan(dst, ta_, ti_, init, [ka, ki, ('rg_hf', i, 1 - k)], [('rg_hf', i, k)])
                                prev[i] = hf[i][k][0:np_, n - 1:n]
                                yt = yb[i][k][0:np_, 0:n]
                                e.tt('pool', ts_, dst, hb[i][0:np_, s:s + n], ALU.add, [('rg_hf', i, k), ('rg_hb', i)], [ks])
                                e.tt('dve', yt, ts_, zs[i][0:np_, s:s + n], ALU.mult, [ks, ('rg_zs', i)], [('rg_yb', i, k)])
                                r0 = row0 + 128 * i
                                B.dma('pool', yc[r0:r0 + np_, s:s + n], yt, reads=[('rg_yb', i, k)], writes=['YC'])
                B.barrier()

    def na_pass(self, l, g, yc, row0, ctx_out):
        B, e, nc = self.B, self.e, self.nc
        bnk = self.banks
        NP = [128, 64]
        with ExitStack() as p0:
            qT = [p0.enter_context(_sbt(nc, "na_q%d" % i, [128, T], BF16)) for i in range(2)]
            kT = [p0.enter_context(_sbt(nc, "na_k%d" % i, [128, T], BF16)) for i in range(2)]
            zn = p0.enter_context(_sbt(nc, "na_zn", [128, NT, 192], BF16))
            v_tm = p0.enter_context(_sbt(nc, "na_v", [128, NT, 192], BF16))
            with ExitStack() as ph:
                wsb = self.load_w(ph, l, g, 384, 768, "na")

                def fm_sink(idx, s, n, ps, bkey):
                    i = idx % 2
                    if idx < 2:
                        e.act(qT[i][0:NP[i], s:s + n], ps, AF.Copy, [], [bkey, 'na_q'], scale=0.125)
                    else:
                        e.cp('dve', kT[i][0:NP[i], s:s + n], ps, [], [bkey, 'na_k'])

                def tm_sink(idx, ti, ps, bkey):
                    e.act(zn[:, ti, :], ps[:, 0:192], AF.Silu, [], [bkey, 'na_zn'])
                    e.cp('dve', v_tm[:, ti, :], ps[:, 192:384], [], [bkey, 'na_v'])

                self.proj(ph, wsb, [(0, 128), (128, 64), (192, 128), (320, 64)], [(384, 384)], fm_sink, tm_sink)
                B.barrier()
            with ExitStack() as ph:
                def sbt(name, shape, dt=F32):
                    return ph.enter_context(_sbt(nc, name, shape, dt))
                nab = sbt("na_bias", [128, 3, 5, 640])
                NB = 2
                S = [[sbt("na_S%d_%d" % (p, h), [128, 896]) for h in range(3)] for p in range(NB)]
                Pb = [[sbt("na_Pb%d_%d" % (p, h), [128, 896], BF16) for h in range(3)] for p in range(NB)]
                PT = [[sbt("na_PT%d_%d" % (p, h), [128, 896], BF16) for h in range(3)] for p in range(NB)]
                st = [[sbt("na_st%d_%d" % (p, h), [128, 4]) for h in range(3)] for p in range(NB)]
                ytm = [sbt("na_ytm%d" % p, [128, 192], BF16) for p in range(NB)]
                ybT = [sbt("na_ybT%d" % p, [128, 256], BF16) for p in range(NB)]
                for h in range(3):
                    B.dma('sp', nab[:, h], self.nab[l, g, h], writes=['na_bias'])

                def qt_gen(it, qt, lat):
                    p = it % NB
                    if lat:
                        cls = {0: 0, 1: 1, 30: 3, 31: 4}.get(qt, 2)
                        kt0 = min(max(qt - 2, 0), 27)
                        tq = 2 + qt
                        kcol = (2 + kt0) * 128
                        ktiles = [2 + kt0 + c for c in range(5)] + [0, 1]
                        nk = 896
                    else:
                        tq = qt
                        ktiles = [0, 1]
                        nk = 256
                    qcol = tq * 128
                    nch = nk // 128
                    hs = range(3)
                    HR = [(0, 0), (0, 64), (1, 0)]
                    kS = [('na_S', p, h) for h in hs]
                    kP = [('na_Pb', p, h) for h in hs]
                    kPT = [('na_PT', p, h) for h in hs]
                    kst = [('na_st', p, h) for h in hs]
                    for h in hs:
                        ti_, hr0 = HR[h]
                        qa = qT[ti_][hr0:hr0 + 64, qcol:qcol + 128]
                        k_ = kT[ti_]
                        b0, b1 = 2 * h, 2 * h + 1
                        if lat:
                            e.mm(bnk[:, b0, 0:512], qa, k_[hr0:hr0 + 64, kcol:kcol + 512], True, True,
                                 ['na_q', 'na_k'], [self.bk(b0)])
                            e.mm(bnk[:, b1, 0:128], qa, k_[hr0:hr0 + 64, kcol + 512:kcol + 640], True, True,
                                 ['na_q', 'na_k'], [self.bk(b1)])
                            e.mm(bnk[:, b1, 128:384], qa, k_[hr0:hr0 + 64, 0:256], True, True,
                                 ['na_q', 'na_k'], [self.bk(b1)])
                        else:
                            e.mm(bnk[:, b1, 128:384], qa, k_[hr0:hr0 + 64, 0:256], True, True,
                                 ['na_q', 'na_k'], [self.bk(b1)])
                    yield
                    for h in hs:
                        b0, b1 = 2 * h, 2 * h + 1
                        if lat:
                            e.tt('dve', S[p][h][:, 0:512], bnk[:, b0, :], nab[:, h, cls, 0:512], ALU.add,
                                 ['na_bias'], [kS[h], self.bk(b0)])
                            e.tt('dve', S[p][h][:, 512:640], bnk[:, b1, 0:128], nab[:, h, cls, 512:640], ALU.add,
                                 ['na_bias'], [kS[h], self.bk(b1)])
                            e.cp('act', S[p][h][:, 640:896], bnk[:, b1, 128:384], [], [kS[h], self.bk(b1)])
                        else:
                            e.cp('act', S[p][h][:, 0:256], bnk[:, b1, 128:384], [], [kS[h], self.bk(b1)])
                    yield
                    for h in hs:
                        e.red(st[p][h][:, 0:1], S[p][h][:, 0:nk], ALU.max, [kS[h]], [kst[h]], negate=True)
                    yield
                    for h in hs:
                        e.act(Pb[p][h][:, 0:nk], S[p][h][:, 0:nk], AF.Exp, [kS[h], kst[h]], [kP[h], (kst[h], 'sum')],
                              bias=st[p][h][:, 0:1], scale=1.0, accum=st[p][h][:, 1:2])
                    yield
                    for h in hs:
                        pbf = bnk[:, 2 * h, :].bitcast(BF16)
                        for c in range(nch):
                            e.tr(pbf[:, c * 128:(c + 1) * 128], Pb[p][h][:, c * 128:(c + 1) * 128], self.identb[:],
                                 [kP[h], 'identb'], [self.bk(2 * h)])
                        e.recip(st[p][h][:, 2:3], st[p][h][:, 1:2], [(kst[h], 'sum')], [(kst[h], 'ri')])
                    yield
                    for h in hs:
                        pbf = bnk[:, 2 * h, :].bitcast(BF16)
                        e.cp('act' if h != 1 else 'dve', PT[p][h][:, 0:nk], pbf[:, 0:nk], [], [kPT[h], self.bk(2 * h)])
                    yield
                    for h in hs:
                        for c in range(nch):
                            e.mm(bnk[:, 6, h * 64:(h + 1) * 64], PT[p][h][:, c * 128:(c + 1) * 128],
                                 v_tm[:, ktiles[c], h * 64:(h + 1) * 64], c == 0, c == nch - 1,
                                 ['na_v', kPT[h]], [self.bk(6)])
                    yield
                    for h in hs:
                        e.stt(ytm[p][:, h * 64:(h + 1) * 64], bnk[:, 6, h * 64:(h + 1) * 64], st[p][h][:, 2:3],
                              zn[:, tq, h * 64:(h + 1) * 64], ALU.mult, ALU.mult, [(kst[h], 'ri'), 'na_zn'],
                              [('na_ytm', p), self.bk(6)])
                    yield
                    pb7 = bnk[:, 7, :].bitcast(BF16)
                    e.tr(pb7[:, 0:128], ytm[p][:, 0:128], self.identb[:], [('na_ytm', p), 'identb'], [self.bk(7)])
                    e.tr(pb7[0:64, 128:256], ytm[p][:, 128:192], self.identb[:], [('na_ytm', p), 'identb'], [self.bk(7)])
                    yield
                    e.cp('act', ybT[p][:, 0:128], pb7[:, 0:128], [], [('na_ybT', p), self.bk(7)])
                    e.cp('dve', ybT[p][0:64, 128:256], pb7[0:64, 128:256], [], [('na_ybT', p), self.bk(7)])
                    yield
                    B.dma('pool', yc[row0:row0 + 128, qcol:qcol + 128], ybT[p][:, 0:128], reads=[('na_ybT', p)],
                          writes=['YC'])
                    B.dma('pool', yc[row0 + 128:row0 + 192, qcol:qcol + 128], ybT[p][0:64, 128:256],
                          reads=[('na_ybT', p)], writes=['YC'])
                    yield

                qts = [(qt, True) for qt in range(32)]
                if ctx_out:
                    qts += [(0, False), (1, False)]
                pending = [qt_gen(i, qt, lat) for i, (qt, lat) in enumerate(qts)]
                active = []
                rounds = 0
                while pending or active:
                    if pending and (rounds % 5 == 0) and len(active) < NB:
                        active.append(pending.pop(0))
                    nxt = []
                    for gnr in active:
                        try:
                            next(gnr)
                            nxt.append(gnr)
                        except StopIteration:
                            pass
                    active = nxt
                    rounds += 1
                B.barrier()

    def gdn_pass(self, l, g, yc, row0):
        B, e, nc = self.B, self.e, self.nc
        bnk = self.banks
        with ExitStack() as p0:
            qT = p0.enter_context(_sbt(nc, "gd_q", [128, T], F32))
            kT = p0.enter_context(_sbt(nc, "gd_k", [128, T], F32))
            zg = p0.enter_context(_sbt(nc, "gd_zg", [128, NT, 128], BF16))
            ba = p0.enter_context(_sbt(nc, "gd_ba", [128, NT, 8], F32))
            vT = p0.enter_context(_sbt(nc, "gd_v", [128, T], F32))
            cw = p0.enter_context(_sbt(nc, "gd_cw", [128, 5, 4], F32))
            with ExitStack() as p1:
                B.dma('sp', cw[:], self.convw[l, g], writes=['cw'])
                with ExitStack() as ph:
                    wsb = self.load_w(ph, l, g, 1152, 520, "gd")
                    P = [ph.enter_context(_sbt(nc, "gd_P%d" % i, [128, PADW], F32)) for i in range(3)]
                    for i in range(3):
                        self.zero_pads(P[i], ('gdP', i))

                    def fm_sink(idx, s, n, ps, bkey):
                        e.cp('act', P[idx][:, pcol(s):pcol(s) + n], ps, [], [bkey, ('gdP', idx)])

                    def tm_sink(idx, ti, ps, bkey):
                        e.act(zg[:, ti, :], ps[:, 0:128], AF.Silu, [], [bkey, 'gd_zg'])
                        e.cp('dve', ba[:, ti, :], ps[:, 128:136], [], [bkey, 'gd_ba'])

                    self.proj(ph, wsb, [(0, 128), (128, 128), (256, 128)], [(384, 136)], fm_sink, tm_sink)
                    for i, dst in enumerate([qT, kT, vT]):
                        self.conv(P[i], dst, cw, 2 + i, 128, [('gdP', i), 'cw'], [('gd_c', i)])
                    B.barrier()
                kv = p1.enter_context(_sbt(nc, "gd_kv", [128, NT, 256], F32))
                with ExitStack() as ph:
                    sq = [ph.enter_context(_sbt(nc, "gd_sq%d" % i, [128, 512], F32)) for i in range(2)]
                    nr = [ph.enter_context(_sbt(nc, "gd_nr%d" % i, [128, 512], F32)) for i in range(2)]
                    t1 = [ph.enter_context(_sbt(nc, "gd_t1%d" % i, [128, 512], F32)) for i in range(2)]
                    cs = [ph.enter_context(_sbt(nc, "gd_cs%d" % i, [128, 512], F32)) for i in range(2)]
                    sn = [ph.enter_context(_sbt(nc, "gd_sn%d" % i, [128, 512], F32)) for i in range(2)]
                    for bi, (s, n) in enumerate(BLOCKS):
                        lat = s >= 256
                        pb = bi % 2
                        if lat:
                            B.dma('sp', cs[pb][:], self.rope_cos[:, s - 256:s - 256 + 512], writes=[('gd_cs', pb)])
                            B.dma('sp', sn[pb][:], self.rope_sin[:, s - 256:s - 256 + 512], writes=[('gd_sn', pb)])
                        for wi, Xt in enumerate([qT, kT]):
                            X = Xt[:, s:s + n]
                            kx = ('gd_c', wi)
                            p = wi
                            b0, b1 = 4 + 2 * p, 5 + 2 * p
                            e.act(X, X, AF.Silu, [kx], [kx])
                            e.tt('pool', sq[p][:, 0:n], X, X, ALU.mult, [kx], [('gd_sq', p)])
                            e.mm(bnk[:, b0, 0:n], self.c(C_BONE), sq[p][:, 0:n], True, True, [('gd_sq', p), 'cst'],
                                 [self.bk(b0)])
                            e.act(nr[p][:, 0:n], bnk[:, b0, 0:n], AF.Sqrt, [], [('gd_nr', p), self.bk(b0)], bias=NORM_EPS)
                            e.recip(nr[p][:, 0:n], nr[p][:, 0:n], [('gd_nr', p)], [('gd_nr', p)])
                            if wi == 0:
                                e.stt(X, X, 0.125, nr[p][:, 0:n], ALU.mult, ALU.mult, [kx, ('gd_nr', p)], [kx])
                            else:
                                e.tt('dve', X, X, nr[p][:, 0:n], ALU.mult, [kx, ('gd_nr', p)], [kx])
                            if lat:
                                e.mm(bnk[:, b1, 0:n], self.c(C_RM), X, True, True, [kx, 'cst'], [self.bk(b1)])
                                e.tt('pool', t1[p][:, 0:n], X, cs[pb][:, 0:n], ALU.mult, [kx, ('gd_cs', pb)], [('gd_t1', p)])
                                e.tt('dve', X, bnk[:, b1, 0:n], sn[pb][:, 0:n], ALU.mult, [('gd_sn', pb)],
                                     [kx, self.bk(b1)])
                                e.tt('pool', X, X, t1[p][:, 0:n], ALU.add, [kx, ('gd_t1', p)], [kx])
                        e.act(vT[:, s:s + n], vT[:, s:s + n], AF.Silu, [('gd_c', 2)], [('gd_c', 2)])
                    for ti in range(NT):
                        b0 = ti % 4
                        e.tr(bnk[:, b0, 0:128], kT[:, ti * 128:(ti + 1) * 128], self.c(C_ID), [('gd_c', 1), 'cst'],
                             [self.bk(b0)])
                        e.tr(bnk[:, b0, 128:256], vT[:, ti * 128:(ti + 1) * 128], self.c(C_ID), [('gd_c', 2), 'cst'],
                             [self.bk(b0)])
                        e.cp('act' if ti % 2 else 'dve', kv[:, ti, :], bnk[:, b0, 0:256], [], ['gd_kv', self.bk(b0)])
                    B.barrier()
                if self.debug:
                    self.dump("gd_q_%d_%d" % (l, g), qT[:], [128, T], [('gd_c', 0)])
                    self.dump("gd_k_%d_%d" % (l, g), kT[:], [128, T], [('gd_c', 1)])
                    self.dump("gd_kv_%d_%d" % (l, g), kv[:], [128, NT, 256], ['gd_kv'])
                with ExitStack() as ph:
                    def sbt(name, shape, dt=F32):
                        return ph.enter_context(_sbt(nc, name, shape, dt))
                    o_tm = [sbt("gd_o0", [128, NT, 128]), vT[:].rearrange("p (t c) -> p t c", c=128)]
                    al = sbt("gd_al", [128, NT, 4])
                    dtb = sbt("gd_dtb", [128, NT, 4])
                    g_all = sbt("gd_g", [128, NT, 4])
                    beta = sbt("gd_beta", [128, NT, 4])
                    nbeta = sbt("gd_nbeta", [128, NT, 4])
                    tg = sbt("gd_tg", [128, NT, 4])
                    B.dma('sp', al[:], self.galog[l, g], writes=['gd_al'])
                    B.dma('sp', dtb[:], self.gdtb[l, g], writes=['gd_dtb'])
                    e.act(beta[:], ba[:, :, 0:4], AF.Sigmoid, ['gd_ba'], ['gd_beta'])
                    e.ts('dve', nbeta[:], beta[:], -1.0, None, ALU.mult, None, ['gd_beta'], ['gd_nbeta'])
                    e.tt('dve', tg[:], ba[:, :, 4:8], dtb[:], ALU.add, ['gd_ba', 'gd_dtb'], ['gd_tg'])
                    e.act(tg[:], tg[:], AF.Exp, ['gd_tg'], ['gd_tg'])
                    e.act(tg[:], tg[:], AF.Ln, ['gd_tg'], ['gd_tg'], bias=1.0)
                    e.act(al[:], al[:], AF.Exp, ['gd_al'], ['gd_al'])
                    e.stt(g_all[:], tg[:], -1.0, al[:], ALU.mult, ALU.mult, ['gd_tg', 'gd_al'], ['gd_g'])
                    if self.debug:
                        self.dump("gd_g_%d_%d" % (l, g), g_all[:], [128, NT, 4], ['gd_g'])
                        self.dump("gd_beta_%d_%d" % (l, g), beta[:], [128, NT, 4], ['gd_beta'])
                    R = 3
                    chains = [(d, h) for d in range(2) for h in range(2)]
                    T2t = {c: [sbt("gd_T2t%d%d_%d" % (c[0], c[1], r), [128, 128]) for r in range(R)] for c in chains}
                    QKt = {c: [sbt("gd_QKt%d%d_%d" % (c[0], c[1], r), [128, 128]) for r in range(R)] for c in chains}
                    vec = {c: [sbt("gd_vec%d%d_%d" % (c[0], c[1], r), [128, 4]) for r in range(R)] for c in chains}
                    Sst = {c: sbt("gd_S%d%d" % c, [128, 64]) for c in chains}
                    Rp = {c: sbt("gd_Rp%d%d" % c, [128, 64]) for c in chains}
                    qs = {c: sbt("gd_qs%d%d" % c, [128, 64]) for c in chains}
                    vnw = {c: sbt("gd_vn%d%d" % c, [128, 64]) for c in chains}
                    kh = {c: sbt("gd_kh%d%d" % c, [128, 64]) for c in chains}
                    Gm = [sbt("gd_G%d" % i, [128, 128]) for i in range(4)]
                    nG = [sbt("gd_nG%d" % i, [128, 128]) for i in range(4)]
                    Et = [sbt("gd_Et%d" % i, [128, 128]) for i in range(4)]
                    tv = [sbt("gd_tv%d" % i, [128, 2]) for i in range(4)]
                    YT = [[sbt("gd_YT%d_%d" % (i, k), [128, 256]) for k in range(2)] for i in range(4)]
                    Yt = [[sbt("gd_Yt%d_%d" % (i, k), [128, 128]) for k in range(2)] for i in range(4)]
                    for c in chains:
                        e.memset('pool', Sst[c][:], 0.0, [('gd_S', c)])
                    ident = self.c(C_ID)
                    ones = self.c(C_ONE)
                    orders = {0: list(range(NT)), 1: [1, 0] + list(range(NT - 1, 1, -1))}

                    def prep(c, ti, slot):
                        d, h = c
                        col = 2 * d + h
                        pp = chains.index(c)
                        U = self.c(C_UF if d == 0 else C_UB)
                        Sm = self.c(C_SF if d == 0 else C_SB)
                        Mk = self.c(C_MF if d == 0 else C_MB)
                        hr0 = 64 * h
                        tc0 = ti * 128
                        gcol = g_all[:, ti, col:col + 1]
                        kG, knG, kEt, ktv = ('gd_G', pp), ('gd_nG', pp), ('gd_Et', pp), ('gd_tv', pp)
                        kslot = ('gd_slot', c, slot)
                        bP = pp
                        kb = self.bk(bP)
                        pD = bnk[:, bP, 0:128]
                        pV = bnk[:, bP, 128:130]
                        pKK = bnk[:, bP, 132:260]
                        pQK = bnk[:, bP, 260:388]
                        pI = bnk[:, bP, 0:256]
                        pJ = bnk[:, bP, 256:384]
                        e.ts('pool', Gm[pp][:], U, gcol, None, ALU.mult, None, ['cst', 'gd_g'], [kG])
                        e.ts('pool', nG[pp][:], U, gcol, -1.0, ALU.mult, ALU.mult, ['cst', 'gd_g'], [knG])
                        yield
                        e.mm(pD, ones, Gm[pp][:], True, False, ['cst', kG], [kb])
                        e.mm(pD, nG[pp][:], ones, False, False, ['cst', knG], [kb])
                        e.mm(pD, ident, Mk, False, True, ['cst'], [kb])
                        e.mm(bnk[:, bP, 128:129], U, gcol, True, True, ['cst', 'gd_g'], [kb])
                        e.mm(bnk[:, bP, 129:130], ones, gcol, True, True, ['cst', 'gd_g'], [kb])
                        ka = kT[hr0:hr0 + 64, tc0:tc0 + 128]
                        qa = qT[hr0:hr0 + 64, tc0:tc0 + 128]
                        e.mm(pKK, ka, ka, True, True, [('gd_c', 1)], [kb])
                        e.mm(pQK, ka, qa, True, True, [('gd_c', 1), ('gd_c', 0)], [kb])
                        yield
                        e.act(Et[pp][:], pD, AF.Exp, [], [kEt, kb])
                        e.cp('dve', tv[pp][:], pV, [], [ktv, kb])
                        yield
                        vc = vec[c][slot]
                        Y0 = YT[pp][0]
                        kYT = [('gd_YT', pp, 0), ('gd_YT', pp, 1)]
                        kYt = [('gd_Yt', pp, 0), ('gd_Yt', pp, 1)]
                        e.stt(Y0[:, 0:128], pKK, nbeta[:, ti, col:col + 1], Et[pp][:], ALU.mult, ALU.mult,
                              ['gd_nbeta', kEt], [kYT[0], kb])
                        e.tt('dve', QKt[c][slot][:], pQK, Et[pp][:], ALU.mult, [kEt], [(kslot, 'QK'), kb])
                        e.act(vc[:, 0:1], tv[pp][:, 0:1], AF.Exp, [ktv], [(kslot, 'v0')])
                        e.act(vc[:, 2:3], tv[pp][:, 0:1], AF.Exp, [ktv], [(kslot, 'v2')], bias=tv[pp][:, 1:2], scale=-1.0)
                        e.act(vc[:, 3:4], tv[pp][:, 1:2], AF.Exp, [ktv], [(kslot, 'v3')])
                        yield
                        e.tt('pool', Y0[:, 0:128], Y0[:, 0:128], Sm, ALU.mult, [kYT[0], 'cst'], [kYT[0]])
                        e.ts('pool', vc[:, 1:2], vc[:, 0:1], -1.0, None, ALU.mult, None, [(kslot, 'v0')], [(kslot, 'v1')])
                        yield
                        e.tr(pJ, Y0[:, 0:128], ident, [kYT[0], 'cst'], [kb])
                        e.tt('pool', Y0[:, 128:256], Y0[:, 0:128], ident, ALU.add, [kYT[0], 'cst'], [kYT[0]])
                        yield
                        e.cp('act', Yt[pp][0][:], pJ, [], [kYt[0], kb])
                        yield
                        e.mm(bnk[:, bP, 0:128], Yt[pp][0][:], Y0[:, 0:128], True, True, [kYt[0], kYT[0]], [kb])
                        e.mm(pJ, Y0[:, 0:128], Yt[pp][0][:], True, True, [kYt[0], kYT[0]], [kb])
                        yield
                        e.cp('act', YT[pp][1][:, 0:128], bnk[:, bP, 0:128], [], [kYT[1], kb])
                        e.cp('dve', Yt[pp][1][:], pJ, [], [kYt[1], kb])
                        e.cp('pool', YT[pp][1][:, 128:256], Y0[:, 128:256], [kYT[0]], [kYT[1]])
                        yield
                        for q in range(1, 7):
                            a = q % 2
                            nb_ = 1 - a
                            cur, curt = YT[pp][a], Yt[pp][a]
                            nx, nxt_ = YT[pp][nb_], Yt[pp][nb_]
                            if q < 6:
                                e.mm(pI, curt[:], cur[:, 0:256], True, True, [kYt[a], kYT[a]], [kb])
                                e.mm(pJ, cur[:, 0:128], curt[:], True, True, [kYt[a], kYT[a]], [kb])
                                yield
                                e.cp('act', nx[:, 0:128], bnk[:, bP, 0:128], [], [kYT[nb_], kb])
                                e.tt('dve', nx[:, 128:256], bnk[:, bP, 128:256], cur[:, 128:256], ALU.add, [kYT[a]],
                                     [kYT[nb_], kb])
                                e.cp('act' if q % 2 else 'dve', nxt_[:], pJ, [], [kYt[nb_], kb])
                                yield
                            else:
                                e.mm(bnk[:, bP, 0:128], curt[:], cur[:, 128:256], True, True, [kYt[a], kYT[a]], [kb])
                                yield
                                e.tt('dve', T2t[c][slot][:], bnk[:, bP, 0:128], cur[:, 128:256], ALU.add, [kYT[a]],
                                     [(kslot, 'T'), kb])
                                yield

                    def seq_stage(stage, c, ti, slot):
                        d, h = c
                        col = 2 * d + h
                        hr0 = 64 * h
                        tc0 = ti * 128
                        ci = chains.index(c)
                        bC = 4 + ci
                        kslot = ('gd_slot', c, slot)
                        vc = vec[c][slot]
                        kS = ('gd_S', c)
                        if stage == 0:
                            e.mm(bnk[:, bC, 0:64], kT[hr0:hr0 + 64, tc0:tc0 + 128], Sst[c][hr0:hr0 + 64, :], True, True,
                                 [('gd_c', 1), kS], [self.bk(bC)])
                            e.mm(bnk[:, bC, 64:128], qT[hr0:hr0 + 64, tc0:tc0 + 128], Sst[c][hr0:hr0 + 64, :], True, True,
                                 [('gd_c', 0), kS], [self.bk(bC)])
                            e.ts('pool', kh[c][:], kv[:, ti, hr0:hr0 + 64], vc[:, 2:3], None, ALU.mult, None,
                                 ['gd_kv', (kslot, 'v2')], [('gd_kh', c)])
                        elif stage == 1:
                            e.stt(Rp[c][:], bnk[:, bC, 0:64], vc[:, 1:2], kv[:, ti, 128 + hr0:128 + hr0 + 64], ALU.mult, ALU.add,
                                  [(kslot, 'v1'), 'gd_kv'], [('gd_Rp', c), self.bk(bC)])
                            e.act(qs[c][:], bnk[:, bC, 64:128], AF.Identity, [(kslot, 'v0')], [('gd_qs', c), self.bk(bC)],
                                  scale=vc[:, 0:1])
                        elif stage == 2:
                            e.mm(bnk[:, bC, 128:192], T2t[c][slot][:], Rp[c][:], True, True, [(kslot, 'T'), ('gd_Rp', c)],
                                 [self.bk(bC)])
                        elif stage == 3:
                            e.act(vnw[c][:], bnk[:, bC, 128:192], AF.Identity, ['gd_beta'], [('gd_vn', c), self.bk(bC)],
                                  scale=beta[:, ti, col:col + 1])
                        elif stage == 4:
                            e.mm(bnk[:, bC, 192:256], QKt[c][slot][:], vnw[c][:], True, True, [(kslot, 'QK'), ('gd_vn', c)],
                                 [self.bk(bC)])
                            e.mm(bnk[hr0:hr0 + 64, bC, 256:320], kh[c][:], vnw[c][:], True, True,
                                 [('gd_kh', c), ('gd_vn', c)], [self.bk(bC)])
                        else:
                            e.tt('dve', o_tm[d][:, ti, hr0:hr0 + 64], bnk[:, bC, 192:256], qs[c][:], ALU.add,
                                 [('gd_qs', c)], [('gd_o', d), self.bk(bC)])
                            e.stt(Sst[c][hr0:hr0 + 64, :], Sst[c][hr0:hr0 + 64, :], vc[hr0:hr0 + 64, 3:4],
                                  bnk[hr0:hr0 + 64, bC, 256:320], ALU.mult, ALU.add, [kS, (kslot, 'v3')], [kS, self.bk(bC)])

                    def run_wave(preps, seqs, ratio=4):
                        active = list(preps)
                        sq = list(seqs)
                        k = 0
                        while active or sq:
                            nxt_active = []
                            for gnr in active:
                                try:
                                    next(gnr)
                                    nxt_active.append(gnr)
                                except StopIteration:
                                    pass
                            active = nxt_active
                            k += 1
                            if sq and (k % ratio == 0 or not active):
                                nsq = []
                                for gnr in sq:
                                    try:
                                        next(gnr)
                                        nsq.append(gnr)
                                    except StopIteration:
                                        pass
                                sq = nsq

                    def seq_gen(c, ti, slot):
                        for stage in range(6):
                            seq_stage(stage, c, ti, slot)
                            yield

                    run_wave([prep(c, orders[c[0]][0], 0) for c in chains], [])
                    for si in range(NT):
                        preps = []
                        if si + 1 < NT:
                            preps = [prep(c, orders[c[0]][si + 1], (si + 1) % R) for c in chains]
                        seqs = [seq_gen(c, orders[c[0]][si], si % R) for c in chains]
                        run_wave(preps, seqs)
                    if self.debug:
                        self.dump("gd_o0_%d_%d" % (l, g), o_tm[0][:], [128, NT, 128], [('gd_o', 0)])
                        self.dump("gd_o1_%d_%d" % (l, g), o_tm[1][:], [128, NT, 128], [('gd_o', 1)])
                    nw = sbt("gd_nw", [128, 128])
                    ss = sbt("gd_ss", [128, NT * 2])
                    yb = [sbt("gd_yb%d" % i, [128, 512], BF16) for i in range(2)]
                    B.dma('sp', nw[:], self.gnw[l], writes=['gd_nw'])
                    O = o_tm[0]
                    O1 = o_tm[1]
                    Of = O[:].rearrange("p t c -> p (t c)")
                    O1f = O1[:].rearrange("p t c -> p (t c)")
                    e.tt('pool', Of, Of, O1f, ALU.add, [('gd_o', 0), ('gd_o', 1)], [('gd_o', 0)])
                    e.tt('dve', O1f, Of, Of, ALU.mult, [('gd_o', 0)], [('gd_o', 1)])
                    e.red(ss[:], O1[:].rearrange("p t (h c) -> p (t h) c", c=64), ALU.add, [('gd_o', 1)], ['gd_ss'])
                    e.ts('dve', ss[:], ss[:], 1.0 / 64.0, NORM_EPS, ALU.mult, ALU.add, ['gd_ss'], ['gd_ss'])
                    e.act(ss[:], ss[:], AF.Sqrt, ['gd_ss'], ['gd_ss'])
                    e.recip(ss[:], ss[:], ['gd_ss'], ['gd_ss'])
                    O3 = O[:].rearrange("p t (h c) -> p (t h) c", c=64)
                    e.tt('dve', O3, O3, ss[:].unsqueeze(2).to_broadcast([128, NT * 2, 64]), ALU.mult,
                         [('gd_o', 0), 'gd_ss'], [('gd_o', 0)])
                    e.tt('pool', O[:], O[:], nw[:].unsqueeze(1).to_broadcast([128, NT, 128]), ALU.mult,
                         [('gd_o', 0), 'gd_nw'], [('gd_o', 0)])
                    e.tt('dve', O[:], O[:], zg[:], ALU.mult, [('gd_o', 0), 'gd_zg'], [('gd_o', 0)])
                    groups = [[0, 1]] + [list(range(2 + 4 * i, 6 + 4 * i)) for i in range(8)]
                    for gi, tl in enumerate(groups):
                        pb = gi % 2
                        bO = gi % 4
                        for k, ti in enumerate(tl):
                            e.tr(bnk[:, bO, k * 128:(k + 1) * 128], O[:, ti, :], ident, [('gd_o', 0), 'cst'], [self.bk(bO)])
                        n = 128 * len(tl)
                        e.cp('act' if gi % 2 else 'dve', yb[pb][:, 0:n], bnk[:, bO, 0:n], [], [('gd_yb', pb), self.bk(bO)])
                        B.dma('pool', yc[row0:row0 + 128, tl[0] * 128:tl[0] * 128 + n], yb[pb][:, 0:n],
                              reads=[('gd_yb', pb)], writes=['YC'])
                    B.barrier()

    def back_phase(self, l, x_cur, yc, x_next, last):
        B, e, nc = self.B, self.e, self.nc
        bnk = self.banks
        with ExitStack() as ph:
            def sbt(name, shape, dt=F32):
                return ph.enter_context(_sbt(nc, name, shape, dt))
            wo = sbt("bk_wo", [128, KC, D], BF16)
            stg = [sbt("bk_stg%d" % i, [128, D]) for i in range(2)]
            ycb = [sbt("bk_yc%d" % i, [128, KC, 128], BF16) for i in range(2)]
            xt = [sbt("bk_xt%d" % i, [128, D]) for i in range(2)]
            tb = [sbt("bk_tb%d" % i, [128, D]) for i in range(2)]
            zb = [sbt("bk_zb%d" % i, [128, D]) for i in range(2)]
            lng = sbt("bk_lng", [128, D])
            lnb = sbt("bk_lnb", [128, D])
            st6 = [sbt("bk_st6%d" % i, [128, 2, 6]) for i in range(2)]
            mv = [sbt("bk_mv%d" % i, [128, 4]) for i in range(2)]
            B.dma('sp', lng[:], self.lng_rep[l], writes=['bk_lng'])
            B.dma('sp', lnb[:], self.lnb_rep[l], writes=['bk_lnb'])
            for kc in range(KC):
                b = kc % 2
                B.dma('sp', stg[b][:], self.w_out[l, kc * 128:(kc + 1) * 128, :], writes=[('bk_stg', b)])
                e.cp('pool', wo[:, kc, :], stg[b][:], [('bk_stg', b)], ['bk_wo'])
            ycv = yc.rearrange("(c p) t -> p c t", p=128)
            tiles = list(range(2, NT)) if last else list(range(NT))
            for n_, ti in enumerate(tiles):
                j = 1 if ti < 2 else 0
                b = n_ % 2
                b0 = 2 * b
                B.dma('sp', ycb[b][:], ycv[:, :, ti * 128:(ti + 1) * 128], reads=['YC'], writes=[('bk_yc', b)])
                B.dma('sp', xt[b][:], x_cur[ti * 128:(ti + 1) * 128, :], reads=['X'], writes=[('bk_xt', b)])
                for nb in range(2):
                    for c in range(KC):
                        e.mm(bnk[:, b0 + nb, :], ycb[b][:, c, :], wo[:, c, nb * 512:(nb + 1) * 512], c == 0, c == KC - 1,
                             [('bk_yc', b), 'bk_wo'], [self.bk(b0 + nb)])
                for nb in range(2):
                    e.tt('dve', tb[b][:, nb * 512:(nb + 1) * 512], bnk[:, b0 + nb, :],
                         self.gate_bc[:, j, nb * 512:(nb + 1) * 512], ALU.mult, ['gate_bc'],
                         [('bk_tb', b), self.bk(b0 + nb)])
                e.stt(zb[b][:], xt[b][:], ALPHA, tb[b][:], ALU.mult, ALU.add, [('bk_xt', b), ('bk_tb', b)], [('bk_zb', b)])
                for nb in range(2):
                    B.op('dve', (lambda o_, i_: (lambda en: en.bn_stats(out=o_, in_=i_)))(
                        st6[b][:, nb, :], zb[b][:, nb * 512:(nb + 1) * 512]), [('bk_zb', b)], [('bk_st6', b)])
                B.op('dve', (lambda o_, i_: (lambda en: en.bn_aggr(out=o_, in_=i_)))(
                    mv[b][:, 0:2], st6[b][:].rearrange("p a s -> p (a s)")), [('bk_st6', b)], [('bk_mv', b)])
                e.act(mv[b][:, 2:3], mv[b][:, 1:2], AF.Sqrt, [('bk_mv', b)], [('bk_sd', b)], bias=LN_EPS)
                e.recip(mv[b][:, 3:4], mv[b][:, 2:3], [('bk_sd', b)], [('bk_rs', b)])
                e.ts('dve', zb[b][:], zb[b][:], mv[b][:, 0:1], mv[b][:, 3:4], ALU.subtract, ALU.mult,
                     [('bk_zb', b), ('bk_mv', b), ('bk_rs', b)], [('bk_zb', b)])
                e.tt('pool', zb[b][:], zb[b][:], lng[:], ALU.mult, [('bk_zb', b), 'bk_lng'], [('bk_zb', b)])
                e.tt('pool', zb[b][:], zb[b][:], lnb[:], ALU.add, [('bk_zb', b), 'bk_lnb'], [('bk_zb', b)])
                if last:
                    dst = x_next[(ti - 2) * 128:(ti - 1) * 128, :]
                else:
                    dst = x_next[ti * 128:(ti + 1) * 128, :]
                B.dma('pool', dst, zb[b][:], reads=[('bk_zb', b)], writes=['Xn'])
            B.barrier()
            B.last_w['X'] = B.last_w.get('Xn')


def group_cols(g):
    R0 = 1152
    rgx = np.arange(192 * g, 192 * g + 192)
    gq = 384 + np.arange(128 * g, 128 * g + 128)
    gk = 640 + np.arange(128 * g, 128 * g + 128)
    gv = 896 + np.arange(128 * g, 128 * g + 128)
    rgg = R0 + np.arange(192 * g, 192 * g + 192)
    naq = R0 + 384 + np.arange(192 * g, 192 * g + 192)
    nak = R0 + 768 + np.arange(192 * g, 192 * g + 192)
    nav = R0 + 1152 + np.arange(192 * g, 192 * g + 192)
    nag = R0 + 1536 + np.arange(192 * g, 192 * g + 192)
    gg = R0 + 1920 + np.arange(128 * g, 128 * g + 128)
    bf = R0 + 2176 + np.arange(2 * g, 2 * g + 2)
    bb = R0 + 2180 + np.arange(2 * g, 2 * g + 2)
    af = R0 + 2184 + np.arange(2 * g, 2 * g + 2)
    ab = R0 + 2188 + np.arange(2 * g, 2 * g + 2)
    cols = np.concatenate([rgx, rgg, naq, nak, nag, nav, gq, gk, gv, gg, bf, bb, af, ab])
    assert cols.shape[0] == GCOLS
    conv_ch = [rgx[:128], rgx[128:], gq, gk, gv]
    return cols, conv_ch


def make_consts():
    c = np.zeros((128, NCONST), np.float32)
    i = np.arange(128)
    c[:, C_ID:C_ID + 128] = np.eye(128)
    c[:, C_ONE:C_ONE + 128] = 1.0
    c[:64, C_BONE:C_BONE + 64] = 1.0
    c[64:, C_BONE + 64:C_BONE + 128] = 1.0
    R = np.zeros((128, 128), np.float32)
    for h in range(2):
        for half in range(2):
            o = 64 * h + 32 * half
            for t in range(16):
                R[o + t, o + t + 16] = -1.0
                R[o + t + 16, o + t] = 1.0
    c[:, C_RM:C_RM + 128] = R.T
    k = i[:, None]
    m = i[None, :]
    c[:, C_UF:C_UF + 128] = (k <= m)
    c[:, C_UB:C_UB + 128] = (k >= m)
    c[:, C_SF:C_SF + 128] = (m > k)
    c[:, C_SB:C_SB + 128] = (m < k)
    c[:, C_MF:C_MF + 128] = np.where(m >= k, 0.0, -BIG)
    c[:, C_MB:C_MB + 128] = np.where(m <= k, 0.0, -BIG)
    return c


def make_rope():
    p = np.arange(128)
    d = p % 64
    half = d // 32
    fi = d % 16
    inv_freq = (np.float32(10000.0) ** (-np.arange(16, dtype=np.float32) / np.float32(16))).astype(np.float32)
    t = np.arange(4096)
    row = (t // 64).astype(np.float32)
    col = (t % 64).astype(np.float32)
    pos = np.where(half[:, None] == 0, row[None, :], col[None, :]).astype(np.float32)
    ang = (pos * inv_freq[fi][:, None]).astype(np.float32)
    return np.cos(ang).astype(np.float32), np.sin(ang).astype(np.float32)


NA_CLASSES = [(0, 0), (1, 0), (2, 0), (30, 27), (31, 27)]


def make_nab(rpb_h):
    out = np.full((128, 5, 640), -BIG, np.float32)
    q = np.arange(128)
    key = np.arange(640)
    for ci, (qt, kt0) in enumerate(NA_CLASSES):
        r = 2 * qt + q // 64
        j = q % 64
        kr = 2 * kt0 + key // 64
        kcn = key % 64
        r0 = np.clip(r - 4, 0, 56)
        c0 = np.clip(j - 8, 0, 48)
        okr = (kr[None, :] >= r0[:, None]) & (kr[None, :] < r0[:, None] + 8)
        okc = (kcn[None, :] >= c0[:, None]) & (kcn[None, :] < c0[:, None] + 16)
        ro = np.clip(kr[None, :] - r[:, None] + 7, 0, 14)
        co = np.clip(kcn[None, :] - j[:, None] + 15, 0, 30)
        vals = rpb_h[ro, co]
        out[:, ci, :] = np.where(okr & okc, vals, np.float32(-BIG))
    return out


def prep_shared(inp, G):
    L = DEPTH
    sh = {}
    sh['consts'] = make_consts()
    cs, sn = make_rope()
    sh['rope_cos'], sh['rope_sin'] = cs, sn
    sh['w_mod'] = np.ascontiguousarray(inp['w_mod'])
    sh['bmod_rep'] = np.ascontiguousarray(np.broadcast_to(inp['b_mod'][:, None, :], (L, 128, 3 * D)))
    sh['gnw'] = np.ascontiguousarray(np.broadcast_to(np.tile(inp['gdn_nw'], (1, 2))[:, None, :], (L, 128, 128)))
    sh['lng_rep'] = np.ascontiguousarray(np.broadcast_to(inp['ln_g'][:, None, :], (L, 128, D)))
    sh['lnb_rep'] = np.ascontiguousarray(np.broadcast_to(inp['ln_b'][:, None, :], (L, 128, D)))
    rows = []
    for g in range(2):
        rows += list(range(192 * g, 192 * g + 192))
        rows += list(range(384 + 192 * g, 384 + 192 * g + 192))
        rows += list(range(768 + 128 * g, 768 + 128 * g + 128))
    sh['w_out'] = np.ascontiguousarray(inp['w_out'][:, np.array(rows), :])
    return sh


def prep_group(inp, g):
    L = DEPTH
    cols, conv_ch = group_cols(g)
    o = {}
    o['w_in'] = inp['w_in'][:, :, cols]
    cw = np.zeros((L, 128, 5, 4), np.float32)
    for ti, ch in enumerate(conv_ch):
        cw[:, :len(ch), ti, :] = np.transpose(inp['conv_w'][:, :, ch], (0, 2, 1))
    o['convw'] = cw
    rgw = np.zeros((L, 128, 2, 2, 192), np.float32)
    rgb = np.zeros((L, 128, 2, 2, 3), np.float32)
    for d in range(2):
        for ai, (wn, bn) in enumerate([('rg_wa', 'rg_ba'), ('rg_wx', 'rg_bx')]):
            W = inp[wn][:, d]
            rgw[:, 0:64, d, ai, 0:64] = W[:, 3 * g]
            rgw[:, 64:128, d, ai, 64:128] = W[:, 3 * g + 1]
            rgw[:, 0:64, d, ai, 128:192] = W[:, 3 * g + 2]
            bvec = inp[bn][:, d, 192 * g:192 * g + 192]
            rgb[:, :, 0, d, ai] = bvec[:, 0:128]
            rgb[:, 0:64, 1, d, ai] = bvec[:, 128:192]
        lam = inp['rg_lam'][:, d, 192 * g:192 * g + 192]
        rgb[:, :, 0, d, 2] = lam[:, 0:128]
        rgb[:, 0:64, 1, d, 2] = lam[:, 128:192]
    o['rgw'] = rgw
    o['rgb'] = rgb
    nab = np.zeros((L, 3, 128, 5, 640), np.float32)
    for l in range(L):
        for h in range(3):
            nab[l, h] = make_nab(inp['na_rpb'][l, 3 * g + h])
    o['nab'] = nab
    al = np.zeros((L, 128, NT, 4), np.float32)
    dt = np.zeros((L, 128, NT, 4), np.float32)
    for d in range(2):
        for h in range(2):
            al[:, :, :, 2 * d + h] = inp['gdn_alog'][:, d, 2 * g + h][:, None, None]
            dt[:, :, :, 2 * d + h] = inp['gdn_dtb'][:, d, 2 * g + h][:, None, None]
    o['galog'] = al
    o['gdtb'] = dt
    return o


def core_inputs(inp, sh, groups, b, x_rows):
    m = dict(sh)
    gs = [prep_group(inp, g) for g in groups]
    m['w_in'] = np.ascontiguousarray(np.concatenate([q['w_in'] for q in gs], axis=2))
    for k in ['convw', 'rgw', 'rgb', 'nab', 'galog', 'gdtb']:
        m[k] = np.ascontiguousarray(np.stack([q[k] for q in gs], axis=1))
    m['x_in'] = np.ascontiguousarray(x_rows)
    cc = np.stack([inp['c'][b], inp['c_ctx']], axis=-1)
    m['cc'] = np.ascontiguousarray(cc.reshape(KC, 128, 2).transpose(1, 0, 2))
    return m


MODE = 'B'
_PROG_CACHE = {}


def _prog(key, **kw):
    if key not in _PROG_CACHE:
        _PROG_CACHE[key] = Prog(**kw)
    return _PROG_CACHE[key]


def kernel_unfused(inp):
    nb = inp['x'].shape[0]
    sh = prep_shared(inp, 1)
    ncore = 2 * nb
    x_rows = [np.concatenate([inp['ctx'][b], inp['x'][b]], 0) for b in range(nb)]
    base = [core_inputs(inp, sh, [c % 2], c // 2, x_rows[c // 2]) for c in range(ncore)]
    yc_full = None
    out = None
    for k in range(DEPTH + 1):
        if k == 0:
            P = _prog(('A', 0), G=1, steps=[('front', 0)], x_ext_out=True, final=False)
        elif k < DEPTH:
            P = _prog(('A', k), G=1, steps=[('back', k - 1), ('front', k)], x_ext_out=True, final=False)
        else:
            P = _prog(('A', k), G=1, steps=[('back', DEPTH - 1)], x_ext_out=True, final=True)
        maps = []
        for c in range(ncore):
            m = dict(base[c])
            m['x_in'] = x_rows[c // 2]
            if k > 0:
                m['yc_in'] = yc_full[c // 2]
            maps.append(m)
        res = run_bass_kernel_spmd(P.nc, maps, core_ids=list(range(ncore)))
        rs = res.results
        if k < DEPTH:
            yc_full = [np.ascontiguousarray(np.concatenate([np.asarray(rs[2 * b]['yc_out']),
                                                            np.asarray(rs[2 * b + 1]['yc_out'])], 0))
                       for b in range(nb)]
        if 0 < k < DEPTH:
            x_rows = [np.asarray(rs[2 * b]['x_out']) for b in range(nb)]
        if k == DEPTH:
            out = np.stack([np.asarray(rs[2 * b]['out']) for b in range(nb)], 0)
    return out.astype(np.float32)


def kernel_fused(inp):
    nb = inp['x'].shape[0]
    sh = prep_shared(inp, 2)
    steps = []
    for l in range(DEPTH):
        steps += [('front', l), ('back', l)]
    P = _prog(('B',), G=2, steps=steps, x_ext_out=False, final=True)
    maps = []
    for c in range(8):
        b = c % nb
        x_rows = np.concatenate([inp['ctx'][b], inp['x'][b]], 0)
        maps.append(core_inputs(inp, sh, [0, 1], b, x_rows))
    res = run_bass_kernel_spmd(P.nc, maps, core_ids=list(range(8)))
    out = np.stack([np.asarray(res.results[b]['out']) for b in range(nb)], 0)
    return out.astype(np.float32)


def kernel(**inputs):
    inp = {k: np.asarray(v) for k, v in inputs.items()}
    if MODE == 'A':
        return kernel_unfused(inp)
    return kernel_fused(inp)
```

```python
from contextlib import ExitStack
import numpy as np
import concourse.bass as bass
import concourse.mybir as mybir
from concourse.bass_utils import run_bass_kernel_spmd

F32 = mybir.dt.float32
BF16 = mybir.dt.bfloat16
AF = mybir.ActivationFunctionType
ALU = mybir.AluOpType
AX = mybir.AxisListType

ENGS = ['pe', 'act', 'dve', 'pool', 'sp']
N_DMA_SEMS = 24


_UNIQ = [0]


def _sbt(nc, name, shape, dt):
    _UNIQ[0] += 1
    return nc.sbuf_tensor("%s_u%d" % (name, _UNIQ[0]), list(shape), dt)


class Builder:
    def __init__(self, nc):
        self.nc = nc
        self.stack = ExitStack()
        self.ops = {e: [] for e in ENGS}
        self.sem = {}
        self.cnt = {}
        self.waited = {e: {} for e in ENGS}
        self.last_w = {}
        self.readers = {}
        for e in ENGS:
            self.sem[e] = self.stack.enter_context(nc.semaphore('sem_' + e))
            self.cnt[e] = 0
        for k in range(N_DMA_SEMS):
            key = ('dma', k)
            self.sem[key] = self.stack.enter_context(nc.semaphore('sem_dma%d' % k))
            self.cnt[key] = 0
        self.dma_rr = 0
        self.n_ops = 0
        self.pending = {e: [] for e in ENGS}

    def sb(self, name, shape, dtype):
        return self.stack.enter_context(_sbt(self.nc, name, list(shape), dtype))

    def ps(self, name, shape, dtype):
        return self.stack.enter_context(self.nc.psum_tensor(name, list(shape), dtype))

    def _deps(self, eng, reads, writes):
        toks = []
        for r in reads:
            t = self.last_w.get(r)
            if t is not None:
                toks.append((t, False))
        for w in writes:
            t = self.last_w.get(w)
            if t is not None:
                toks.append((t, False))
            for sk, (val, peng) in self.readers.get(w, {}).items():
                toks.append(((sk, val, peng), True))
        waits = []
        for (sk, val, peng), is_war in toks:
            if peng == eng:
                if eng == 'pe' or is_war:
                    continue
            if self.waited[eng].get(sk, 0) >= val:
                continue
            self.waited[eng][sk] = val
            waits.append((sk, val))
        best = {}
        for sk, val in waits:
            best[sk] = max(best.get(sk, 0), val)
        return list(best.items())

    def _commit(self, tok, reads, writes):
        for w in writes:
            self.last_w[w] = tok
            self.readers[w] = {}
        for r in reads:
            d = self.readers.setdefault(r, {})
            sk, val, peng = tok
            if d.get(sk, (0, None))[0] < val:
                d[sk] = (val, peng)

    def barrier(self):
        for e in ENGS:
            for k, v in self.cnt.items():
                if v > 0 and self.waited[e].get(k, 0) < v:
                    self.waited[e][k] = v
                    self.pending[e].append((k, v))

    def _take_pending(self, eng, waits):
        if self.pending[eng]:
            best = dict(waits)
            for k, v in self.pending[eng]:
                best[k] = max(best.get(k, 0), v)
            self.pending[eng] = []
            return list(best.items())
        return waits

    def op(self, eng, fn, reads=(), writes=()):
        waits = self._take_pending(eng, self._deps(eng, reads, writes))
        self.cnt[eng] += 1
        tok = (eng, self.cnt[eng], eng)
        self.ops[eng].append((waits, fn, eng, 1))
        self._commit(tok, reads, writes)
        self.n_ops += 1

    def dma(self, q, out, in_, reads=(), writes=(), fn=None, **kw):
        k = self.dma_rr
        self.dma_rr = (self.dma_rr + 1) % N_DMA_SEMS
        key = ('dma', k)
        waits = self._deps(q, reads, writes)
        if self.cnt[key] > 0 and self.waited[q].get(key, 0) < self.cnt[key]:
            self.waited[q][key] = self.cnt[key]
            waits = [w for w in waits if w[0] != key] + [(key, self.cnt[key])]
        waits = self._take_pending(q, waits)
        self.cnt[key] += 16
        tok = (key, self.cnt[key], 'dma')
        if fn is None:
            fn = lambda e: e.dma_start(out=out, in_=in_, **kw)
        self.ops[q].append((waits, fn, key, 16))
        self._commit(tok, reads, writes)
        self.n_ops += 1

    def finish(self):
        nc = self.nc
        fin = [(k, v) for k, v in self.cnt.items() if v > 0]
        ops = self.ops
        sem = self.sem

        def replay(name):
            def run(e):
                for waits, fn, sk, inc in ops[name]:
                    for wk, wv in waits:
                        e.wait_ge(sem[wk], wv)
                    ins = fn(e)
                    ins.then_inc(sem[sk], inc)
                if name == 'sp':
                    for k, v in fin:
                        e.wait_ge(sem[k], v)
            return run

        with nc.Block() as block:
            block.tensor(replay('pe'))
            block.scalar(replay('act'))
            block.vector(replay('dve'))
            block.gpsimd(replay('pool'))
            block.sync(replay('sp'))
        self.stack.close()


DEPTH = 4
D = 1024
KC = 8
T = 4352
NT = 34
GCOLS = 1672
ALPHA = (2.0 * DEPTH) ** 0.25
LN_EPS = 1e-5
NORM_EPS = 1e-6
BIG = 30000.0
BLOCKS = [(0, 256)] + [(256 + 512 * i, 512) for i in range(8)]
PADW = 4360
C_ID, C_ONE, C_BONE, C_RM, C_UF, C_UB, C_SF, C_SB, C_MF, C_MB = [128 * i for i in range(10)]
NCONST = 1280
ONLY = {'rg', 'na', 'gdn'}


def pcol(t):
    return t + 2 if t < 256 else t + 6


class E:
    def __init__(self, B):
        self.B = B

    def mm(self, out, lhsT, rhs, st, sp, r, w):
        self.B.op('pe', lambda e: e.matmul(out, lhsT=lhsT, rhs=rhs, start=st, stop=sp), r, w)

    def tr(self, out, in_, ident, r, w):
        self.B.op('pe', lambda e: e.transpose(out=out, in_=in_, identity=ident), r, w)

    def act(self, out, in_, func, r, w, bias=None, scale=None, accum=None):
        kw = {}
        if bias is not None:
            kw['bias'] = bias
        if scale is not None:
            kw['scale'] = scale
        if accum is not None:
            kw['accum_out'] = accum
        self.B.op('act', lambda e: e.activation(out=out, in_=in_, func=func, **kw), r, w)

    def tt(self, eng, out, in0, in1, op, r, w):
        self.B.op(eng, lambda e: e.tensor_tensor(out=out, in0=in0, in1=in1, op=op), r, w)

    def ts(self, eng, out, in0, s1, s2, op0, op1, r, w):
        if op1 is None:
            self.B.op(eng, lambda e: e.tensor_scalar(out=out, in0=in0, scalar1=s1, scalar2=None, op0=op0), r, w)
        else:
            self.B.op(eng, lambda e: e.tensor_scalar(out=out, in0=in0, scalar1=s1, scalar2=s2, op0=op0, op1=op1), r, w)

    def stt(self, out, in0, sc, in1, op0, op1, r, w):
        self.B.op('dve', lambda e: e.scalar_tensor_tensor(out=out, in0=in0, scalar=sc, in1=in1, op0=op0, op1=op1), r, w)

    def cp(self, eng, out, in_, r, w):
        if eng == 'act':
            self.B.op('act', lambda e: e.activation(out=out, in_=in_, func=AF.Copy), r, w)
        else:
            self.B.op(eng, lambda e: e.tensor_copy(out=out, in_=in_), r, w)

    def red(self, out, in_, op, r, w, negate=False):
        self.B.op('dve', lambda e: e.tensor_reduce(out=out, in_=in_, axis=AX.X, op=op, negate=negate), r, w)

    def recip(self, out, in_, r, w):
        self.B.op('dve', lambda e: e.reciprocal(out=out, in_=in_), r, w)

    def memset(self, eng, ap, val, w):
        self.B.op(eng, lambda e: e.memset(ap, val), (), w)

    def scan(self, out, d0, d1, init, r, w):
        self.B.op('dve', lambda e: e.tensor_tensor_scan(out=out, data0=d0, data1=d1, initial=init,
                                                        op0=ALU.mult, op1=ALU.add), r, w)


def rev(ap):
    return ap[:, ::-1]


class Prog:
    def __init__(self, G, steps, x_ext_out, final, debug=False):
        self.G = G
        self.steps = steps
        self.final = final
        nc = self.nc = bass.Bass("TRN2", target_bir_lowering=False)
        self.B = B = Builder(nc)
        self.e = E(B)
        self.debug = debug
        self.dbg_outs = {}
        L = DEPTH

        def din(name, shape, dt=F32):
            return nc.dram_tensor(name, list(shape), dt, kind="ExternalInput").ap()

        def dout(name, shape, dt=F32):
            return nc.dram_tensor(name, list(shape), dt, kind="ExternalOutput").ap()

        def dint(name, shape, dt=F32):
            return nc.dram_tensor(name, list(shape), dt, kind="Internal").ap()

        self.dout = dout
        self.x_in = din("x_in", [T, D])
        self.cc = din("cc", [128, KC, 2])
        self.consts_d = din("consts", [128, NCONST])
        self.w_mod = din("w_mod", [L, D, 3 * D])
        self.bmod_rep = din("bmod_rep", [L, 128, 3 * D])
        self.w_in = din("w_in", [L, D, G * GCOLS])
        self.convw = din("convw", [L, G, 128, 5, 4])
        self.rgw = din("rgw", [L, G, 128, 2, 2, 192])
        self.rgb = din("rgb", [L, G, 128, 2, 2, 3])
        self.nab = din("nab", [L, G, 3, 128, 5, 640])
        self.galog = din("galog", [L, G, 128, NT, 4])
        self.gdtb = din("gdtb", [L, G, 128, NT, 4])
        self.gnw = din("gnw", [L, 128, 128])
        self.rope_cos = din("rope_cos", [128, 4096])
        self.rope_sin = din("rope_sin", [128, 4096])
        self.w_out = din("w_out", [L, D, D])
        self.lng_rep = din("lng_rep", [L, 128, D])
        self.lnb_rep = din("lnb_rep", [L, 128, D])
        has_front = any(s[0] == 'front' for s in steps)
        has_back = any(s[0] == 'back' for s in steps)
        self.uT_d = dint("uT_d", [128, KC, T], BF16)
        if steps[0][0] == 'back':
            self.yc_in = din("yc_in", [D, T], BF16)
        else:
            self.yc_in = None
        if has_front and x_ext_out:
            self.yc_out = dout("yc_out", [512 * G, T], BF16)
        else:
            self.yc_out = None
        self.yc_int = dint("yc_int", [D, T], BF16) if not x_ext_out else None
        self.x_ext_out = x_ext_out
        if x_ext_out and has_back and not final:
            self.x_out = dout("x_out", [T, D])
        else:
            self.x_out = None
        self.x_scr = [dint("x_scr0", [T, D]), dint("x_scr1", [T, D])] if not x_ext_out else None
        self.out_d = dout("out", [4096, D]) if final else None

        self.banks = B.ps("banks", [128, 8, 512], F32)
        self.cst = B.sb("cst", [128, NCONST], F32)
        self.identb = B.sb("identb", [128, 128], BF16)
        self.screp = B.sb("screp", [128, 2, KC, 128], F32)
        self.modcol = B.sb("modcol", [128, 16, 2], F32)
        self.gate_bc = B.sb("gate_bc", [128, 2, D], F32)
        self.build()

    def c(self, off, n=128, p0=0, p1=128):
        return self.cst[p0:p1, off:off + n]

    def bk(self, i):
        return ('bk', i)

    def dump(self, name, ap, shape, reads, dt=F32):
        if not self.debug:
            return
        d = self.nc.dram_tensor("dbg_" + name, list(shape), dt, kind="ExternalOutput").ap()
        self.dbg_outs[name] = d
        self.B.dma('pool', d, ap, reads=reads, writes=[('dbg', name)])

    def build(self):
        B, e = self.B, self.e
        B.dma('sp', self.cst[:], self.consts_d, writes=['cst'])
        e.cp('dve', self.identb[:], self.c(C_ID), ['cst'], ['identb'])
        with ExitStack() as ph:
            cct = ph.enter_context(_sbt(self.nc, "cct", [128, KC, 2], F32))
            sct = ph.enter_context(_sbt(self.nc, "sct", [128, KC, 2], F32))
            B.dma('sp', cct[:], self.cc, writes=['cct'])
            e.act(sct[:], cct[:], AF.Silu, ['cct'], ['sct'])
            for j in range(2):
                for kc in range(KC):
                    e.ts('dve', self.screp[:, j, kc, :], self.c(C_ONE), sct[:, kc, j:j + 1], None, ALU.mult, None,
                         ['cst', 'sct'], ['screp'])
            B.barrier()
        x_cur = self.x_in
        nxt = 0
        for kind, l in self.steps:
            last = (l == DEPTH - 1)
            if kind == 'front':
                yc = self.yc_out if self.x_ext_out else self.yc_int
                self.mod_phase(l)
                self.u_phase(l, x_cur)
                for g in range(self.G):
                    row0 = 512 * g
                    if 'rg' in ONLY:
                        self.rg_pass(l, g, yc, row0)
                    if 'na' in ONLY:
                        self.na_pass(l, g, yc, row0 + 192, ctx_out=not last)
                    if 'gdn' in ONLY:
                        self.gdn_pass(l, g, yc, row0 + 384)
            else:
                if self.yc_in is not None and (kind, l) == self.steps[0]:
                    yc = self.yc_in
                    self.mod_phase(l, gate_only=True)
                else:
                    yc = self.yc_int
                if last:
                    self.back_phase(l, x_cur, yc, self.out_d, True)
                else:
                    if self.x_ext_out:
                        xn = self.x_out
                    else:
                        xn = self.x_scr[nxt]
                        nxt ^= 1
                    self.back_phase(l, x_cur, yc, xn, False)
                    x_cur = xn
        B.finish()

    def mod_phase(self, l, gate_only=False):
        B, e, nc = self.B, self.e, self.nc
        with ExitStack() as ph:
            wm = [ph.enter_context(_sbt(nc, "wm%d" % i, [128, 3 * D], F32)) for i in range(2)]
            modbc = ph.enter_context(_sbt(nc, "modbc", [128, 3 * D], F32))
            bmod = ph.enter_context(_sbt(nc, "bmod", [128, 3 * D], F32))
            junk = ph.enter_context(_sbt(nc, "junk", [128, 128], F32))
            B.dma('sp', bmod[:], self.bmod_rep[l], writes=['bmod'])
            for j in range(2):
                for kc in range(KC):
                    b = kc % 2
                    B.dma('sp', wm[b][:], self.w_mod[l, kc * 128:(kc + 1) * 128, :], writes=[('wm', b)])
                    for nb in range(6):
                        e.mm(self.banks[:, nb, :], self.screp[:, j, kc, :], wm[b][:, nb * 512:(nb + 1) * 512],
                             kc == 0, kc == KC - 1, [('wm', b), 'screp'], [self.bk(nb)])
                for nb in range(6):
                    e.tt('dve', modbc[:, nb * 512:(nb + 1) * 512], self.banks[:, nb, :],
                         bmod[:, nb * 512:(nb + 1) * 512], ALU.add, ['bmod'], ['modbc', self.bk(nb)])
                if not gate_only:
                    for ch in range(16):
                        e.tt('dve', junk[:], modbc[:, ch * 128:(ch + 1) * 128], self.c(C_ID), ALU.mult,
                             ['modbc', 'cst'], ['junk'])
                        e.red(self.modcol[:, ch, j:j + 1], junk[:], ALU.add, ['junk'], ['modcol'])
                e.cp('pool', self.gate_bc[:, j, :], modbc[:, 2 * D:3 * D], ['modbc'], ['gate_bc'])
            if not gate_only:
                e.ts('dve', self.modcol[:, 8:16, :], self.modcol[:, 8:16, :], 1.0, None, ALU.add, None,
                     ['modcol'], ['modcol'])
            B.barrier()

    def u_phase(self, l, x_cur):
        B, e, nc = self.B, self.e, self.nc
        with ExitStack() as ph:
            xt = [ph.enter_context(_sbt(nc, "u_xt%d" % i, [128, D], F32)) for i in range(2)]
            ut = [ph.enter_context(_sbt(nc, "u_ut%d" % i, [128, KC, 128], BF16)) for i in range(2)]
            for ti in range(NT):
                j = 1 if ti < 2 else 0
                b = ti % 2
                B.dma('sp', xt[b][:], x_cur[ti * 128:(ti + 1) * 128, :], reads=['X'], writes=[('u_xt', b)])
                for kc in range(KC):
                    bank = 2 * b + kc // 4
                    e.tr(self.banks[:, bank, (kc % 4) * 128:(kc % 4 + 1) * 128], xt[b][:, kc * 128:(kc + 1) * 128],
                         self.c(C_ID), [('u_xt', b), 'cst'], [self.bk(bank)])
                for kc in range(KC):
                    bank = 2 * b + kc // 4
                    src = self.banks[:, bank, (kc % 4) * 128:(kc % 4 + 1) * 128]
                    if kc % 2 == 0:
                        e.ts('dve', ut[b][:, kc, :], src, self.modcol[:, 8 + kc, j:j + 1], self.modcol[:, kc, j:j + 1],
                             ALU.mult, ALU.add, ['modcol'], [('u_ut', b), self.bk(bank)])
                    else:
                        e.act(ut[b][:, kc, :], src, AF.Identity, ['modcol'], [('u_ut', b), self.bk(bank)],
                              bias=self.modcol[:, kc, j:j + 1], scale=self.modcol[:, 8 + kc, j:j + 1])
                B.dma('pool', self.uT_d[:, :, ti * 128:(ti + 1) * 128], ut[b][:], reads=[('u_ut', b)], writes=['uT'])
            B.barrier()

    def load_w(self, ph, l, g, col0, ncols, tag):
        B, e, nc = self.B, self.e, self.nc
        wsb = ph.enter_context(_sbt(nc, "wsb_" + tag, [128, KC, ncols], BF16))
        stg = [ph.enter_context(_sbt(nc, "wstg%d_%s" % (i, tag), [128, ncols], F32)) for i in range(2)]
        c0 = g * GCOLS + col0
        for kc in range(KC):
            b = kc % 2
            B.dma('sp', stg[b][:], self.w_in[l, kc * 128:(kc + 1) * 128, c0:c0 + ncols], writes=[('wstg', b)])
            e.cp('pool', wsb[:, kc, :], stg[b][:], [('wstg', b)], ['wsb'])
        return wsb

    def proj(self, ph, wsb, fm_tiles, tm_ranges, fm_sink, tm_sink):
        B, e, nc = self.B, self.e, self.nc
        ub = [ph.enter_context(_sbt(nc, "ub%d" % i, [128, KC, 512], BF16)) for i in range(2)]
        rr = 0
        for bi, (s, n) in enumerate(BLOCKS):
            b = bi % 2
            B.dma('sp', ub[b][:, :, 0:n], self.uT_d[:, :, s:s + n], reads=['uT'], writes=[('ub', b)])
            for idx, (c0, M) in enumerate(fm_tiles):
                bank = rr % 4
                rr += 1
                for kc in range(KC):
                    e.mm(self.banks[0:M, bank, 0:n], wsb[:, kc, c0:c0 + M], ub[b][:, kc, 0:n], kc == 0, kc == KC - 1,
                         ['wsb', ('ub', b)], [self.bk(bank)])
                fm_sink(idx, s, n, self.banks[0:M, bank, 0:n], self.bk(bank))
            for tl in range(n // 128):
                ti = s // 128 + tl
                for idx, (c0, W) in enumerate(tm_ranges):
                    bank = rr % 4
                    rr += 1
                    for kc in range(KC):
                        e.mm(self.banks[:, bank, 0:W], ub[b][:, kc, tl * 128:(tl + 1) * 128], wsb[:, kc, c0:c0 + W],
                             kc == 0, kc == KC - 1, ['wsb', ('ub', b)], [self.bk(bank)])
                    tm_sink(idx, ti, self.banks[:, bank, 0:W], self.bk(bank))

    def conv(self, P, out, cw, tile, np_, r, w):
        e = self.e
        for (s, n) in [(0, 256), (256, 4096)]:
            base = pcol(s) - 2
            o = out[0:np_, s:s + n]
            e.ts('dve', o, P[0:np_, base:base + n], cw[0:np_, tile, 0:1], None, ALU.mult, None, r, w)
            for j in range(1, 4):
                e.stt(o, P[0:np_, base + j:base + j + n], cw[0:np_, tile, j:j + 1], o, ALU.mult, ALU.add, r, w)

    def zero_pads(self, P, key):
        e = self.e
        for a, b in [(0, 2), (258, 262), (4358, 4360)]:
            e.memset('pool', P[:, a:b], 0.0, [key])

    def rg_pass(self, l, g, yc, row0):
        B, e, nc = self.B, self.e, self.nc
        with ExitStack() as p0:
            xa = [p0.enter_context(_sbt(nc, "rg_xa%d" % i, [128, T], F32)) for i in range(2)]
            zs = [p0.enter_context(_sbt(nc, "rg_zs%d" % i, [128, T], BF16)) for i in range(2)]
            cw = p0.enter_context(_sbt(nc, "rg_cw", [128, 5, 4], F32))
            B.dma('sp', cw[:], self.convw[l, g], writes=['cw'])
            NP = [128, 64]
            with ExitStack() as ph:
                wsb = self.load_w(ph, l, g, 0, 384, "rg")
                P = [ph.enter_context(_sbt(nc, "rg_P%d" % i, [128, PADW], F32)) for i in range(2)]
                for i in range(2):
                    self.zero_pads(P[i], ('rgP', i))

                def fm_sink(idx, s, n, ps, bkey):
                    if idx < 2:
                        e.cp('act', P[idx][0:NP[idx], pcol(s):pcol(s) + n], ps, [], [bkey, ('rgP', idx)])
                    else:
                        i = idx - 2
                        e.act(zs[i][0:NP[i], s:s + n], ps, AF.Silu, [], [bkey, ('rg_zs', i)])

                self.proj(ph, wsb, [(0, 128), (128, 64), (192, 128), (320, 64)], [], fm_sink, None)
                for i in range(2):
                    self.conv(P[i], xa[i], cw, i, NP[i], [('rgP', i), 'cw'], [('rg_xa', i)])
                B.barrier()
            if self.debug:
                self.dump("rg_xa0_%d_%d" % (l, g), xa[0][:], [128, T], [('rg_xa', 0)])
            with ExitStack() as ph:
                hh = [[ph.enter_context(_sbt(nc, "rg_h%d_%d" % (d, i), [128, T], F32)) for i in range(2)]
                      for d in range(2)]
                w = ph.enter_context(_sbt(nc, "rg_w", [128, 2, 2, 192], F32))
                bb = ph.enter_context(_sbt(nc, "rg_b", [128, 2, 2, 3], F32))
                c8 = ph.enter_context(_sbt(nc, "rg_c8", [128, 2, 2], F32))
                tmp = {}
                for d in range(2):
                    for i in range(2):
                        for nm in ['r', 'i', 's']:
                            tmp[(nm, d, i)] = ph.enter_context(_sbt(nc, "rg_t%s%d%d" % (nm, d, i), [128, 512], F32))
                B.dma('sp', w[:], self.rgw[l, g], writes=['rg_w'])
                B.dma('sp', bb[:], self.rgb[l, g], writes=['rg_b'])
                e.act(c8[:], bb[:, :, :, 2], AF.Exp, ['rg_b'], ['rg_c8'], scale=-1.0)
                e.act(c8[:], c8[:], AF.Ln, ['rg_c8'], ['rg_c8'], bias=1.0)
                e.ts('dve', c8[:], c8[:], -8.0, None, ALU.mult, None, ['rg_c8'], ['rg_c8'])
                wcols = [(0, 128), (128, 192)]
                orders = {0: BLOCKS, 1: [BLOCKS[0]] + BLOCKS[:0:-1]}

                def chain(d, i):
                    np_ = NP[i]
                    prev = None
                    ci = 2 * d + i
                    b0, b1 = 2 * ci, 2 * ci + 1
                    c0, c1 = wcols[i]
                    kh_ = ('rg_h', d, i)
                    for (s, n) in orders[d]:
                        tr_, ti_, ts_ = (tmp[(nm, d, i)][0:np_, 0:n] for nm in ['r', 'i', 's'])
                        kr, ki, ks = (('rg_t', nm, d, i) for nm in ['r', 'i', 's'])
                        xin = xa[i][0:np_, s:s + n]
                        e.mm(self.banks[0:np_, b0, 0:n], w[0:np_, d, 0, c0:c1], xin, True, True,
                             ['rg_w', ('rg_xa', i)], [self.bk(b0)])
                        e.mm(self.banks[0:np_, b1, 0:n], w[0:np_, d, 1, c0:c1], xin, True, True,
                             ['rg_w', ('rg_xa', i)], [self.bk(b1)])
                        yield
                        e.act(tr_, self.banks[0:np_, b0, 0:n], AF.Sigmoid, ['rg_b'], [kr, self.bk(b0)],
                              bias=bb[0:np_, i, d, 0:1])
                        e.act(ti_, self.banks[0:np_, b1, 0:n], AF.Sigmoid, ['rg_b'], [ki, self.bk(b1)],
                              bias=bb[0:np_, i, d, 1:2])
                        yield
                        e.act(tr_, tr_, AF.Exp, [kr, 'rg_c8'], [kr], scale=c8[0:np_, i, d:d + 1])
                        yield
                        e.tt('pool', ts_, tr_, tr_, ALU.mult, [kr], [ks])
                        yield
                        e.act(ts_, ts_, AF.Sqrt, [ks], [ks], bias=1.0, scale=-1.0)
                        yield
                        e.tt('dve', ti_, ts_, ti_, ALU.mult, [ks, ki], [ki])
                        yield
                        e.tt('pool', ti_, ti_, xin, ALU.mult, [ki, ('rg_xa', i)], [ki])
                        yield
                        init = prev if prev is not None else 0.0
                        dst = hh[d][i][0:np_, s:s + n]
                        if d == 1:
                            e.scan(rev(dst), rev(tr_), rev(ti_), init, [kr, ki, kh_], [kh_])
                            prev = hh[d][i][0:np_, s:s + 1]
                        else:
                            e.scan(dst, tr_, ti_, init, [kr, ki, kh_], [kh_])
                            prev = hh[d][i][0:np_, s + n - 1:s + n]
                        yield

                active = [chain(d, i) for d in range(2) for i in range(2)]
                while active:
                    nxt = []
                    for gnr in active:
                        try:
                            next(gnr)
                            nxt.append(gnr)
                        except StopIteration:
                            pass
                    active = nxt
                for i in range(2):
                    np_ = NP[i]
                    hf, hb = hh[0][i][0:np_, :], hh[1][i][0:np_, :]
                    for (s, n) in [(0, 2176), (2176, 2176)]:
                        e.tt('pool', hf[:, s:s + n], hf[:, s:s + n], hb[:, s:s + n], ALU.add,
                             [('rg_h', 0, i), ('rg_h', 1, i)], [('rg_h', 0, i)])
                        e.tt('dve', zs[i][0:np_, s:s + n], hf[:, s:s + n], zs[i][0:np_, s:s + n], ALU.mult,
                             [('rg_h', 0, i), ('rg_zs', i)], [('rg_zs', i)])
                    r0 = row0 + 128 * i
                    B.dma('pool', yc[r0:r0 + np_, :], zs[i][0:np_, :], reads=[('rg_zs', i)], writes=['YC'])
                B.barrier()

    def na_pass(self, l, g, yc, row0, ctx_out):
        B, e, nc = self.B, self.e, self.nc
        bnk = self.banks
        NP = [128, 64]
        with ExitStack() as p0:
            qT = [p0.enter_context(_sbt(nc, "na_q%d" % i, [128, T], BF16)) for i in range(2)]
            kT = [p0.enter_context(_sbt(nc, "na_k%d" % i, [128, T], BF16)) for i in range(2)]
            zn = p0.enter_context(_sbt(nc, "na_zn", [128, NT, 192], BF16))
            v_tm = p0.enter_context(_sbt(nc, "na_v", [128, NT, 192], BF16))
            with ExitStack() as ph:
                wsb = self.load_w(ph, l, g, 384, 768, "na")

                def fm_sink(idx, s, n, ps, bkey):
                    i = idx % 2
                    if idx < 2:
                        e.act(qT[i][0:NP[i], s:s + n], ps, AF.Copy, [], [bkey, 'na_q'], scale=0.125)
                    else:
                        e.cp('dve', kT[i][0:NP[i], s:s + n], ps, [], [bkey, 'na_k'])

                def tm_sink(idx, ti, ps, bkey):
                    e.act(zn[:, ti, :], ps[:, 0:192], AF.Silu, [], [bkey, 'na_zn'])
                    e.cp('dve', v_tm[:, ti, :], ps[:, 192:384], [], [bkey, 'na_v'])

                self.proj(ph, wsb, [(0, 128), (128, 64), (192, 128), (320, 64)], [(384, 384)], fm_sink, tm_sink)
                B.barrier()
            with ExitStack() as ph:
                def sbt(name, shape, dt=F32):
                    return ph.enter_context(_sbt(nc, name, shape, dt))
                nab = sbt("na_bias", [128, 3, 5, 640])
                NB = 2
                S = [[sbt("na_S%d_%d" % (p, h), [128, 896]) for h in range(3)] for p in range(NB)]
                Pb = [[sbt("na_Pb%d_%d" % (p, h), [128, 896], BF16) for h in range(3)] for p in range(NB)]
                PT = [[sbt("na_PT%d_%d" % (p, h), [128, 896], BF16) for h in range(3)] for p in range(NB)]
                st = [[sbt("na_st%d_%d" % (p, h), [128, 4]) for h in range(3)] for p in range(NB)]
                ytm = [sbt("na_ytm%d" % p, [128, 192], BF16) for p in range(NB)]
                ybT = [sbt("na_ybT%d" % p, [128, 256], BF16) for p in range(NB)]
                for h in range(3):
                    B.dma('sp', nab[:, h], self.nab[l, g, h], writes=['na_bias'])

                def qt_gen(it, qt, lat):
                    p = it % NB
                    if lat:
                        cls = {0: 0, 1: 1, 30: 3, 31: 4}.get(qt, 2)
                        kt0 = min(max(qt - 2, 0), 27)
                        tq = 2 + qt
                        kcol = (2 + kt0) * 128
                        ktiles = [2 + kt0 + c for c in range(5)] + [0, 1]
                        nk = 896
                    else:
                        tq = qt
                        ktiles = [0, 1]
                        nk = 256
                    qcol = tq * 128
                    nch = nk // 128
                    hs = range(3)
                    HR = [(0, 0), (0, 64), (1, 0)]
                    kS = [('na_S', p, h) for h in hs]
                    kP = [('na_Pb', p, h) for h in hs]
                    kPT = [('na_PT', p, h) for h in hs]
                    kst = [('na_st', p, h) for h in hs]
                    for h in hs:
                        ti_, hr0 = HR[h]
                        qa = qT[ti_][hr0:hr0 + 64, qcol:qcol + 128]
                        k_ = kT[ti_]
                        b0, b1 = 2 * h, 2 * h + 1
                        if lat:
                            e.mm(bnk[:, b0, 0:512], qa, k_[hr0:hr0 + 64, kcol:kcol + 512], True, True,
                                 ['na_q', 'na_k'], [self.bk(b0)])
                            e.mm(bnk[:, b1, 0:128], qa, k_[hr0:hr0 + 64, kcol + 512:kcol + 640], True, True,
                                 ['na_q', 'na_k'], [self.bk(b1)])
                            e.mm(bnk[:, b1, 128:384], qa, k_[hr0:hr0 + 64, 0:256], True, True,
                                 ['na_q', 'na_k'], [self.bk(b1)])
                        else:
                            e.mm(bnk[:, b1, 128:384], qa, k_[hr0:hr0 + 64, 0:256], True, True,
                                 ['na_q', 'na_k'], [self.bk(b1)])
                    yield
                    for h in hs:
                        b0, b1 = 2 * h, 2 * h + 1
                        if lat:
                            e.tt('dve', S[p][h][:, 0:512], bnk[:, b0, :], nab[:, h, cls, 0:512], ALU.add,
                                 ['na_bias'], [kS[h], self.bk(b0)])
                            e.tt('dve', S[p][h][:, 512:640], bnk[:, b1, 0:128], nab[:, h, cls, 512:640], ALU.add,
                                 ['na_bias'], [kS[h], self.bk(b1)])
                            e.cp('act', S[p][h][:, 640:896], bnk[:, b1, 128:384], [], [kS[h], self.bk(b1)])
                        else:
                            e.cp('act', S[p][h][:, 0:256], bnk[:, b1, 128:384], [], [kS[h], self.bk(b1)])
                    yield
                    for h in hs:
                        e.red(st[p][h][:, 0:1], S[p][h][:, 0:nk], ALU.max, [kS[h]], [kst[h]], negate=True)
                    yield
                    for h in hs:
                        e.act(Pb[p][h][:, 0:nk], S[p][h][:, 0:nk], AF.Exp, [kS[h], kst[h]], [kP[h], (kst[h], 'sum')],
                              bias=st[p][h][:, 0:1], scale=1.0, accum=st[p][h][:, 1:2])
                    yield
                    for h in hs:
                        pbf = bnk[:, 2 * h, :].bitcast(BF16)
                        for c in range(nch):
                            e.tr(pbf[:, c * 128:(c + 1) * 128], Pb[p][h][:, c * 128:(c + 1) * 128], self.identb[:],
                                 [kP[h], 'identb'], [self.bk(2 * h)])
                        e.recip(st[p][h][:, 2:3], st[p][h][:, 1:2], [(kst[h], 'sum')], [(kst[h], 'ri')])
                    yield
                    for h in hs:
                        pbf = bnk[:, 2 * h, :].bitcast(BF16)
                        e.cp('act' if h != 1 else 'dve', PT[p][h][:, 0:nk], pbf[:, 0:nk], [], [kPT[h], self.bk(2 * h)])
                    yield
                    for h in hs:
                        for c in range(nch):
                            e.mm(bnk[:, 6, h * 64:(h + 1) * 64], PT[p][h][:, c * 128:(c + 1) * 128],
                                 v_tm[:, ktiles[c], h * 64:(h + 1) * 64], c == 0, c == nch - 1,
                                 ['na_v', kPT[h]], [self.bk(6)])
                    yield
                    for h in hs:
                        e.stt(ytm[p][:, h * 64:(h + 1) * 64], bnk[:, 6, h * 64:(h + 1) * 64], st[p][h][:, 2:3],
                              zn[:, tq, h * 64:(h + 1) * 64], ALU.mult, ALU.mult, [(kst[h], 'ri'), 'na_zn'],
                              [('na_ytm', p), self.bk(6)])
                    yield
                    pb7 = bnk[:, 7, :].bitcast(BF16)
                    e.tr(pb7[:, 0:128], ytm[p][:, 0:128], self.identb[:], [('na_ytm', p), 'identb'], [self.bk(7)])
                    e.tr(pb7[0:64, 128:256], ytm[p][:, 128:192], self.identb[:], [('na_ytm', p), 'identb'], [self.bk(7)])
                    yield
                    e.cp('act', ybT[p][:, 0:128], pb7[:, 0:128], [], [('na_ybT', p), self.bk(7)])
                    e.cp('dve', ybT[p][0:64, 128:256], pb7[0:64, 128:256], [], [('na_ybT', p), self.bk(7)])
                    yield
                    B.dma('pool', yc[row0:row0 + 128, qcol:qcol + 128], ybT[p][:, 0:128], reads=[('na_ybT', p)],
                          writes=['YC'])
                    B.dma('pool', yc[row0 + 128:row0 + 192, qcol:qcol + 128], ybT[p][0:64, 128:256],
                          reads=[('na_ybT', p)], writes=['YC'])
                    yield

                qts = [(qt, True) for qt in range(32)]
                if ctx_out:
                    qts += [(0, False), (1, False)]
                pending = [qt_gen(i, qt, lat) for i, (qt, lat) in enumerate(qts)]
                active = []
                rounds = 0
                while pending or active:
                    if pending and (rounds % 5 == 0) and len(active) < NB:
                        active.append(pending.pop(0))
                    nxt = []
                    for gnr in active:
                        try:
                            next(gnr)
                            nxt.append(gnr)
                        except StopIteration:
                            pass
                    active = nxt
                    rounds += 1
                B.barrier()

    def gdn_pass(self, l, g, yc, row0):
        B, e, nc = self.B, self.e, self.nc
        bnk = self.banks
        with ExitStack() as p0:
            qT = p0.enter_context(_sbt(nc, "gd_q", [128, T], F32))
            kT = p0.enter_context(_sbt(nc, "gd_k", [128, T], F32))
            zg = p0.enter_context(_sbt(nc, "gd_zg", [128, NT, 128], BF16))
            ba = p0.enter_context(_sbt(nc, "gd_ba", [128, NT, 8], F32))
            vT = p0.enter_context(_sbt(nc, "gd_v", [128, T], F32))
            cw = p0.enter_context(_sbt(nc, "gd_cw", [128, 5, 4], F32))
            with ExitStack() as p1:
                B.dma('sp', cw[:], self.convw[l, g], writes=['cw'])
                with ExitStack() as ph:
                    wsb = self.load_w(ph, l, g, 1152, 520, "gd")
                    P = [ph.enter_context(_sbt(nc, "gd_P%d" % i, [128, PADW], F32)) for i in range(3)]
                    for i in range(3):
                        self.zero_pads(P[i], ('gdP', i))

                    def fm_sink(idx, s, n, ps, bkey):
                        e.cp('act', P[idx][:, pcol(s):pcol(s) + n], ps, [], [bkey, ('gdP', idx)])

                    def tm_sink(idx, ti, ps, bkey):
                        e.act(zg[:, ti, :], ps[:, 0:128], AF.Silu, [], [bkey, 'gd_zg'])
                        e.cp('dve', ba[:, ti, :], ps[:, 128:136], [], [bkey, 'gd_ba'])

                    self.proj(ph, wsb, [(0, 128), (128, 128), (256, 128)], [(384, 136)], fm_sink, tm_sink)
                    for i, dst in enumerate([qT, kT, vT]):
                        self.conv(P[i], dst, cw, 2 + i, 128, [('gdP', i), 'cw'], [('gd_c', i)])
                    B.barrier()
                kv = p1.enter_context(_sbt(nc, "gd_kv", [128, NT, 256], F32))
                with ExitStack() as ph:
                    sq = [ph.enter_context(_sbt(nc, "gd_sq%d" % i, [128, 512], F32)) for i in range(2)]
                    nr = [ph.enter_context(_sbt(nc, "gd_nr%d" % i, [128, 512], F32)) for i in range(2)]
                    t1 = [ph.enter_context(_sbt(nc, "gd_t1%d" % i, [128, 512], F32)) for i in range(2)]
                    cs = [ph.enter_context(_sbt(nc, "gd_cs%d" % i, [128, 512], F32)) for i in range(2)]
                    sn = [ph.enter_context(_sbt(nc, "gd_sn%d" % i, [128, 512], F32)) for i in range(2)]
                    for bi, (s, n) in enumerate(BLOCKS):
                        lat = s >= 256
                        pb = bi % 2
                        if lat:
                            B.dma('sp', cs[pb][:], self.rope_cos[:, s - 256:s - 256 + 512], writes=[('gd_cs', pb)])
                            B.dma('sp', sn[pb][:], self.rope_sin[:, s - 256:s - 256 + 512], writes=[('gd_sn', pb)])
                        for wi, Xt in enumerate([qT, kT]):
                            X = Xt[:, s:s + n]
                            kx = ('gd_c', wi)
                            p = wi
                            b0, b1 = 4 + 2 * p, 5 + 2 * p
                            e.act(X, X, AF.Silu, [kx], [kx])
                            e.tt('pool', sq[p][:, 0:n], X, X, ALU.mult, [kx], [('gd_sq', p)])
                            e.mm(bnk[:, b0, 0:n], self.c(C_BONE), sq[p][:, 0:n], True, True, [('gd_sq', p), 'cst'],
                                 [self.bk(b0)])
                            e.act(nr[p][:, 0:n], bnk[:, b0, 0:n], AF.Sqrt, [], [('gd_nr', p), self.bk(b0)], bias=NORM_EPS)
                            e.recip(nr[p][:, 0:n], nr[p][:, 0:n], [('gd_nr', p)], [('gd_nr', p)])
                            if wi == 0:
                                e.stt(X, X, 0.125, nr[p][:, 0:n], ALU.mult, ALU.mult, [kx, ('gd_nr', p)], [kx])
                            else:
                                e.tt('dve', X, X, nr[p][:, 0:n], ALU.mult, [kx, ('gd_nr', p)], [kx])
                            if lat:
                                e.mm(bnk[:, b1, 0:n], self.c(C_RM), X, True, True, [kx, 'cst'], [self.bk(b1)])
                                e.tt('pool', t1[p][:, 0:n], X, cs[pb][:, 0:n], ALU.mult, [kx, ('gd_cs', pb)], [('gd_t1', p)])
                                e.tt('dve', X, bnk[:, b1, 0:n], sn[pb][:, 0:n], ALU.mult, [('gd_sn', pb)],
                                     [kx, self.bk(b1)])
                                e.tt('pool', X, X, t1[p][:, 0:n], ALU.add, [kx, ('gd_t1', p)], [kx])
                        e.act(vT[:, s:s + n], vT[:, s:s + n], AF.Silu, [('gd_c', 2)], [('gd_c', 2)])
                    for ti in range(NT):
                        b0 = ti % 4
                        e.tr(bnk[:, b0, 0:128], kT[:, ti * 128:(ti + 1) * 128], self.c(C_ID), [('gd_c', 1), 'cst'],
                             [self.bk(b0)])
                        e.tr(bnk[:, b0, 128:256], vT[:, ti * 128:(ti + 1) * 128], self.c(C_ID), [('gd_c', 2), 'cst'],
                             [self.bk(b0)])
                        e.cp('act' if ti % 2 else 'dve', kv[:, ti, :], bnk[:, b0, 0:256], [], ['gd_kv', self.bk(b0)])
                    B.barrier()
                if self.debug:
                    self.dump("gd_q_%d_%d" % (l, g), qT[:], [128, T], [('gd_c', 0)])
                    self.dump("gd_k_%d_%d" % (l, g), kT[:], [128, T], [('gd_c', 1)])
                    self.dump("gd_kv_%d_%d" % (l, g), kv[:], [128, NT, 256], ['gd_kv'])
                with ExitStack() as ph:
                    def sbt(name, shape, dt=F32):
                        return ph.enter_context(_sbt(nc, name, shape, dt))
                    o_tm = [sbt("gd_o0", [128, NT, 128]), vT[:].rearrange("p (t c) -> p t c", c=128)]
                    al = sbt("gd_al", [128, NT, 4])
                    dtb = sbt("gd_dtb", [128, NT, 4])
                    g_all = sbt("gd_g", [128, NT, 4])
                    beta = sbt("gd_beta", [128, NT, 4])
                    nbeta = sbt("gd_nbeta", [128, NT, 4])
                    tg = sbt("gd_tg", [128, NT, 4])
                    B.dma('sp', al[:], self.galog[l, g], writes=['gd_al'])
                    B.dma('sp', dtb[:], self.gdtb[l, g], writes=['gd_dtb'])
                    e.act(beta[:], ba[:, :, 0:4], AF.Sigmoid, ['gd_ba'], ['gd_beta'])
                    e.ts('dve', nbeta[:], beta[:], -1.0, None, ALU.mult, None, ['gd_beta'], ['gd_nbeta'])
                    e.tt('dve', tg[:], ba[:, :, 4:8], dtb[:], ALU.add, ['gd_ba', 'gd_dtb'], ['gd_tg'])
                    e.act(tg[:], tg[:], AF.Exp, ['gd_tg'], ['gd_tg'])
                    e.act(tg[:], tg[:], AF.Ln, ['gd_tg'], ['gd_tg'], bias=1.0)
                    e.act(al[:], al[:], AF.Exp, ['gd_al'], ['gd_al'])
                    e.stt(g_all[:], tg[:], -1.0, al[:], ALU.mult, ALU.mult, ['gd_tg', 'gd_al'], ['gd_g'])
                    if self.debug:
                        self.dump("gd_g_%d_%d" % (l, g), g_all[:], [128, NT, 4], ['gd_g'])
                        self.dump("gd_beta_%d_%d" % (l, g), beta[:], [128, NT, 4], ['gd_beta'])
                    gc = sbt("gd_gc", [128, NT, 4])
                    gl = sbt("gd_gl", [128, NT, 4])
                    eg = sbt("gd_eg", [128, NT, 4])
                    neg_eg = sbt("gd_negeg", [128, NT, 4])
                    negc = sbt("gd_negc", [128, NT, 4])
                    egl = sbt("gd_egl", [128, NT, 4])
                    eglast = sbt("gd_eglast", [128, NT, 4])
                    gflat = g_all[:].rearrange("p t c -> p (t c)")
                    NV = NT * 4
                    e.mm(bnk[:, 0, 0:NV], self.c(C_UF), gflat, True, True, ['cst', 'gd_g'], [self.bk(0)])
                    e.mm(bnk[:, 0, NV:2 * NV], self.c(C_UB), gflat, True, True, ['cst', 'gd_g'], [self.bk(0)])
                    e.mm(bnk[:, 0, 2 * NV:3 * NV], self.c(C_ONE), gflat, True, True, ['cst', 'gd_g'], [self.bk(0)])
                    pv0 = bnk[:, 0, 0:NV].rearrange("p (t c) -> p t c", c=4)
                    pv1 = bnk[:, 0, NV:2 * NV].rearrange("p (t c) -> p t c", c=4)
                    pv2 = bnk[:, 0, 2 * NV:3 * NV].rearrange("p (t c) -> p t c", c=4)
                    e.cp('dve', gc[:, :, 0:2], pv0[:, :, 0:2], [], ['gd_gc', self.bk(0)])
                    e.cp('dve', gc[:, :, 2:4], pv1[:, :, 2:4], [], ['gd_gc', self.bk(0)])
                    e.cp('dve', gl[:], pv2, [], ['gd_gl', self.bk(0)])
                    e.act(eg[:], gc[:], AF.Exp, ['gd_gc'], ['gd_vecs'])
                    e.ts('pool', neg_eg[:], eg[:], -1.0, None, ALU.mult, None, ['gd_vecs'], ['gd_vecs2'])
                    e.ts('pool', negc[:], gc[:], -1.0, None, ALU.mult, None, ['gd_gc'], ['gd_negc'])
                    e.tt('pool', egl[:], gl[:], gc[:], ALU.subtract, ['gd_gc', 'gd_gl'], ['gd_egl'])
                    e.act(egl[:], egl[:], AF.Exp, ['gd_egl'], ['gd_egl'])
                    e.act(eglast[:], gl[:], AF.Exp, ['gd_gl'], ['gd_eglast'])
                    R = 3
                    chains = [(d, h) for d in range(2) for h in range(2)]
                    T2t = {c: [sbt("gd_T2t%d%d_%d" % (c[0], c[1], r), [128, 128]) for r in range(R)] for c in chains}
                    QKt = {c: [sbt("gd_QKt%d%d_%d" % (c[0], c[1], r), [128, 128]) for r in range(R)] for c in chains}
                    vec = {c: [sbt("gd_vec%d%d_%d" % (c[0], c[1], r), [128, 4]) for r in range(R)] for c in chains}
                    Sst = {c: sbt("gd_S%d%d" % c, [128, 64]) for c in chains}
                    Rp = {c: sbt("gd_Rp%d%d" % c, [128, 64]) for c in chains}
                    qs = {c: sbt("gd_qs%d%d" % c, [128, 64]) for c in chains}
                    vnw = {c: sbt("gd_vn%d%d" % c, [128, 64]) for c in chains}
                    kh = {c: sbt("gd_kh%d%d" % c, [128, 64]) for c in chains}
                    Gm = [sbt("gd_G%d" % i, [128, 128]) for i in range(4)]
                    nG = [sbt("gd_nG%d" % i, [128, 128]) for i in range(4)]
                    Et = [sbt("gd_Et%d" % i, [128, 128]) for i in range(4)]
                    tv = [sbt("gd_tv%d" % i, [128, 2]) for i in range(4)]
                    YT = [[sbt("gd_YT%d_%d" % (i, k), [128, 256]) for k in range(2)] for i in range(4)]
                    Yt = [[sbt("gd_Yt%d_%d" % (i, k), [128, 128]) for k in range(2)] for i in range(4)]
                    for c in chains:
                        e.memset('pool', Sst[c][:], 0.0, [('gd_S', c)])
                    ident = self.c(C_ID)
                    ones = self.c(C_ONE)
                    orders = {0: list(range(NT)), 1: [1, 0] + list(range(NT - 1, 1, -1))}

                    def prep(c, ti, slot):
                        d, h = c
                        col = 2 * d + h
                        pp = chains.index(c)
                        U = self.c(C_UF if d == 0 else C_UB)
                        Sm = self.c(C_SF if d == 0 else C_SB)
                        Mk = self.c(C_MF if d == 0 else C_MB)
                        hr0 = 64 * h
                        tc0 = ti * 128
                        gcol = g_all[:, ti, col:col + 1]
                        kG, knG, kEt, ktv = ('gd_G', pp), ('gd_nG', pp), ('gd_Et', pp), ('gd_tv', pp)
                        kslot = ('gd_slot', c, slot)
                        bP = pp
                        kb = self.bk(bP)
                        pD = bnk[:, bP, 0:128]
                        pV = bnk[:, bP, 128:130]
                        pKK = bnk[:, bP, 132:260]
                        pQK = bnk[:, bP, 260:388]
                        pI = bnk[:, bP, 0:256]
                        pJ = bnk[:, bP, 256:384]
                        e.ts('pool', Gm[pp][:], U, gcol, None, ALU.mult, None, ['cst', 'gd_g'], [kG])
                        yield
                        e.mm(pD, ones, Gm[pp][:], True, True, ['cst', kG], [kb])
                        ka = kT[hr0:hr0 + 64, tc0:tc0 + 128]
                        qa = qT[hr0:hr0 + 64, tc0:tc0 + 128]
                        e.mm(pKK, ka, ka, True, True, [('gd_c', 1)], [kb])
                        e.mm(pQK, ka, qa, True, True, [('gd_c', 1), ('gd_c', 0)], [kb])
                        yield
                        e.stt(nG[pp][:], pD, negc[:, ti, col:col + 1], Mk, ALU.add, ALU.add, ['gd_negc', 'cst'], [knG, kb])
                        yield
                        e.act(Et[pp][:], nG[pp][:], AF.Exp, [knG], [kEt])
                        yield
                        vc = vec[c][slot]
                        Y0 = YT[pp][0]
                        kYT = [('gd_YT', pp, 0), ('gd_YT', pp, 1)]
                        kYt = [('gd_Yt', pp, 0), ('gd_Yt', pp, 1)]
                        e.stt(Y0[:, 0:128], pKK, nbeta[:, ti, col:col + 1], Et[pp][:], ALU.mult, ALU.mult,
                              ['gd_nbeta', kEt], [kYT[0], kb])
                        e.tt('dve', QKt[c][slot][:], pQK, Et[pp][:], ALU.mult, [kEt], [(kslot, 'QK'), kb])
                        yield
                        e.tt('pool', Y0[:, 0:128], Y0[:, 0:128], Sm, ALU.mult, [kYT[0], 'cst'], [kYT[0]])
                        yield
                        e.tr(pJ, Y0[:, 0:128], ident, [kYT[0], 'cst'], [kb])
                        e.tt('pool', Y0[:, 128:256], Y0[:, 0:128], ident, ALU.add, [kYT[0], 'cst'], [kYT[0]])
                        yield
                        e.cp('act', Yt[pp][0][:], pJ, [], [kYt[0], kb])
                        yield
                        e.mm(bnk[:, bP, 0:128], Yt[pp][0][:], Y0[:, 0:128], True, True, [kYt[0], kYT[0]], [kb])
                        e.mm(pJ, Y0[:, 0:128], Yt[pp][0][:], True, True, [kYt[0], kYT[0]], [kb])
                        yield
                        e.cp('act', YT[pp][1][:, 0:128], bnk[:, bP, 0:128], [], [kYT[1], kb])
                        e.cp('dve', Yt[pp][1][:], pJ, [], [kYt[1], kb])
                        e.cp('pool', YT[pp][1][:, 128:256], Y0[:, 128:256], [kYT[0]], [kYT[1]])
                        yield
                        for q in range(1, 7):
                            a = q % 2
                            nb_ = 1 - a
                            cur, curt = YT[pp][a], Yt[pp][a]
                            nx, nxt_ = YT[pp][nb_], Yt[pp][nb_]
                            if q < 6:
                                e.mm(pI, curt[:], cur[:, 0:256], True, True, [kYt[a], kYT[a]], [kb])
                                e.mm(pJ, cur[:, 0:128], curt[:], True, True, [kYt[a], kYT[a]], [kb])
                                yield
                                e.cp('act', nx[:, 0:128], bnk[:, bP, 0:128], [], [kYT[nb_], kb])
                                e.tt('dve', nx[:, 128:256], bnk[:, bP, 128:256], cur[:, 128:256], ALU.add, [kYT[a]],
                                     [kYT[nb_], kb])
                                e.cp('act' if q % 2 else 'dve', nxt_[:], pJ, [], [kYt[nb_], kb])
                                yield
                            else:
                                e.mm(bnk[:, bP, 0:128], curt[:], cur[:, 128:256], True, True, [kYt[a], kYT[a]], [kb])
                                yield
                                e.tt('dve', T2t[c][slot][:], bnk[:, bP, 0:128], cur[:, 128:256], ALU.add, [kYT[a]],
                                     [(kslot, 'T'), kb])
                                yield

                    def seq_stage(stage, c, ti, slot):
                        d, h = c
                        col = 2 * d + h
                        hr0 = 64 * h
                        tc0 = ti * 128
                        ci = chains.index(c)
                        bC = 4 + ci
                        kslot = ('gd_slot', c, slot)
                        vc = vec[c][slot]
                        kS = ('gd_S', c)
                        if stage == 0:
                            e.mm(bnk[:, bC, 0:64], kT[hr0:hr0 + 64, tc0:tc0 + 128], Sst[c][hr0:hr0 + 64, :], True, True,
                                 [('gd_c', 1), kS], [self.bk(bC)])
                            e.mm(bnk[:, bC, 64:128], qT[hr0:hr0 + 64, tc0:tc0 + 128], Sst[c][hr0:hr0 + 64, :], True, True,
                                 [('gd_c', 0), kS], [self.bk(bC)])
                            e.ts('pool', kh[c][:], kv[:, ti, hr0:hr0 + 64], egl[:, ti, col:col + 1], None, ALU.mult, None,
                                 ['gd_kv', 'gd_egl'], [('gd_kh', c)])
                        elif stage == 1:
                            e.stt(Rp[c][:], bnk[:, bC, 0:64], neg_eg[:, ti, col:col + 1], kv[:, ti, 128 + hr0:128 + hr0 + 64],
                                  ALU.mult, ALU.add, ['gd_vecs2', 'gd_kv'], [('gd_Rp', c), self.bk(bC)])
                            e.act(qs[c][:], bnk[:, bC, 64:128], AF.Identity, ['gd_vecs'], [('gd_qs', c), self.bk(bC)],
                                  scale=eg[:, ti, col:col + 1])
                        elif stage == 2:
                            e.mm(bnk[:, bC, 128:192], T2t[c][slot][:], Rp[c][:], True, True, [(kslot, 'T'), ('gd_Rp', c)],
                                 [self.bk(bC)])
                        elif stage == 3:
                            e.act(vnw[c][:], bnk[:, bC, 128:192], AF.Identity, ['gd_beta'], [('gd_vn', c), self.bk(bC)],
                                  scale=beta[:, ti, col:col + 1])
                        elif stage == 4:
                            e.mm(bnk[:, bC, 192:256], QKt[c][slot][:], vnw[c][:], True, True, [(kslot, 'QK'), ('gd_vn', c)],
                                 [self.bk(bC)])
                            e.mm(bnk[hr0:hr0 + 64, bC, 256:320], kh[c][:], vnw[c][:], True, True,
                                 [('gd_kh', c), ('gd_vn', c)], [self.bk(bC)])
                        else:
                            e.tt('dve', o_tm[d][:, ti, hr0:hr0 + 64], bnk[:, bC, 192:256], qs[c][:], ALU.add,
                                 [('gd_qs', c)], [('gd_o', d), self.bk(bC)])
                            e.stt(Sst[c][hr0:hr0 + 64, :], Sst[c][hr0:hr0 + 64, :], eglast[hr0:hr0 + 64, ti, col:col + 1],
                                  bnk[hr0:hr0 + 64, bC, 256:320], ALU.mult, ALU.add, [kS, 'gd_eglast'], [kS, self.bk(bC)])

                    def run_wave(preps, seqs, ratio=4):
                        active = list(preps)
                        sq = list(seqs)
                        k = 0
                        while active or sq:
                            nxt_active = []
                            for gnr in active:
                                try:
                                    next(gnr)
                                    nxt_active.append(gnr)
                                except StopIteration:
                                    pass
                            active = nxt_active
                            k += 1
                            if sq and (k % ratio == 0 or not active):
                                nsq = []
                                for gnr in sq:
                                    try:
                                        next(gnr)
                                        nsq.append(gnr)
                                    except StopIteration:
                                        pass
                                sq = nsq

                    def seq_gen(c, ti, slot):
                        for stage in range(6):
                            seq_stage(stage, c, ti, slot)
                            yield

                    run_wave([prep(c, orders[c[0]][0], 0) for c in chains], [])
                    for si in range(NT):
                        preps = []
                        if si + 1 < NT:
                            preps = [prep(c, orders[c[0]][si + 1], (si + 1) % R) for c in chains]
                        seqs = [seq_gen(c, orders[c[0]][si], si % R) for c in chains]
                        run_wave(preps, seqs)
                    if self.debug:
                        self.dump("gd_o0_%d_%d" % (l, g), o_tm[0][:], [128, NT, 128], [('gd_o', 0)])
                        self.dump("gd_o1_%d_%d" % (l, g), o_tm[1][:], [128, NT, 128], [('gd_o', 1)])
                    nw = sbt("gd_nw", [128, 128])
                    ss = sbt("gd_ss", [128, NT * 2])
                    yb = [sbt("gd_yb%d" % i, [128, 512], BF16) for i in range(2)]
                    B.dma('sp', nw[:], self.gnw[l], writes=['gd_nw'])
                    O = o_tm[0]
                    O1 = o_tm[1]
                    Of = O[:].rearrange("p t c -> p (t c)")
                    O1f = O1[:].rearrange("p t c -> p (t c)")
                    e.tt('pool', Of, Of, O1f, ALU.add, [('gd_o', 0), ('gd_o', 1)], [('gd_o', 0)])
                    e.tt('dve', O1f, Of, Of, ALU.mult, [('gd_o', 0)], [('gd_o', 1)])
                    e.red(ss[:], O1[:].rearrange("p t (h c) -> p (t h) c", c=64), ALU.add, [('gd_o', 1)], ['gd_ss'])
                    e.ts('dve', ss[:], ss[:], 1.0 / 64.0, NORM_EPS, ALU.mult, ALU.add, ['gd_ss'], ['gd_ss'])
                    e.act(ss[:], ss[:], AF.Sqrt, ['gd_ss'], ['gd_ss'])
                    e.recip(ss[:], ss[:], ['gd_ss'], ['gd_ss'])
                    O3 = O[:].rearrange("p t (h c) -> p (t h) c", c=64)
                    e.tt('dve', O3, O3, ss[:].unsqueeze(2).to_broadcast([128, NT * 2, 64]), ALU.mult,
                         [('gd_o', 0), 'gd_ss'], [('gd_o', 0)])
                    e.tt('pool', O[:], O[:], nw[:].unsqueeze(1).to_broadcast([128, NT, 128]), ALU.mult,
                         [('gd_o', 0), 'gd_nw'], [('gd_o', 0)])
                    e.tt('dve', O[:], O[:], zg[:], ALU.mult, [('gd_o', 0), 'gd_zg'], [('gd_o', 0)])
                    groups = [[0, 1]] + [list(range(2 + 4 * i, 6 + 4 * i)) for i in range(8)]
                    for gi, tl in enumerate(groups):
                        pb = gi % 2
                        bO = gi % 4
                        for k, ti in enumerate(tl):
                            e.tr(bnk[:, bO, k * 128:(k + 1) * 128], O[:, ti, :], ident, [('gd_o', 0), 'cst'], [self.bk(bO)])
                        n = 128 * len(tl)
                        e.cp('act' if gi % 2 else 'dve', yb[pb][:, 0:n], bnk[:, bO, 0:n], [], [('gd_yb', pb), self.bk(bO)])
                        B.dma('pool', yc[row0:row0 + 128, tl[0] * 128:tl[0] * 128 + n], yb[pb][:, 0:n],
                              reads=[('gd_yb', pb)], writes=['YC'])
                    B.barrier()

    def back_phase(self, l, x_cur, yc, x_next, last):
        B, e, nc = self.B, self.e, self.nc
        bnk = self.banks
        with ExitStack() as ph:
            def sbt(name, shape, dt=F32):
                return ph.enter_context(_sbt(nc, name, shape, dt))
            wo = sbt("bk_wo", [128, KC, D], BF16)
            stg = [sbt("bk_stg%d" % i, [128, D]) for i in range(2)]
            ycb = [sbt("bk_yc%d" % i, [128, KC, 128], BF16) for i in range(2)]
            xt = [sbt("bk_xt%d" % i, [128, D]) for i in range(2)]
            tb = [sbt("bk_tb%d" % i, [128, D]) for i in range(2)]
            zb = [sbt("bk_zb%d" % i, [128, D]) for i in range(2)]
            lng = sbt("bk_lng", [128, D])
            lnb = sbt("bk_lnb", [128, D])
            st6 = [sbt("bk_st6%d" % i, [128, 2, 6]) for i in range(2)]
            mv = [sbt("bk_mv%d" % i, [128, 4]) for i in range(2)]
            B.dma('sp', lng[:], self.lng_rep[l], writes=['bk_lng'])
            B.dma('sp', lnb[:], self.lnb_rep[l], writes=['bk_lnb'])
            for kc in range(KC):
                b = kc % 2
                B.dma('sp', stg[b][:], self.w_out[l, kc * 128:(kc + 1) * 128, :], writes=[('bk_stg', b)])
                e.cp('pool', wo[:, kc, :], stg[b][:], [('bk_stg', b)], ['bk_wo'])
            ycv = yc.rearrange("(c p) t -> p c t", p=128)
            tiles = list(range(2, NT)) if last else list(range(NT))
            for n_, ti in enumerate(tiles):
                j = 1 if ti < 2 else 0
                b = n_ % 2
                b0 = 2 * b
                B.dma('sp', ycb[b][:], ycv[:, :, ti * 128:(ti + 1) * 128], reads=['YC'], writes=[('bk_yc', b)])
                B.dma('sp', xt[b][:], x_cur[ti * 128:(ti + 1) * 128, :], reads=['X'], writes=[('bk_xt', b)])
                for nb in range(2):
                    for c in range(KC):
                        e.mm(bnk[:, b0 + nb, :], ycb[b][:, c, :], wo[:, c, nb * 512:(nb + 1) * 512], c == 0, c == KC - 1,
                             [('bk_yc', b), 'bk_wo'], [self.bk(b0 + nb)])
                for nb in range(2):
                    e.tt('dve', tb[b][:, nb * 512:(nb + 1) * 512], bnk[:, b0 + nb, :],
                         self.gate_bc[:, j, nb * 512:(nb + 1) * 512], ALU.mult, ['gate_bc'],
                         [('bk_tb', b), self.bk(b0 + nb)])
                e.stt(zb[b][:], xt[b][:], ALPHA, tb[b][:], ALU.mult, ALU.add, [('bk_xt', b), ('bk_tb', b)], [('bk_zb', b)])
                for nb in range(2):
                    B.op('dve', (lambda o_, i_: (lambda en: en.bn_stats(out=o_, in_=i_)))(
                        st6[b][:, nb, :], zb[b][:, nb * 512:(nb + 1) * 512]), [('bk_zb', b)], [('bk_st6', b)])
                B.op('dve', (lambda o_, i_: (lambda en: en.bn_aggr(out=o_, in_=i_)))(
                    mv[b][:, 0:2], st6[b][:].rearrange("p a s -> p (a s)")), [('bk_st6', b)], [('bk_mv', b)])
                e.act(mv[b][:, 2:3], mv[b][:, 1:2], AF.Sqrt, [('bk_mv', b)], [('bk_sd', b)], bias=LN_EPS)
                e.recip(mv[b][:, 3:4], mv[b][:, 2:3], [('bk_sd', b)], [('bk_rs', b)])
                e.ts('dve', zb[b][:], zb[b][:], mv[b][:, 0:1], mv[b][:, 3:4], ALU.subtract, ALU.mult,
                     [('bk_zb', b), ('bk_mv', b), ('bk_rs', b)], [('bk_zb', b)])
                e.tt('pool', zb[b][:], zb[b][:], lng[:], ALU.mult, [('bk_zb', b), 'bk_lng'], [('bk_zb', b)])
                e.tt('pool', zb[b][:], zb[b][:], lnb[:], ALU.add, [('bk_zb', b), 'bk_lnb'], [('bk_zb', b)])
                if last:
                    dst = x_next[(ti - 2) * 128:(ti - 1) * 128, :]
                else:
                    dst = x_next[ti * 128:(ti + 1) * 128, :]
                B.dma('pool', dst, zb[b][:], reads=[('bk_zb', b)], writes=['Xn'])
            B.barrier()
            B.last_w['X'] = B.last_w.get('Xn')


def group_cols(g):
    R0 = 1152
    rgx = np.arange(192 * g, 192 * g + 192)
    gq = 384 + np.arange(128 * g, 128 * g + 128)
    gk = 640 + np.arange(128 * g, 128 * g + 128)
    gv = 896 + np.arange(128 * g, 128 * g + 128)
    rgg = R0 + np.arange(192 * g, 192 * g + 192)
    naq = R0 + 384 + np.arange(192 * g, 192 * g + 192)
    nak = R0 + 768 + np.arange(192 * g, 192 * g + 192)
    nav = R0 + 1152 + np.arange(192 * g, 192 * g + 192)
    nag = R0 + 1536 + np.arange(192 * g, 192 * g + 192)
    gg = R0 + 1920 + np.arange(128 * g, 128 * g + 128)
    bf = R0 + 2176 + np.arange(2 * g, 2 * g + 2)
    bb = R0 + 2180 + np.arange(2 * g, 2 * g + 2)
    af = R0 + 2184 + np.arange(2 * g, 2 * g + 2)
    ab = R0 + 2188 + np.arange(2 * g, 2 * g + 2)
    cols = np.concatenate([rgx, rgg, naq, nak, nag, nav, gq, gk, gv, gg, bf, bb, af, ab])
    assert cols.shape[0] == GCOLS
    conv_ch = [rgx[:128], rgx[128:], gq, gk, gv]
    return cols, conv_ch


def make_consts():
    c = np.zeros((128, NCONST), np.float32)
    i = np.arange(128)
    c[:, C_ID:C_ID + 128] = np.eye(128)
    c[:, C_ONE:C_ONE + 128] = 1.0
    c[:64, C_BONE:C_BONE + 64] = 1.0
    c[64:, C_BONE + 64:C_BONE + 128] = 1.0
    R = np.zeros((128, 128), np.float32)
    for h in range(2):
        for half in range(2):
            o = 64 * h + 32 * half
            for t in range(16):
                R[o + t, o + t + 16] = -1.0
                R[o + t + 16, o + t] = 1.0
    c[:, C_RM:C_RM + 128] = R.T
    k = i[:, None]
    m = i[None, :]
    c[:, C_UF:C_UF + 128] = (k <= m)
    c[:, C_UB:C_UB + 128] = (k >= m)
    c[:, C_SF:C_SF + 128] = (m > k)
    c[:, C_SB:C_SB + 128] = (m < k)
    c[:, C_MF:C_MF + 128] = np.where(m >= k, 0.0, -BIG)
    c[:, C_MB:C_MB + 128] = np.where(m <= k, 0.0, -BIG)
    return c


def make_rope():
    p = np.arange(128)
    d = p % 64
    half = d // 32
    fi = d % 16
    inv_freq = (np.float32(10000.0) ** (-np.arange(16, dtype=np.float32) / np.float32(16))).astype(np.float32)
    t = np.arange(4096)
    row = (t // 64).astype(np.float32)
    col = (t % 64).astype(np.float32)
    pos = np.where(half[:, None] == 0, row[None, :], col[None, :]).astype(np.float32)
    ang = (pos * inv_freq[fi][:, None]).astype(np.float32)
    return np.cos(ang).astype(np.float32), np.sin(ang).astype(np.float32)


NA_CLASSES = [(0, 0), (1, 0), (2, 0), (30, 27), (31, 27)]


def make_nab(rpb_h):
    out = np.full((128, 5, 640), -BIG, np.float32)
    q = np.arange(128)
    key = np.arange(640)
    for ci, (qt, kt0) in enumerate(NA_CLASSES):
        r = 2 * qt + q // 64
        j = q % 64
        kr = 2 * kt0 + key // 64
        kcn = key % 64
        r0 = np.clip(r - 4, 0, 56)
        c0 = np.clip(j - 8, 0, 48)
        okr = (kr[None, :] >= r0[:, None]) & (kr[None, :] < r0[:, None] + 8)
        okc = (kcn[None, :] >= c0[:, None]) & (kcn[None, :] < c0[:, None] + 16)
        ro = np.clip(kr[None, :] - r[:, None] + 7, 0, 14)
        co = np.clip(kcn[None, :] - j[:, None] + 15, 0, 30)
        vals = rpb_h[ro, co]
        out[:, ci, :] = np.where(okr & okc, vals, np.float32(-BIG))
    return out


def prep_shared(inp, G):
    L = DEPTH
    sh = {}
    sh['consts'] = make_consts()
    cs, sn = make_rope()
    sh['rope_cos'], sh['rope_sin'] = cs, sn
    sh['w_mod'] = np.ascontiguousarray(inp['w_mod'])
    sh['bmod_rep'] = np.ascontiguousarray(np.broadcast_to(inp['b_mod'][:, None, :], (L, 128, 3 * D)))
    sh['gnw'] = np.ascontiguousarray(np.broadcast_to(np.tile(inp['gdn_nw'], (1, 2))[:, None, :], (L, 128, 128)))
    sh['lng_rep'] = np.ascontiguousarray(np.broadcast_to(inp['ln_g'][:, None, :], (L, 128, D)))
    sh['lnb_rep'] = np.ascontiguousarray(np.broadcast_to(inp['ln_b'][:, None, :], (L, 128, D)))
    rows = []
    for g in range(2):
        rows += list(range(192 * g, 192 * g + 192))
        rows += list(range(384 + 192 * g, 384 + 192 * g + 192))
        rows += list(range(768 + 128 * g, 768 + 128 * g + 128))
    sh['w_out'] = np.ascontiguousarray(inp['w_out'][:, np.array(rows), :])
    return sh


def prep_group(inp, g):
    L = DEPTH
    cols, conv_ch = group_cols(g)
    o = {}
    o['w_in'] = inp['w_in'][:, :, cols]
    cw = np.zeros((L, 128, 5, 4), np.float32)
    for ti, ch in enumerate(conv_ch):
        cw[:, :len(ch), ti, :] = np.transpose(inp['conv_w'][:, :, ch], (0, 2, 1))
    o['convw'] = cw
    rgw = np.zeros((L, 128, 2, 2, 192), np.float32)
    rgb = np.zeros((L, 128, 2, 2, 3), np.float32)
    for d in range(2):
        for ai, (wn, bn) in enumerate([('rg_wa', 'rg_ba'), ('rg_wx', 'rg_bx')]):
            W = inp[wn][:, d]
            rgw[:, 0:64, d, ai, 0:64] = W[:, 3 * g]
            rgw[:, 64:128, d, ai, 64:128] = W[:, 3 * g + 1]
            rgw[:, 0:64, d, ai, 128:192] = W[:, 3 * g + 2]
            bvec = inp[bn][:, d, 192 * g:192 * g + 192]
            rgb[:, :, 0, d, ai] = bvec[:, 0:128]
            rgb[:, 0:64, 1, d, ai] = bvec[:, 128:192]
        lam = inp['rg_lam'][:, d, 192 * g:192 * g + 192]
        rgb[:, :, 0, d, 2] = lam[:, 0:128]
        rgb[:, 0:64, 1, d, 2] = lam[:, 128:192]
    o['rgw'] = rgw
    o['rgb'] = rgb
    nab = np.zeros((L, 3, 128, 5, 640), np.float32)
    for l in range(L):
        for h in range(3):
            nab[l, h] = make_nab(inp['na_rpb'][l, 3 * g + h])
    o['nab'] = nab
    al = np.zeros((L, 128, NT, 4), np.float32)
    dt = np.zeros((L, 128, NT, 4), np.float32)
    for d in range(2):
        for h in range(2):
            al[:, :, :, 2 * d + h] = inp['gdn_alog'][:, d, 2 * g + h][:, None, None]
            dt[:, :, :, 2 * d + h] = inp['gdn_dtb'][:, d, 2 * g + h][:, None, None]
    o['galog'] = al
    o['gdtb'] = dt
    return o


def core_inputs(inp, sh, groups, b, x_rows):
    m = dict(sh)
    gs = [prep_group(inp, g) for g in groups]
    m['w_in'] = np.ascontiguousarray(np.concatenate([q['w_in'] for q in gs], axis=2))
    for k in ['convw', 'rgw', 'rgb', 'nab', 'galog', 'gdtb']:
        m[k] = np.ascontiguousarray(np.stack([q[k] for q in gs], axis=1))
    m['x_in'] = np.ascontiguousarray(x_rows)
    cc = np.stack([inp['c'][b], inp['c_ctx']], axis=-1)
    m['cc'] = np.ascontiguousarray(cc.reshape(KC, 128, 2).transpose(1, 0, 2))
    return m


MODE = 'B'
_PROG_CACHE = {}


def _prog(key, **kw):
    if key not in _PROG_CACHE:
        _PROG_CACHE[key] = Prog(**kw)
    return _PROG_CACHE[key]


def kernel_unfused(inp):
    nb = inp['x'].shape[0]
    sh = prep_shared(inp, 1)
    ncore = 2 * nb
    x_rows = [np.concatenate([inp['ctx'][b], inp['x'][b]], 0) for b in range(nb)]
    base = [core_inputs(inp, sh, [c % 2], c // 2, x_rows[c // 2]) for c in range(ncore)]
    yc_full = None
    out = None
    for k in range(DEPTH + 1):
        if k == 0:
            P = _prog(('A', 0), G=1, steps=[('front', 0)], x_ext_out=True, final=False)
        elif k < DEPTH:
            P = _prog(('A', k), G=1, steps=[('back', k - 1), ('front', k)], x_ext_out=True, final=False)
        else:
            P = _prog(('A', k), G=1, steps=[('back', DEPTH - 1)], x_ext_out=True, final=True)
        maps = []
        for c in range(ncore):
            m = dict(base[c])
            m['x_in'] = x_rows[c // 2]
            if k > 0:
                m['yc_in'] = yc_full[c // 2]
            maps.append(m)
        res = run_bass_kernel_spmd(P.nc, maps, core_ids=list(range(ncore)))
        rs = res.results
        if k < DEPTH:
            yc_full = [np.ascontiguousarray(np.concatenate([np.asarray(rs[2 * b]['yc_out']),
                                                            np.asarray(rs[2 * b + 1]['yc_out'])], 0))
                       for b in range(nb)]
        if 0 < k < DEPTH:
            x_rows = [np.asarray(rs[2 * b]['x_out']) for b in range(nb)]
        if k == DEPTH:
            out = np.stack([np.asarray(rs[2 * b]['out']) for b in range(nb)], 0)
    return out.astype(np.float32)


def kernel_fused(inp):
    nb = inp['x'].shape[0]
    sh = prep_shared(inp, 2)
    steps = []
    for l in range(DEPTH):
        steps += [('front', l), ('back', l)]
    P = _prog(('B',), G=2, steps=steps, x_ext_out=False, final=True)
    maps = []
    for c in range(8):
        b = c % nb
        x_rows = np.concatenate([inp['ctx'][b], inp['x'][b]], 0)
        maps.append(core_inputs(inp, sh, [0, 1], b, x_rows))
    res = run_bass_kernel_spmd(P.nc, maps, core_ids=list(range(8)))
    out = np.stack([np.asarray(res.results[b]['out']) for b in range(nb)], 0)
    return out.astype(np.float32)


def kernel(**inputs):
    inp = {k: np.asarray(v) for k, v in inputs.items()}
    if MODE == 'A':
        return kernel_unfused(inp)
    return kernel_fused(inp)
```

```python
from contextlib import ExitStack
import numpy as np
import concourse.bass as bass
import concourse.mybir as mybir
from concourse.bass_utils import run_bass_kernel_spmd

F32 = mybir.dt.float32
BF16 = mybir.dt.bfloat16
AF = mybir.ActivationFunctionType
ALU = mybir.AluOpType
AX = mybir.AxisListType

ENGS = ['pe', 'act', 'dve', 'pool', 'sp']
N_DMA_SEMS = 24


_UNIQ = [0]


def _sbt(nc, name, shape, dt):
    _UNIQ[0] += 1
    return nc.sbuf_tensor("%s_u%d" % (name, _UNIQ[0]), list(shape), dt)


class Builder:
    def __init__(self, nc):
        self.nc = nc
        self.stack = ExitStack()
        self.ops = {e: [] for e in ENGS}
        self.sem = {}
        self.cnt = {}
        self.waited = {e: {} for e in ENGS}
        self.last_w = {}
        self.readers = {}
        for e in ENGS:
            self.sem[e] = self.stack.enter_context(nc.semaphore('sem_' + e))
            self.cnt[e] = 0
        for k in range(N_DMA_SEMS):
            key = ('dma', k)
            self.sem[key] = self.stack.enter_context(nc.semaphore('sem_dma%d' % k))
            self.cnt[key] = 0
        self.dma_rr = 0
        self.n_ops = 0
        self.pending = {e: [] for e in ENGS}

    def sb(self, name, shape, dtype):
        return self.stack.enter_context(_sbt(self.nc, name, list(shape), dtype))

    def ps(self, name, shape, dtype):
        return self.stack.enter_context(self.nc.psum_tensor(name, list(shape), dtype))

    def _deps(self, eng, reads, writes):
        toks = []
        for r in reads:
            t = self.last_w.get(r)
            if t is not None:
                toks.append((t, False))
        for w in writes:
            t = self.last_w.get(w)
            if t is not None:
                toks.append((t, False))
            for sk, (val, peng) in self.readers.get(w, {}).items():
                toks.append(((sk, val, peng), True))
        waits = []
        for (sk, val, peng), is_war in toks:
            if peng == eng:
                if eng == 'pe' or is_war:
                    continue
            if self.waited[eng].get(sk, 0) >= val:
                continue
            self.waited[eng][sk] = val
            waits.append((sk, val))
        best = {}
        for sk, val in waits:
            best[sk] = max(best.get(sk, 0), val)
        return list(best.items())

    def _commit(self, tok, reads, writes):
        for w in writes:
            self.last_w[w] = tok
            self.readers[w] = {}
        for r in reads:
            d = self.readers.setdefault(r, {})
            sk, val, peng = tok
            if d.get(sk, (0, None))[0] < val:
                d[sk] = (val, peng)

    def barrier(self):
        for e in ENGS:
            for k, v in self.cnt.items():
                if v > 0 and self.waited[e].get(k, 0) < v:
                    self.waited[e][k] = v
                    self.pending[e].append((k, v))

    def _take_pending(self, eng, waits):
        if self.pending[eng]:
            best = dict(waits)
            for k, v in self.pending[eng]:
                best[k] = max(best.get(k, 0), v)
            self.pending[eng] = []
            return list(best.items())
        return waits

    def op(self, eng, fn, reads=(), writes=()):
        waits = self._take_pending(eng, self._deps(eng, reads, writes))
        self.cnt[eng] += 1
        tok = (eng, self.cnt[eng], eng)
        self.ops[eng].append((waits, fn, eng, 1))
        self._commit(tok, reads, writes)
        self.n_ops += 1

    def dma(self, q, out, in_, reads=(), writes=(), fn=None, **kw):
        k = self.dma_rr
        self.dma_rr = (self.dma_rr + 1) % N_DMA_SEMS
        key = ('dma', k)
        waits = self._deps(q, reads, writes)
        if self.cnt[key] > 0 and self.waited[q].get(key, 0) < self.cnt[key]:
            self.waited[q][key] = self.cnt[key]
            waits = [w for w in waits if w[0] != key] + [(key, self.cnt[key])]
        waits = self._take_pending(q, waits)
        self.cnt[key] += 16
        tok = (key, self.cnt[key], 'dma')
        if fn is None:
            fn = lambda e: e.dma_start(out=out, in_=in_, **kw)
        self.ops[q].append((waits, fn, key, 16))
        self._commit(tok, reads, writes)
        self.n_ops += 1

    def finish(self):
        nc = self.nc
        fin = [(k, v) for k, v in self.cnt.items() if v > 0]
        ops = self.ops
        sem = self.sem

        def replay(name):
            def run(e):
                for waits, fn, sk, inc in ops[name]:
                    for wk, wv in waits:
                        e.wait_ge(sem[wk], wv)
                    ins = fn(e)
                    ins.then_inc(sem[sk], inc)
                if name == 'sp':
                    for k, v in fin:
                        e.wait_ge(sem[k], v)
            return run

        with nc.Block() as block:
            block.tensor(replay('pe'))
            block.scalar(replay('act'))
            block.vector(replay('dve'))
            block.gpsimd(replay('pool'))
            block.sync(replay('sp'))
        self.stack.close()


DEPTH = 4
D = 1024
KC = 8
T = 4352
NT = 34
GCOLS = 1672
ALPHA = (2.0 * DEPTH) ** 0.25
LN_EPS = 1e-5
NORM_EPS = 1e-6
BIG = 30000.0
BLOCKS = [(0, 256)] + [(256 + 512 * i, 512) for i in range(8)]
PADW = 4360
C_ID, C_ONE, C_BONE, C_RM, C_UF, C_UB, C_SF, C_SB, C_MF, C_MB = [128 * i for i in range(10)]
NCONST = 1280
ONLY = {'rg', 'na', 'gdn'}


def pcol(t):
    return t + 2 if t < 256 else t + 6


class E:
    def __init__(self, B):
        self.B = B

    def mm(self, out, lhsT, rhs, st, sp, r, w):
        self.B.op('pe', lambda e: e.matmul(out, lhsT=lhsT, rhs=rhs, start=st, stop=sp), r, w)

    def tr(self, out, in_, ident, r, w):
        self.B.op('pe', lambda e: e.transpose(out=out, in_=in_, identity=ident), r, w)

    def act(self, out, in_, func, r, w, bias=None, scale=None, accum=None):
        kw = {}
        if bias is not None:
            kw['bias'] = bias
        if scale is not None:
            kw['scale'] = scale
        if accum is not None:
            kw['accum_out'] = accum
        self.B.op('act', lambda e: e.activation(out=out, in_=in_, func=func, **kw), r, w)

    def tt(self, eng, out, in0, in1, op, r, w):
        self.B.op(eng, lambda e: e.tensor_tensor(out=out, in0=in0, in1=in1, op=op), r, w)

    def ts(self, eng, out, in0, s1, s2, op0, op1, r, w):
        if op1 is None:
            self.B.op(eng, lambda e: e.tensor_scalar(out=out, in0=in0, scalar1=s1, scalar2=None, op0=op0), r, w)
        else:
            self.B.op(eng, lambda e: e.tensor_scalar(out=out, in0=in0, scalar1=s1, scalar2=s2, op0=op0, op1=op1), r, w)

    def stt(self, out, in0, sc, in1, op0, op1, r, w):
        self.B.op('dve', lambda e: e.scalar_tensor_tensor(out=out, in0=in0, scalar=sc, in1=in1, op0=op0, op1=op1), r, w)

    def cp(self, eng, out, in_, r, w):
        if eng == 'act':
            self.B.op('act', lambda e: e.activation(out=out, in_=in_, func=AF.Copy), r, w)
        else:
            self.B.op(eng, lambda e: e.tensor_copy(out=out, in_=in_), r, w)

    def red(self, out, in_, op, r, w, negate=False):
        self.B.op('dve', lambda e: e.tensor_reduce(out=out, in_=in_, axis=AX.X, op=op, negate=negate), r, w)

    def recip(self, out, in_, r, w):
        self.B.op('dve', lambda e: e.reciprocal(out=out, in_=in_), r, w)

    def memset(self, eng, ap, val, w):
        self.B.op(eng, lambda e: e.memset(ap, val), (), w)

    def scan(self, out, d0, d1, init, r, w):
        self.B.op('dve', lambda e: e.tensor_tensor_scan(out=out, data0=d0, data1=d1, initial=init,
                                                        op0=ALU.mult, op1=ALU.add), r, w)


def rev(ap):
    return ap[:, ::-1]


class Prog:
    def __init__(self, G, steps, x_ext_out, final, debug=False):
        self.G = G
        self.steps = steps
        self.final = final
        nc = self.nc = bass.Bass("TRN2", target_bir_lowering=False)
        self.B = B = Builder(nc)
        self.e = E(B)
        self.debug = debug
        self.dbg_outs = {}
        L = DEPTH

        def din(name, shape, dt=F32):
            return nc.dram_tensor(name, list(shape), dt, kind="ExternalInput").ap()

        def dout(name, shape, dt=F32):
            return nc.dram_tensor(name, list(shape), dt, kind="ExternalOutput").ap()

        def dint(name, shape, dt=F32):
            return nc.dram_tensor(name, list(shape), dt, kind="Internal").ap()

        self.dout = dout
        self.x_in = din("x_in", [T, D])
        self.cc = din("cc", [128, KC, 2])
        self.consts_d = din("consts", [128, NCONST])
        self.w_mod = din("w_mod", [L, D, 3 * D])
        self.bmod_rep = din("bmod_rep", [L, 128, 3 * D])
        self.w_in = din("w_in", [L, D, G * GCOLS])
        self.convw = din("convw", [L, G, 128, 5, 4])
        self.rgw = din("rgw", [L, G, 128, 2, 2, 192])
        self.rgb = din("rgb", [L, G, 128, 2, 2, 3])
        self.nab = din("nab", [L, G, 3, 128, 5, 640])
        self.galog = din("galog", [L, G, 128, NT, 4])
        self.gdtb = din("gdtb", [L, G, 128, NT, 4])
        self.gnw = din("gnw", [L, 128, 128])
        self.rope_cos = din("rope_cos", [128, 4096])
        self.rope_sin = din("rope_sin", [128, 4096])
        self.w_out = din("w_out", [L, D, D])
        self.lng_rep = din("lng_rep", [L, 128, D])
        self.lnb_rep = din("lnb_rep", [L, 128, D])
        has_front = any(s[0] == 'front' for s in steps)
        has_back = any(s[0] == 'back' for s in steps)
        self.uT_d = dint("uT_d", [128, KC, T], BF16)
        if steps[0][0] == 'back':
            self.yc_in = din("yc_in", [D, T], BF16)
        else:
            self.yc_in = None
        if has_front and x_ext_out:
            self.yc_out = dout("yc_out", [512 * G, T], BF16)
        else:
            self.yc_out = None
        self.yc_int = dint("yc_int", [D, T], BF16) if not x_ext_out else None
        self.x_ext_out = x_ext_out
        if x_ext_out and has_back and not final:
            self.x_out = dout("x_out", [T, D])
        else:
            self.x_out = None
        self.x_scr = [dint("x_scr0", [T, D]), dint("x_scr1", [T, D])] if not x_ext_out else None
        self.out_d = dout("out", [4096, D]) if final else None

        self.banks = B.ps("banks", [128, 8, 512], F32)
        self.cst = B.sb("cst", [128, NCONST], F32)
        self.identb = B.sb("identb", [128, 128], BF16)
        self.screp = B.sb("screp", [128, 2, KC, 128], F32)
        self.modcol = B.sb("modcol", [128, 16, 2], F32)
        self.gate_bc = B.sb("gate_bc", [128, 2, D], F32)
        self.build()

    def c(self, off, n=128, p0=0, p1=128):
        return self.cst[p0:p1, off:off + n]

    def bk(self, i):
        return ('bk', i)

    def dump(self, name, ap, shape, reads, dt=F32):
        if not self.debug:
            return
        d = self.nc.dram_tensor("dbg_" + name, list(shape), dt, kind="ExternalOutput").ap()
        self.dbg_outs[name] = d
        self.B.dma('pool', d, ap, reads=reads, writes=[('dbg', name)])

    def build(self):
        B, e = self.B, self.e
        B.dma('sp', self.cst[:], self.consts_d, writes=['cst'])
        e.cp('dve', self.identb[:], self.c(C_ID), ['cst'], ['identb'])
        with ExitStack() as ph:
            cct = ph.enter_context(_sbt(self.nc, "cct", [128, KC, 2], F32))
            sct = ph.enter_context(_sbt(self.nc, "sct", [128, KC, 2], F32))
            B.dma('sp', cct[:], self.cc, writes=['cct'])
            e.act(sct[:], cct[:], AF.Silu, ['cct'], ['sct'])
            for j in range(2):
                for kc in range(KC):
                    e.ts('dve', self.screp[:, j, kc, :], self.c(C_ONE), sct[:, kc, j:j + 1], None, ALU.mult, None,
                         ['cst', 'sct'], ['screp'])
            B.barrier()
        x_cur = self.x_in
        nxt = 0
        for kind, l in self.steps:
            last = (l == DEPTH - 1)
            if kind == 'front':
                yc = self.yc_out if self.x_ext_out else self.yc_int
                self.mod_phase(l)
                self.u_phase(l, x_cur)
                for g in range(self.G):
                    row0 = 512 * g
                    if 'rg' in ONLY:
                        self.rg_pass(l, g, yc, row0)
                    if 'na' in ONLY:
                        self.na_pass(l, g, yc, row0 + 192, ctx_out=not last)
                    if 'gdn' in ONLY:
                        self.gdn_pass(l, g, yc, row0 + 384)
            else:
                if self.yc_in is not None and (kind, l) == self.steps[0]:
                    yc = self.yc_in
                    self.mod_phase(l, gate_only=True)
                else:
                    yc = self.yc_int
                if last:
                    self.back_phase(l, x_cur, yc, self.out_d, True)
                else:
                    if self.x_ext_out:
                        xn = self.x_out
                    else:
                        xn = self.x_scr[nxt]
                        nxt ^= 1
                    self.back_phase(l, x_cur, yc, xn, False)
                    x_cur = xn
        B.finish()

    def mod_phase(self, l, gate_only=False):
        B, e, nc = self.B, self.e, self.nc
        with ExitStack() as ph:
            wm = [ph.enter_context(_sbt(nc, "wm%d" % i, [128, 3 * D], F32)) for i in range(2)]
            modbc = ph.enter_context(_sbt(nc, "modbc", [128, 3 * D], F32))
            bmod = ph.enter_context(_sbt(nc, "bmod", [128, 3 * D], F32))
            junk = ph.enter_context(_sbt(nc, "junk", [128, 128], F32))
            B.dma('sp', bmod[:], self.bmod_rep[l], writes=['bmod'])
            for j in range(2):
                for kc in range(KC):
                    b = kc % 2
                    B.dma('sp', wm[b][:], self.w_mod[l, kc * 128:(kc + 1) * 128, :], writes=[('wm', b)])
                    for nb in range(6):
                        e.mm(self.banks[:, nb, :], self.screp[:, j, kc, :], wm[b][:, nb * 512:(nb + 1) * 512],
                             kc == 0, kc == KC - 1, [('wm', b), 'screp'], [self.bk(nb)])
                for nb in range(6):
                    e.tt('dve', modbc[:, nb * 512:(nb + 1) * 512], self.banks[:, nb, :],
                         bmod[:, nb * 512:(nb + 1) * 512], ALU.add, ['bmod'], ['modbc', self.bk(nb)])
                if not gate_only:
                    for ch in range(16):
                        e.tt('dve', junk[:], modbc[:, ch * 128:(ch + 1) * 128], self.c(C_ID), ALU.mult,
                             ['modbc', 'cst'], ['junk'])
                        e.red(self.modcol[:, ch, j:j + 1], junk[:], ALU.add, ['junk'], ['modcol'])
                e.cp('pool', self.gate_bc[:, j, :], modbc[:, 2 * D:3 * D], ['modbc'], ['gate_bc'])
            if not gate_only:
                e.ts('dve', self.modcol[:, 8:16, :], self.modcol[:, 8:16, :], 1.0, None, ALU.add, None,
                     ['modcol'], ['modcol'])
            B.barrier()

    def u_phase(self, l, x_cur):
        B, e, nc = self.B, self.e, self.nc
        with ExitStack() as ph:
            xt = [ph.enter_context(_sbt(nc, "u_xt%d" % i, [128, D], F32)) for i in range(2)]
            ut = [ph.enter_context(_sbt(nc, "u_ut%d" % i, [128, KC, 128], BF16)) for i in range(2)]
            for ti in range(NT):
                j = 1 if ti < 2 else 0
                b = ti % 2
                B.dma('sp', xt[b][:], x_cur[ti * 128:(ti + 1) * 128, :], reads=['X'], writes=[('u_xt', b)])
                for kc in range(KC):
                    bank = 2 * b + kc // 4
                    e.tr(self.banks[:, bank, (kc % 4) * 128:(kc % 4 + 1) * 128], xt[b][:, kc * 128:(kc + 1) * 128],
                         self.c(C_ID), [('u_xt', b), 'cst'], [self.bk(bank)])
                for kc in range(KC):
                    bank = 2 * b + kc // 4
                    src = self.banks[:, bank, (kc % 4) * 128:(kc % 4 + 1) * 128]
                    if kc % 2 == 0:
                        e.ts('dve', ut[b][:, kc, :], src, self.modcol[:, 8 + kc, j:j + 1], self.modcol[:, kc, j:j + 1],
                             ALU.mult, ALU.add, ['modcol'], [('u_ut', b), self.bk(bank)])
                    else:
                        e.act(ut[b][:, kc, :], src, AF.Identity, ['modcol'], [('u_ut', b), self.bk(bank)],
                              bias=self.modcol[:, kc, j:j + 1], scale=self.modcol[:, 8 + kc, j:j + 1])
                B.dma('pool', self.uT_d[:, :, ti * 128:(ti + 1) * 128], ut[b][:], reads=[('u_ut', b)], writes=['uT'])
            B.barrier()

    def load_w(self, ph, l, g, col0, ncols, tag):
        B, e, nc = self.B, self.e, self.nc
        wsb = ph.enter_context(_sbt(nc, "wsb_" + tag, [128, KC, ncols], BF16))
        stg = [ph.enter_context(_sbt(nc, "wstg%d_%s" % (i, tag), [128, ncols], F32)) for i in range(2)]
        c0 = g * GCOLS + col0
        for kc in range(KC):
            b = kc % 2
            B.dma('sp', stg[b][:], self.w_in[l, kc * 128:(kc + 1) * 128, c0:c0 + ncols], writes=[('wstg', b)])
            e.cp('pool', wsb[:, kc, :], stg[b][:], [('wstg', b)], ['wsb'])
        return wsb

    def proj(self, ph, wsb, fm_tiles, tm_ranges, fm_sink, tm_sink):
        B, e, nc = self.B, self.e, self.nc
        ub = [ph.enter_context(_sbt(nc, "ub%d" % i, [128, KC, 512], BF16)) for i in range(2)]
        rr = 0
        for bi, (s, n) in enumerate(BLOCKS):
            b = bi % 2
            B.dma('sp', ub[b][:, :, 0:n], self.uT_d[:, :, s:s + n], reads=['uT'], writes=[('ub', b)])
            for idx, (c0, M) in enumerate(fm_tiles):
                bank = rr % 4
                rr += 1
                for kc in range(KC):
                    e.mm(self.banks[0:M, bank, 0:n], wsb[:, kc, c0:c0 + M], ub[b][:, kc, 0:n], kc == 0, kc == KC - 1,
                         ['wsb', ('ub', b)], [self.bk(bank)])
                fm_sink(idx, s, n, self.banks[0:M, bank, 0:n], self.bk(bank))
            for tl in range(n // 128):
                ti = s // 128 + tl
                for idx, (c0, W) in enumerate(tm_ranges):
                    bank = rr % 4
                    rr += 1
                    for kc in range(KC):
                        e.mm(self.banks[:, bank, 0:W], ub[b][:, kc, tl * 128:(tl + 1) * 128], wsb[:, kc, c0:c0 + W],
                             kc == 0, kc == KC - 1, ['wsb', ('ub', b)], [self.bk(bank)])
                    tm_sink(idx, ti, self.banks[:, bank, 0:W], self.bk(bank))

    def conv(self, P, out, cw, tile, np_, r, w):
        e = self.e
        for (s, n) in [(0, 256), (256, 4096)]:
            base = pcol(s) - 2
            o = out[0:np_, s:s + n]
            e.ts('dve', o, P[0:np_, base:base + n], cw[0:np_, tile, 0:1], None, ALU.mult, None, r, w)
            for j in range(1, 4):
                e.stt(o, P[0:np_, base + j:base + j + n], cw[0:np_, tile, j:j + 1], o, ALU.mult, ALU.add, r, w)

    def zero_pads(self, P, key):
        e = self.e
        for a, b in [(0, 2), (258, 262), (4358, 4360)]:
            e.memset('pool', P[:, a:b], 0.0, [key])

    def rg_pass(self, l, g, yc, row0):
        B, e, nc = self.B, self.e, self.nc
        with ExitStack() as p0:
            xa = [p0.enter_context(_sbt(nc, "rg_xa%d" % i, [128, T], F32)) for i in range(2)]
            zs = [p0.enter_context(_sbt(nc, "rg_zs%d" % i, [128, T], BF16)) for i in range(2)]
            cw = p0.enter_context(_sbt(nc, "rg_cw", [128, 5, 4], F32))
            B.dma('sp', cw[:], self.convw[l, g], writes=['cw'])
            NP = [128, 64]
            with ExitStack() as ph:
                wsb = self.load_w(ph, l, g, 0, 384, "rg")
                P = [ph.enter_context(_sbt(nc, "rg_P%d" % i, [128, PADW], F32)) for i in range(2)]
                for i in range(2):
                    self.zero_pads(P[i], ('rgP', i))

                def fm_sink(idx, s, n, ps, bkey):
                    if idx < 2:
                        e.cp('act', P[idx][0:NP[idx], pcol(s):pcol(s) + n], ps, [], [bkey, ('rgP', idx)])
                    else:
                        i = idx - 2
                        e.act(zs[i][0:NP[i], s:s + n], ps, AF.Silu, [], [bkey, ('rg_zs', i)])

                self.proj(ph, wsb, [(0, 128), (128, 64), (192, 128), (320, 64)], [], fm_sink, None)
                for i in range(2):
                    self.conv(P[i], xa[i], cw, i, NP[i], [('rgP', i), 'cw'], [('rg_xa', i)])
                B.barrier()
            if self.debug:
                self.dump("rg_xa0_%d_%d" % (l, g), xa[0][:], [128, T], [('rg_xa', 0)])
            with ExitStack() as ph:
                hh = [[ph.enter_context(_sbt(nc, "rg_h%d_%d" % (d, i), [128, T], F32)) for i in range(2)]
                      for d in range(2)]
                w = ph.enter_context(_sbt(nc, "rg_w", [128, 2, 2, 192], F32))
                bb = ph.enter_context(_sbt(nc, "rg_b", [128, 2, 2, 3], F32))
                c8 = ph.enter_context(_sbt(nc, "rg_c8", [128, 2, 2], F32))
                tmp = {}
                for d in range(2):
                    for i in range(2):
                        for nm in ['r', 'i', 's']:
                            tmp[(nm, d, i)] = ph.enter_context(_sbt(nc, "rg_t%s%d%d" % (nm, d, i), [128, 512], F32))
                B.dma('sp', w[:], self.rgw[l, g], writes=['rg_w'])
                B.dma('sp', bb[:], self.rgb[l, g], writes=['rg_b'])
                e.act(c8[:], bb[:, :, :, 2], AF.Exp, ['rg_b'], ['rg_c8'], scale=-1.0)
                e.act(c8[:], c8[:], AF.Ln, ['rg_c8'], ['rg_c8'], bias=1.0)
                e.ts('dve', c8[:], c8[:], -8.0, None, ALU.mult, None, ['rg_c8'], ['rg_c8'])
                wcols = [(0, 128), (128, 192)]
                orders = {0: BLOCKS, 1: [BLOCKS[0]] + BLOCKS[:0:-1]}

                def chain(d, i):
                    np_ = NP[i]
                    prev = None
                    ci = 2 * d + i
                    b0, b1 = 2 * ci, 2 * ci + 1
                    c0, c1 = wcols[i]
                    kh_ = ('rg_h', d, i)
                    for (s, n) in orders[d]:
                        tr_, ti_, ts_ = (tmp[(nm, d, i)][0:np_, 0:n] for nm in ['r', 'i', 's'])
                        kr, ki, ks = (('rg_t', nm, d, i) for nm in ['r', 'i', 's'])
                        xin = xa[i][0:np_, s:s + n]
                        e.mm(self.banks[0:np_, b0, 0:n], w[0:np_, d, 0, c0:c1], xin, True, True,
                             ['rg_w', ('rg_xa', i)], [self.bk(b0)])
                        e.mm(self.banks[0:np_, b1, 0:n], w[0:np_, d, 1, c0:c1], xin, True, True,
                             ['rg_w', ('rg_xa', i)], [self.bk(b1)])
                        yield
                        e.act(tr_, self.banks[0:np_, b0, 0:n], AF.Sigmoid, ['rg_b'], [kr, self.bk(b0)],
                              bias=bb[0:np_, i, d, 0:1])
                        e.act(ti_, self.banks[0:np_, b1, 0:n], AF.Sigmoid, ['rg_b'], [ki, self.bk(b1)],
                              bias=bb[0:np_, i, d, 1:2])
                        yield
                        e.act(tr_, tr_, AF.Exp, [kr, 'rg_c8'], [kr], scale=c8[0:np_, i, d:d + 1])
                        yield
                        e.tt('pool', ts_, tr_, tr_, ALU.mult, [kr], [ks])
                        yield
                        e.act(ts_, ts_, AF.Sqrt, [ks], [ks], bias=1.0, scale=-1.0)
                        yield
                        e.tt('dve', ti_, ts_, ti_, ALU.mult, [ks, ki], [ki])
                        yield
                        e.tt('pool', ti_, ti_, xin, ALU.mult, [ki, ('rg_xa', i)], [ki])
                        yield
                        init = prev if prev is not None else 0.0
                        dst = hh[d][i][0:np_, s:s + n]
                        if d == 1:
                            e.scan(rev(dst), rev(tr_), rev(ti_), init, [kr, ki, kh_], [kh_])
                            prev = hh[d][i][0:np_, s:s + 1]
                        else:
                            e.scan(dst, tr_, ti_, init, [kr, ki, kh_], [kh_])
                            prev = hh[d][i][0:np_, s + n - 1:s + n]
                        yield

                active = [chain(d, i) for d in range(2) for i in range(2)]
                while active:
                    nxt = []
                    for gnr in active:
                        try:
                            next(gnr)
                            nxt.append(gnr)
                        except StopIteration:
                            pass
                    active = nxt
                for i in range(2):
                    np_ = NP[i]
                    hf, hb = hh[0][i][0:np_, :], hh[1][i][0:np_, :]
                    for (s, n) in [(0, 2176), (2176, 2176)]:
                        e.tt('pool', hf[:, s:s + n], hf[:, s:s + n], hb[:, s:s + n], ALU.add,
                             [('rg_h', 0, i), ('rg_h', 1, i)], [('rg_h', 0, i)])
                        e.tt('dve', zs[i][0:np_, s:s + n], hf[:, s:s + n], zs[i][0:np_, s:s + n], ALU.mult,
                             [('rg_h', 0, i), ('rg_zs', i)], [('rg_zs', i)])
                    r0 = row0 + 128 * i
                    B.dma('pool', yc[r0:r0 + np_, :], zs[i][0:np_, :], reads=[('rg_zs', i)], writes=['YC'])
                B.barrier()

    def na_pass(self, l, g, yc, row0, ctx_out):
        B, e, nc = self.B, self.e, self.nc
        bnk = self.banks
        NP = [128, 64]
        with ExitStack() as p0:
            qT = [p0.enter_context(_sbt(nc, "na_q%d" % i, [128, T], BF16)) for i in range(2)]
            kT = [p0.enter_context(_sbt(nc, "na_k%d" % i, [128, T], BF16)) for i in range(2)]
            zn = p0.enter_context(_sbt(nc, "na_zn", [128, NT, 192], BF16))
            v_tm = p0.enter_context(_sbt(nc, "na_v", [128, NT, 192], BF16))
            with ExitStack() as ph:
                wsb = self.load_w(ph, l, g, 384, 768, "na")

                def fm_sink(idx, s, n, ps, bkey):
                    i = idx % 2
                    if idx < 2:
                        e.act(qT[i][0:NP[i], s:s + n], ps, AF.Copy, [], [bkey, 'na_q'], scale=0.125)
                    else:
                        e.cp('dve', kT[i][0:NP[i], s:s + n], ps, [], [bkey, 'na_k'])

                def tm_sink(idx, ti, ps, bkey):
                    e.act(zn[:, ti, :], ps[:, 0:192], AF.Silu, [], [bkey, 'na_zn'])
                    e.cp('dve', v_tm[:, ti, :], ps[:, 192:384], [], [bkey, 'na_v'])

                self.proj(ph, wsb, [(0, 128), (128, 64), (192, 128), (320, 64)], [(384, 384)], fm_sink, tm_sink)
                B.barrier()
            with ExitStack() as ph:
                def sbt(name, shape, dt=F32):
                    return ph.enter_context(_sbt(nc, name, shape, dt))
                nab = sbt("na_bias", [128, 3, 5, 640])
                NB = 2
                S = [[sbt("na_S%d_%d" % (p, h), [128, 896]) for h in range(3)] for p in range(NB)]
                Pb = [[sbt("na_Pb%d_%d" % (p, h), [128, 896], BF16) for h in range(3)] for p in range(NB)]
                PT = [[sbt("na_PT%d_%d" % (p, h), [128, 896], BF16) for h in range(3)] for p in range(NB)]
                st = [[sbt("na_st%d_%d" % (p, h), [128, 4]) for h in range(3)] for p in range(NB)]
                ytm = [sbt("na_ytm%d" % p, [128, 192], BF16) for p in range(NB)]
                ybT = [sbt("na_ybT%d" % p, [128, 256], BF16) for p in range(NB)]
                for h in range(3):
                    B.dma('sp', nab[:, h], self.nab[l, g, h], writes=['na_bias'])

                def qt_gen(it, qt, lat):
                    p = it % NB
                    if lat:
                        cls = {0: 0, 1: 1, 30: 3, 31: 4}.get(qt, 2)
                        kt0 = min(max(qt - 2, 0), 27)
                        tq = 2 + qt
                        kcol = (2 + kt0) * 128
                        ktiles = [2 + kt0 + c for c in range(5)] + [0, 1]
                        nk = 896
                    else:
                        tq = qt
                        ktiles = [0, 1]
                        nk = 256
                    qcol = tq * 128
                    nch = nk // 128
                    hs = range(3)
                    HR = [(0, 0), (0, 64), (1, 0)]
                    kS = [('na_S', p, h) for h in hs]
                    kP = [('na_Pb', p, h) for h in hs]
                    kPT = [('na_PT', p, h) for h in hs]
                    kst = [('na_st', p, h) for h in hs]
                    for h in hs:
                        ti_, hr0 = HR[h]
                        qa = qT[ti_][hr0:hr0 + 64, qcol:qcol + 128]
                        k_ = kT[ti_]
                        b0, b1 = 2 * h, 2 * h + 1
                        if lat:
                            e.mm(bnk[:, b0, 0:512], qa, k_[hr0:hr0 + 64, kcol:kcol + 512], True, True,
                                 ['na_q', 'na_k'], [self.bk(b0)])
                            e.mm(bnk[:, b1, 0:128], qa, k_[hr0:hr0 + 64, kcol + 512:kcol + 640], True, True,
                                 ['na_q', 'na_k'], [self.bk(b1)])
                            e.mm(bnk[:, b1, 128:384], qa, k_[hr0:hr0 + 64, 0:256], True, True,
                                 ['na_q', 'na_k'], [self.bk(b1)])
                        else:
                            e.mm(bnk[:, b1, 128:384], qa, k_[hr0:hr0 + 64, 0:256], True, True,
                                 ['na_q', 'na_k'], [self.bk(b1)])
                    yield
                    for h in hs:
                        b0, b1 = 2 * h, 2 * h + 1
                        if lat:
                            e.tt('dve', S[p][h][:, 0:512], bnk[:, b0, :], nab[:, h, cls, 0:512], ALU.add,
                                 ['na_bias'], [kS[h], self.bk(b0)])
                            e.tt('dve', S[p][h][:, 512:640], bnk[:, b1, 0:128], nab[:, h, cls, 512:640], ALU.add,
                                 ['na_bias'], [kS[h], self.bk(b1)])
                            e.cp('act', S[p][h][:, 640:896], bnk[:, b1, 128:384], [], [kS[h], self.bk(b1)])
                        else:
                            e.cp('act', S[p][h][:, 0:256], bnk[:, b1, 128:384], [], [kS[h], self.bk(b1)])
                    yield
                    for h in hs:
                        e.red(st[p][h][:, 0:1], S[p][h][:, 0:nk], ALU.max, [kS[h]], [kst[h]], negate=True)
                    yield
                    for h in hs:
                        e.act(Pb[p][h][:, 0:nk], S[p][h][:, 0:nk], AF.Exp, [kS[h], kst[h]], [kP[h], (kst[h], 'sum')],
                              bias=st[p][h][:, 0:1], scale=1.0, accum=st[p][h][:, 1:2])
                    yield
                    for h in hs:
                        pbf = bnk[:, 2 * h, :].bitcast(BF16)
                        for c in range(nch):
                            e.tr(pbf[:, c * 128:(c + 1) * 128], Pb[p][h][:, c * 128:(c + 1) * 128], self.identb[:],
                                 [kP[h], 'identb'], [self.bk(2 * h)])
                        e.recip(st[p][h][:, 2:3], st[p][h][:, 1:2], [(kst[h], 'sum')], [(kst[h], 'ri')])
                    yield
                    for h in hs:
                        pbf = bnk[:, 2 * h, :].bitcast(BF16)
                        e.cp('act' if h != 1 else 'dve', PT[p][h][:, 0:nk], pbf[:, 0:nk], [], [kPT[h], self.bk(2 * h)])
                    yield
                    for h in hs:
                        for c in range(nch):
                            e.mm(bnk[:, 6, h * 64:(h + 1) * 64], PT[p][h][:, c * 128:(c + 1) * 128],
                                 v_tm[:, ktiles[c], h * 64:(h + 1) * 64], c == 0, c == nch - 1,
                                 ['na_v', kPT[h]], [self.bk(6)])
                    yield
                    for h in hs:
                        e.stt(ytm[p][:, h * 64:(h + 1) * 64], bnk[:, 6, h * 64:(h + 1) * 64], st[p][h][:, 2:3],
                              zn[:, tq, h * 64:(h + 1) * 64], ALU.mult, ALU.mult, [(kst[h], 'ri'), 'na_zn'],
                              [('na_ytm', p), self.bk(6)])
                    yield
                    pb7 = bnk[:, 7, :].bitcast(BF16)
                    e.tr(pb7[:, 0:128], ytm[p][:, 0:128], self.identb[:], [('na_ytm', p), 'identb'], [self.bk(7)])
                    e.tr(pb7[0:64, 128:256], ytm[p][:, 128:192], self.identb[:], [('na_ytm', p), 'identb'], [self.bk(7)])
                    yield
                    e.cp('act', ybT[p][:, 0:128], pb7[:, 0:128], [], [('na_ybT', p), self.bk(7)])
                    e.cp('dve', ybT[p][0:64, 128:256], pb7[0:64, 128:256], [], [('na_ybT', p), self.bk(7)])
                    yield
                    B.dma('pool', yc[row0:row0 + 128, qcol:qcol + 128], ybT[p][:, 0:128], reads=[('na_ybT', p)],
                          writes=['YC'])
                    B.dma('pool', yc[row0 + 128:row0 + 192, qcol:qcol + 128], ybT[p][0:64, 128:256],
                          reads=[('na_ybT', p)], writes=['YC'])
                    yield

                qts = [(qt, True) for qt in range(32)]
                if ctx_out:
                    qts += [(0, False), (1, False)]
                pending = [qt_gen(i, qt, lat) for i, (qt, lat) in enumerate(qts)]
                active = []
                rounds = 0
                while pending or active:
                    if pending and (rounds % 5 == 0) and len(active) < NB:
                        active.append(pending.pop(0))
                    nxt = []
                    for gnr in active:
                        try:
                            next(gnr)
                            nxt.append(gnr)
                        except StopIteration:
                            pass
                    active = nxt
                    rounds += 1
                B.barrier()

    def gdn_pass(self, l, g, yc, row0):
        B, e, nc = self.B, self.e, self.nc
        bnk = self.banks
        with ExitStack() as p0:
            qT = p0.enter_context(_sbt(nc, "gd_q", [128, T], F32))
            kT = p0.enter_context(_sbt(nc, "gd_k", [128, T], F32))
            zg = p0.enter_context(_sbt(nc, "gd_zg", [128, NT, 128], BF16))
            ba = p0.enter_context(_sbt(nc, "gd_ba", [128, NT, 8], F32))
            vT = p0.enter_context(_sbt(nc, "gd_v", [128, T], F32))
            cw = p0.enter_context(_sbt(nc, "gd_cw", [128, 5, 4], F32))
            with ExitStack() as p1:
                B.dma('sp', cw[:], self.convw[l, g], writes=['cw'])
                with ExitStack() as ph:
                    wsb = self.load_w(ph, l, g, 1152, 520, "gd")
                    P = [ph.enter_context(_sbt(nc, "gd_P%d" % i, [128, PADW], F32)) for i in range(3)]
                    for i in range(3):
                        self.zero_pads(P[i], ('gdP', i))

                    def fm_sink(idx, s, n, ps, bkey):
                        e.cp('act', P[idx][:, pcol(s):pcol(s) + n], ps, [], [bkey, ('gdP', idx)])

                    def tm_sink(idx, ti, ps, bkey):
                        e.act(zg[:, ti, :], ps[:, 0:128], AF.Silu, [], [bkey, 'gd_zg'])
                        e.cp('dve', ba[:, ti, :], ps[:, 128:136], [], [bkey, 'gd_ba'])

                    self.proj(ph, wsb, [(0, 128), (128, 128), (256, 128)], [(384, 136)], fm_sink, tm_sink)
                    for i, dst in enumerate([qT, kT, vT]):
                        self.conv(P[i], dst, cw, 2 + i, 128, [('gdP', i), 'cw'], [('gd_c', i)])
                    B.barrier()
                kv = p1.enter_context(_sbt(nc, "gd_kv", [128, NT, 256], F32))
                with ExitStack() as ph:
                    sq = [[ph.enter_context(_sbt(nc, "gd_sq%d%d" % (i, k), [128, 512], F32)) for k in range(2)] for i in range(2)]
                    nr = [[ph.enter_context(_sbt(nc, "gd_nr%d%d" % (i, k), [128, 512], F32)) for k in range(2)] for i in range(2)]
                    t1 = [[ph.enter_context(_sbt(nc, "gd_t1%d%d" % (i, k), [128, 512], F32)) for k in range(2)] for i in range(2)]
                    cs = [ph.enter_context(_sbt(nc, "gd_cs%d" % i, [128, 512], F32)) for i in range(2)]
                    sn = [ph.enter_context(_sbt(nc, "gd_sn%d" % i, [128, 512], F32)) for i in range(2)]

                    def blk_gen(bi, s, n):
                        lat = s >= 256
                        pb = bi % 2
                        if lat:
                            B.dma('sp', cs[pb][:], self.rope_cos[:, s - 256:s - 256 + 512], writes=[('gd_cs', pb)])
                            B.dma('sp', sn[pb][:], self.rope_sin[:, s - 256:s - 256 + 512], writes=[('gd_sn', pb)])
                        Xs = [qT[:, s:s + n], kT[:, s:s + n]]
                        kx = [('gd_cb', 0, bi), ('gd_cb', 1, bi)]
                        ksq = [('gd_sq', wi, pb) for wi in range(2)]
                        knr = [('gd_nr', wi, pb) for wi in range(2)]
                        kt1 = [('gd_t1', wi, pb) for wi in range(2)]
                        bb0 = [2 * (2 * pb + wi) for wi in range(2)]
                        for wi in range(2):
                            e.act(Xs[wi], Xs[wi], AF.Silu, [kx[wi]], [kx[wi]])
                        e.act(vT[:, s:s + n], vT[:, s:s + n], AF.Silu, [('gd_vb', bi)], [('gd_vb', bi)])
                        yield
                        for wi in range(2):
                            e.tt('pool', sq[wi][pb][:, 0:n], Xs[wi], Xs[wi], ALU.mult, [kx[wi]], [ksq[wi]])
                        yield
                        for wi in range(2):
                            e.mm(bnk[:, bb0[wi], 0:n], self.c(C_BONE), sq[wi][pb][:, 0:n], True, True, [ksq[wi], 'cst'],
                                 [self.bk(bb0[wi])])
                        yield
                        for wi in range(2):
                            e.act(nr[wi][pb][:, 0:n], bnk[:, bb0[wi], 0:n], AF.Sqrt, [], [knr[wi], self.bk(bb0[wi])],
                                  bias=NORM_EPS)
                        yield
                        for wi in range(2):
                            e.recip(nr[wi][pb][:, 0:n], nr[wi][pb][:, 0:n], [knr[wi]], [knr[wi]])
                        yield
                        e.stt(Xs[0], Xs[0], 0.125, nr[0][pb][:, 0:n], ALU.mult, ALU.mult, [kx[0], knr[0]], [kx[0]])
                        e.tt('dve', Xs[1], Xs[1], nr[1][pb][:, 0:n], ALU.mult, [kx[1], knr[1]], [kx[1]])
                        yield
                        if lat:
                            for wi in range(2):
                                b1 = bb0[wi] + 1
                                e.mm(bnk[:, b1, 0:n], self.c(C_RM), Xs[wi], True, True, [kx[wi], 'cst'], [self.bk(b1)])
                                e.tt('pool', t1[wi][pb][:, 0:n], Xs[wi], cs[pb][:, 0:n], ALU.mult, [kx[wi], ('gd_cs', pb)],
                                     [kt1[wi]])
                            yield
                            for wi in range(2):
                                b1 = bb0[wi] + 1
                                e.tt('dve', Xs[wi], bnk[:, b1, 0:n], sn[pb][:, 0:n], ALU.mult, [('gd_sn', pb)],
                                     [kx[wi], self.bk(b1)])
                            yield
                            for wi in range(2):
                                e.tt('pool', Xs[wi], Xs[wi], t1[wi][pb][:, 0:n], ALU.add, [kx[wi], kt1[wi]], [kx[wi]])
                            yield

                    pend = [blk_gen(bi, s, n) for bi, (s, n) in enumerate(BLOCKS)]
                    active = []
                    rounds = 0
                    while pend or active:
                        if pend and rounds % 4 == 0 and len(active) < 2:
                            active.append(pend.pop(0))
                        nxt = []
                        for gnr in active:
                            try:
                                next(gnr)
                                nxt.append(gnr)
                            except StopIteration:
                                pass
                        active = nxt
                        rounds += 1
                    for ti in range(NT):
                        b0 = ti % 4
                        blk = 0 if ti < 2 else 1 + (ti - 2) // 4
                        e.tr(bnk[:, b0, 0:128], kT[:, ti * 128:(ti + 1) * 128], self.c(C_ID), [('gd_cb', 1, blk), 'cst'],
                             [self.bk(b0)])
                        e.tr(bnk[:, b0, 128:256], vT[:, ti * 128:(ti + 1) * 128], self.c(C_ID), [('gd_vb', blk), 'cst'],
                             [self.bk(b0)])
                        e.cp('act' if ti % 2 else 'dve', kv[:, ti, :], bnk[:, b0, 0:256], [], ['gd_kv', self.bk(b0)])
                    B.barrier()
                if self.debug:
                    self.dump("gd_q_%d_%d" % (l, g), qT[:], [128, T], [('gd_c', 0)])
                    self.dump("gd_k_%d_%d" % (l, g), kT[:], [128, T], [('gd_c', 1)])
                    self.dump("gd_kv_%d_%d" % (l, g), kv[:], [128, NT, 256], ['gd_kv'])
                with ExitStack() as ph:
                    def sbt(name, shape, dt=F32):
                        return ph.enter_context(_sbt(nc, name, shape, dt))
                    o_tm = [sbt("gd_o0", [128, NT, 128]), vT[:].rearrange("p (t c) -> p t c", c=128)]
                    al = sbt("gd_al", [128, NT, 4])
                    dtb = sbt("gd_dtb", [128, NT, 4])
                    g_all = sbt("gd_g", [128, NT, 4])
                    beta = sbt("gd_beta", [128, NT, 4])
                    nbeta = sbt("gd_nbeta", [128, NT, 4])
                    tg = sbt("gd_tg", [128, NT, 4])
                    B.dma('sp', al[:], self.galog[l, g], writes=['gd_al'])
                    B.dma('sp', dtb[:], self.gdtb[l, g], writes=['gd_dtb'])
                    e.act(beta[:], ba[:, :, 0:4], AF.Sigmoid, ['gd_ba'], ['gd_beta'])
                    e.ts('dve', nbeta[:], beta[:], -1.0, None, ALU.mult, None, ['gd_beta'], ['gd_nbeta'])
                    e.tt('dve', tg[:], ba[:, :, 4:8], dtb[:], ALU.add, ['gd_ba', 'gd_dtb'], ['gd_tg'])
                    e.act(tg[:], tg[:], AF.Exp, ['gd_tg'], ['gd_tg'])
                    e.act(tg[:], tg[:], AF.Ln, ['gd_tg'], ['gd_tg'], bias=1.0)
                    e.act(al[:], al[:], AF.Exp, ['gd_al'], ['gd_al'])
                    e.stt(g_all[:], tg[:], -1.0, al[:], ALU.mult, ALU.mult, ['gd_tg', 'gd_al'], ['gd_g'])
                    if self.debug:
                        self.dump("gd_g_%d_%d" % (l, g), g_all[:], [128, NT, 4], ['gd_g'])
                        self.dump("gd_beta_%d_%d" % (l, g), beta[:], [128, NT, 4], ['gd_beta'])
                    gc = sbt("gd_gc", [128, NT, 4])
                    gl = sbt("gd_gl", [128, NT, 4])
                    eg = sbt("gd_eg", [128, NT, 4])
                    neg_eg = sbt("gd_negeg", [128, NT, 4])
                    negc = sbt("gd_negc", [128, NT, 4])
                    egl = sbt("gd_egl", [128, NT, 4])
                    eglast = sbt("gd_eglast", [128, NT, 4])
                    gflat = g_all[:].rearrange("p t c -> p (t c)")
                    NV = NT * 4
                    e.mm(bnk[:, 0, 0:NV], self.c(C_UF), gflat, True, True, ['cst', 'gd_g'], [self.bk(0)])
                    e.mm(bnk[:, 0, NV:2 * NV], self.c(C_UB), gflat, True, True, ['cst', 'gd_g'], [self.bk(0)])
                    e.mm(bnk[:, 0, 2 * NV:3 * NV], self.c(C_ONE), gflat, True, True, ['cst', 'gd_g'], [self.bk(0)])
                    pv0 = bnk[:, 0, 0:NV].rearrange("p (t c) -> p t c", c=4)
                    pv1 = bnk[:, 0, NV:2 * NV].rearrange("p (t c) -> p t c", c=4)
                    pv2 = bnk[:, 0, 2 * NV:3 * NV].rearrange("p (t c) -> p t c", c=4)
                    e.cp('dve', gc[:, :, 0:2], pv0[:, :, 0:2], [], ['gd_gc', self.bk(0)])
                    e.cp('dve', gc[:, :, 2:4], pv1[:, :, 2:4], [], ['gd_gc', self.bk(0)])
                    e.cp('dve', gl[:], pv2, [], ['gd_gl', self.bk(0)])
                    e.act(eg[:], gc[:], AF.Exp, ['gd_gc'], ['gd_vecs'])
                    e.ts('pool', neg_eg[:], eg[:], -1.0, None, ALU.mult, None, ['gd_vecs'], ['gd_vecs2'])
                    e.ts('pool', negc[:], gc[:], -1.0, None, ALU.mult, None, ['gd_gc'], ['gd_negc'])
                    e.tt('pool', egl[:], gl[:], gc[:], ALU.subtract, ['gd_gc', 'gd_gl'], ['gd_egl'])
                    e.act(egl[:], egl[:], AF.Exp, ['gd_egl'], ['gd_egl'])
                    e.act(eglast[:], gl[:], AF.Exp, ['gd_gl'], ['gd_eglast'])
                    R = 3
                    chains = [(d, h) for d in range(2) for h in range(2)]
                    T2t = {c: [sbt("gd_T2t%d%d_%d" % (c[0], c[1], r), [128, 128]) for r in range(R)] for c in chains}
                    QKt = {c: [sbt("gd_QKt%d%d_%d" % (c[0], c[1], r), [128, 128]) for r in range(R)] for c in chains}
                    vec = {c: [sbt("gd_vec%d%d_%d" % (c[0], c[1], r), [128, 4]) for r in range(R)] for c in chains}
                    Sst = {c: sbt("gd_S%d%d" % c, [128, 64]) for c in chains}
                    Rp = {c: sbt("gd_Rp%d%d" % c, [128, 64]) for c in chains}
                    qs = {c: sbt("gd_qs%d%d" % c, [128, 64]) for c in chains}
                    vnw = {c: sbt("gd_vn%d%d" % c, [128, 64]) for c in chains}
                    kh = {c: sbt("gd_kh%d%d" % c, [128, 64]) for c in chains}
                    Gm = [sbt("gd_G%d" % i, [128, 128]) for i in range(4)]
                    nG = [sbt("gd_nG%d" % i, [128, 128]) for i in range(4)]
                    Et = [sbt("gd_Et%d" % i, [128, 128]) for i in range(4)]
                    tv = [sbt("gd_tv%d" % i, [128, 2]) for i in range(4)]
                    YT = [[sbt("gd_YT%d_%d" % (i, k), [128, 256]) for k in range(2)] for i in range(4)]
                    Yt = [[sbt("gd_Yt%d_%d" % (i, k), [128, 128]) for k in range(2)] for i in range(4)]
                    for c in chains:
                        e.memset('pool', Sst[c][:], 0.0, [('gd_S', c)])
                    ident = self.c(C_ID)
                    ones = self.c(C_ONE)
                    orders = {0: list(range(NT)), 1: [1, 0] + list(range(NT - 1, 1, -1))}

                    def prep(c, ti, slot):
                        d, h = c
                        col = 2 * d + h
                        pp = chains.index(c)
                        U = self.c(C_UF if d == 0 else C_UB)
                        Sm = self.c(C_SF if d == 0 else C_SB)
                        Mk = self.c(C_MF if d == 0 else C_MB)
                        hr0 = 64 * h
                        tc0 = ti * 128
                        gcol = g_all[:, ti, col:col + 1]
                        kG, knG, kEt, ktv = ('gd_G', pp), ('gd_nG', pp), ('gd_Et', pp), ('gd_tv', pp)
                        kslot = ('gd_slot', c, slot)
                        bP = pp
                        kb = self.bk(bP)
                        pD = bnk[:, bP, 0:128]
                        pV = bnk[:, bP, 128:130]
                        pKK = bnk[:, bP, 132:260]
                        pQK = bnk[:, bP, 260:388]
                        pI = bnk[:, bP, 0:256]
                        pJ = bnk[:, bP, 256:384]
                        e.ts('pool', Gm[pp][:], U, gcol, None, ALU.mult, None, ['cst', 'gd_g'], [kG])
                        yield
                        e.mm(pD, ones, Gm[pp][:], True, True, ['cst', kG], [kb])
                        ka = kT[hr0:hr0 + 64, tc0:tc0 + 128]
                        qa = qT[hr0:hr0 + 64, tc0:tc0 + 128]
                        e.mm(pKK, ka, ka, True, True, [('gd_c', 1)], [kb])
                        e.mm(pQK, ka, qa, True, True, [('gd_c', 1), ('gd_c', 0)], [kb])
                        yield
                        e.stt(nG[pp][:], pD, negc[:, ti, col:col + 1], Mk, ALU.add, ALU.add, ['gd_negc', 'cst'], [knG, kb])
                        yield
                        e.act(Et[pp][:], nG[pp][:], AF.Exp, [knG], [kEt])
                        yield
                        vc = vec[c][slot]
                        Y0 = YT[pp][0]
                        kYT = [('gd_YT', pp, 0), ('gd_YT', pp, 1)]
                        kYt = [('gd_Yt', pp, 0), ('gd_Yt', pp, 1)]
                        e.stt(Y0[:, 0:128], pKK, nbeta[:, ti, col:col + 1], Et[pp][:], ALU.mult, ALU.mult,
                              ['gd_nbeta', kEt], [kYT[0], kb])
                        e.tt('dve', QKt[c][slot][:], pQK, Et[pp][:], ALU.mult, [kEt], [(kslot, 'QK'), kb])
                        yield
                        e.tt('pool', Y0[:, 0:128], Y0[:, 0:128], Sm, ALU.mult, [kYT[0], 'cst'], [kYT[0]])
                        yield
                        e.tr(pJ, Y0[:, 0:128], ident, [kYT[0], 'cst'], [kb])
                        e.tt('pool', Y0[:, 128:256], Y0[:, 0:128], ident, ALU.add, [kYT[0], 'cst'], [kYT[0]])
                        yield
                        e.cp('act', Yt[pp][0][:], pJ, [], [kYt[0], kb])
                        yield
                        e.mm(bnk[:, bP, 0:128], Yt[pp][0][:], Y0[:, 0:128], True, True, [kYt[0], kYT[0]], [kb])
                        e.mm(pJ, Y0[:, 0:128], Yt[pp][0][:], True, True, [kYt[0], kYT[0]], [kb])
                        yield
                        e.cp('act', YT[pp][1][:, 0:128], bnk[:, bP, 0:128], [], [kYT[1], kb])
                        e.cp('dve', Yt[pp][1][:], pJ, [], [kYt[1], kb])
                        e.cp('pool', YT[pp][1][:, 128:256], Y0[:, 128:256], [kYT[0]], [kYT[1]])
                        yield
                        for q in range(1, 7):
                            a = q % 2
                            nb_ = 1 - a
                            cur, curt = YT[pp][a], Yt[pp][a]
                            nx, nxt_ = YT[pp][nb_], Yt[pp][nb_]
                            if q < 6:
                                e.mm(pI, curt[:], cur[:, 0:256], True, True, [kYt[a], kYT[a]], [kb])
                                e.mm(pJ, cur[:, 0:128], curt[:], True, True, [kYt[a], kYT[a]], [kb])
                                yield
                                e.cp('act', nx[:, 0:128], bnk[:, bP, 0:128], [], [kYT[nb_], kb])
                                e.tt('dve', nx[:, 128:256], bnk[:, bP, 128:256], cur[:, 128:256], ALU.add, [kYT[a]],
                                     [kYT[nb_], kb])
                                e.cp('act' if q % 2 else 'dve', nxt_[:], pJ, [], [kYt[nb_], kb])
                                yield
                            else:
                                e.mm(bnk[:, bP, 0:128], curt[:], cur[:, 128:256], True, True, [kYt[a], kYT[a]], [kb])
                                yield
                                e.tt('dve', T2t[c][slot][:], bnk[:, bP, 0:128], cur[:, 128:256], ALU.add, [kYT[a]],
                                     [(kslot, 'T'), kb])
                                yield

                    def seq_stage(stage, c, ti, slot):
                        d, h = c
                        col = 2 * d + h
                        hr0 = 64 * h
                        tc0 = ti * 128
                        ci = chains.index(c)
                        bC = 4 + ci
                        kslot = ('gd_slot', c, slot)
                        vc = vec[c][slot]
                        kS = ('gd_S', c)
                        if stage == 0:
                            e.mm(bnk[:, bC, 0:64], kT[hr0:hr0 + 64, tc0:tc0 + 128], Sst[c][hr0:hr0 + 64, :], True, True,
                                 [('gd_c', 1), kS], [self.bk(bC)])
                            e.mm(bnk[:, bC, 64:128], qT[hr0:hr0 + 64, tc0:tc0 + 128], Sst[c][hr0:hr0 + 64, :], True, True,
                                 [('gd_c', 0), kS], [self.bk(bC)])
                            e.ts('pool', kh[c][:], kv[:, ti, hr0:hr0 + 64], egl[:, ti, col:col + 1], None, ALU.mult, None,
                                 ['gd_kv', 'gd_egl'], [('gd_kh', c)])
                        elif stage == 1:
                            e.stt(Rp[c][:], bnk[:, bC, 0:64], neg_eg[:, ti, col:col + 1], kv[:, ti, 128 + hr0:128 + hr0 + 64],
                                  ALU.mult, ALU.add, ['gd_vecs2', 'gd_kv'], [('gd_Rp', c), self.bk(bC)])
                            e.act(qs[c][:], bnk[:, bC, 64:128], AF.Identity, ['gd_vecs'], [('gd_qs', c), self.bk(bC)],
                                  scale=eg[:, ti, col:col + 1])
                        elif stage == 2:
                            e.mm(bnk[:, bC, 128:192], T2t[c][slot][:], Rp[c][:], True, True, [(kslot, 'T'), ('gd_Rp', c)],
                                 [self.bk(bC)])
                        elif stage == 3:
                            e.act(vnw[c][:], bnk[:, bC, 128:192], AF.Identity, ['gd_beta'], [('gd_vn', c), self.bk(bC)],
                                  scale=beta[:, ti, col:col + 1])
                        elif stage == 4:
                            e.mm(bnk[:, bC, 192:256], QKt[c][slot][:], vnw[c][:], True, True, [(kslot, 'QK'), ('gd_vn', c)],
                                 [self.bk(bC)])
                            e.mm(bnk[hr0:hr0 + 64, bC, 256:320], kh[c][:], vnw[c][:], True, True,
                                 [('gd_kh', c), ('gd_vn', c)], [self.bk(bC)])
                        else:
                            e.tt('dve', o_tm[d][:, ti, hr0:hr0 + 64], bnk[:, bC, 192:256], qs[c][:], ALU.add,
                                 [('gd_qs', c)], [('gd_o', d), self.bk(bC)])
                            e.stt(Sst[c][hr0:hr0 + 64, :], Sst[c][hr0:hr0 + 64, :], eglast[hr0:hr0 + 64, ti, col:col + 1],
                                  bnk[hr0:hr0 + 64, bC, 256:320], ALU.mult, ALU.add, [kS, 'gd_eglast'], [kS, self.bk(bC)])

                    def run_wave(preps, seqs, ratio=4):
                        active = list(preps)
                        sq = list(seqs)
                        k = 0
                        while active or sq:
                            nxt_active = []
                            for gnr in active:
                                try:
                                    next(gnr)
                                    nxt_active.append(gnr)
                                except StopIteration:
                                    pass
                            active = nxt_active
                            k += 1
                            if sq and (k % ratio == 0 or not active):
                                nsq = []
                                for gnr in sq:
                                    try:
                                        next(gnr)
                                        nsq.append(gnr)
                                    except StopIteration:
                                        pass
                                sq = nsq

                    def seq_gen(c, ti, slot):
                        for stage in range(6):
                            seq_stage(stage, c, ti, slot)
                            yield

                    run_wave([prep(c, orders[c[0]][0], 0) for c in chains], [])
                    for si in range(NT):
                        preps = []
                        if si + 1 < NT:
                            preps = [prep(c, orders[c[0]][si + 1], (si + 1) % R) for c in chains]
                        seqs = [seq_gen(c, orders[c[0]][si], si % R) for c in chains]
                        run_wave(preps, seqs)
                    if self.debug:
                        self.dump("gd_o0_%d_%d" % (l, g), o_tm[0][:], [128, NT, 128], [('gd_o', 0)])
                        self.dump("gd_o1_%d_%d" % (l, g), o_tm[1][:], [128, NT, 128], [('gd_o', 1)])
                    nw = sbt("gd_nw", [128, 128])
                    ss = sbt("gd_ss", [128, NT * 2])
                    yb = [sbt("gd_yb%d" % i, [128, 512], BF16) for i in range(2)]
                    B.dma('sp', nw[:], self.gnw[l], writes=['gd_nw'])
                    O = o_tm[0]
                    O1 = o_tm[1]
                    Of = O[:].rearrange("p t c -> p (t c)")
                    O1f = O1[:].rearrange("p t c -> p (t c)")
                    e.tt('pool', Of, Of, O1f, ALU.add, [('gd_o', 0), ('gd_o', 1)], [('gd_o', 0)])
                    e.tt('dve', O1f, Of, Of, ALU.mult, [('gd_o', 0)], [('gd_o', 1)])
                    e.red(ss[:], O1[:].rearrange("p t (h c) -> p (t h) c", c=64), ALU.add, [('gd_o', 1)], ['gd_ss'])
                    e.ts('dve', ss[:], ss[:], 1.0 / 64.0, NORM_EPS, ALU.mult, ALU.add, ['gd_ss'], ['gd_ss'])
                    e.act(ss[:], ss[:], AF.Sqrt, ['gd_ss'], ['gd_ss'])
                    e.recip(ss[:], ss[:], ['gd_ss'], ['gd_ss'])
                    O3 = O[:].rearrange("p t (h c) -> p (t h) c", c=64)
                    e.tt('dve', O3, O3, ss[:].unsqueeze(2).to_broadcast([128, NT * 2, 64]), ALU.mult,
                         [('gd_o', 0), 'gd_ss'], [('gd_o', 0)])
                    e.tt('pool', O[:], O[:], nw[:].unsqueeze(1).to_broadcast([128, NT, 128]), ALU.mult,
                         [('gd_o', 0), 'gd_nw'], [('gd_o', 0)])
                    e.tt('dve', O[:], O[:], zg[:], ALU.mult, [('gd_o', 0), 'gd_zg'], [('gd_o', 0)])
                    groups = [[0, 1]] + [list(range(2 + 4 * i, 6 + 4 * i)) for i in range(8)]
                    for gi, tl in enumerate(groups):
                        pb = gi % 2
                        bO = gi % 4
                        for k, ti in enumerate(tl):
                            e.tr(bnk[:, bO, k * 128:(k + 1) * 128], O[:, ti, :], ident, [('gd_o', 0), 'cst'], [self.bk(bO)])
                        n = 128 * len(tl)
                        e.cp('act' if gi % 2 else 'dve', yb[pb][:, 0:n], bnk[:, bO, 0:n], [], [('gd_yb', pb), self.bk(bO)])
                        B.dma('pool', yc[row0:row0 + 128, tl[0] * 128:tl[0] * 128 + n], yb[pb][:, 0:n],
                              reads=[('gd_yb', pb)], writes=['YC'])
                    B.barrier()

    def back_phase(self, l, x_cur, yc, x_next, last):
        B, e, nc = self.B, self.e, self.nc
        bnk = self.banks
        with ExitStack() as ph:
            def sbt(name, shape, dt=F32):
                return ph.enter_context(_sbt(nc, name, shape, dt))
            wo = sbt("bk_wo", [128, KC, D], BF16)
            stg = [sbt("bk_stg%d" % i, [128, D]) for i in range(2)]
            ycb = [sbt("bk_yc%d" % i, [128, KC, 128], BF16) for i in range(2)]
            xt = [sbt("bk_xt%d" % i, [128, D]) for i in range(2)]
            tb = [sbt("bk_tb%d" % i, [128, D]) for i in range(2)]
            zb = [sbt("bk_zb%d" % i, [128, D]) for i in range(2)]
            lng = sbt("bk_lng", [128, D])
            lnb = sbt("bk_lnb", [128, D])
            st6 = [sbt("bk_st6%d" % i, [128, 2, 6]) for i in range(2)]
            mv = [sbt("bk_mv%d" % i, [128, 4]) for i in range(2)]
            B.dma('sp', lng[:], self.lng_rep[l], writes=['bk_lng'])
            B.dma('sp', lnb[:], self.lnb_rep[l], writes=['bk_lnb'])
            for kc in range(KC):
                b = kc % 2
                B.dma('sp', stg[b][:], self.w_out[l, kc * 128:(kc + 1) * 128, :], writes=[('bk_stg', b)])
                e.cp('pool', wo[:, kc, :], stg[b][:], [('bk_stg', b)], ['bk_wo'])
            ycv = yc.rearrange("(c p) t -> p c t", p=128)
            tiles = list(range(2, NT)) if last else list(range(NT))
            for n_, ti in enumerate(tiles):
                j = 1 if ti < 2 else 0
                b = n_ % 2
                b0 = 2 * b
                B.dma('sp', ycb[b][:], ycv[:, :, ti * 128:(ti + 1) * 128], reads=['YC'], writes=[('bk_yc', b)])
                B.dma('sp', xt[b][:], x_cur[ti * 128:(ti + 1) * 128, :], reads=['X'], writes=[('bk_xt', b)])
                for nb in range(2):
                    for c in range(KC):
                        e.mm(bnk[:, b0 + nb, :], ycb[b][:, c, :], wo[:, c, nb * 512:(nb + 1) * 512], c == 0, c == KC - 1,
                             [('bk_yc', b), 'bk_wo'], [self.bk(b0 + nb)])
                for nb in range(2):
                    e.tt('dve', tb[b][:, nb * 512:(nb + 1) * 512], bnk[:, b0 + nb, :],
                         self.gate_bc[:, j, nb * 512:(nb + 1) * 512], ALU.mult, ['gate_bc'],
                         [('bk_tb', b), self.bk(b0 + nb)])
                e.stt(zb[b][:], xt[b][:], ALPHA, tb[b][:], ALU.mult, ALU.add, [('bk_xt', b), ('bk_tb', b)], [('bk_zb', b)])
                for nb in range(2):
                    B.op('dve', (lambda o_, i_: (lambda en: en.bn_stats(out=o_, in_=i_)))(
                        st6[b][:, nb, :], zb[b][:, nb * 512:(nb + 1) * 512]), [('bk_zb', b)], [('bk_st6', b)])
                B.op('dve', (lambda o_, i_: (lambda en: en.bn_aggr(out=o_, in_=i_)))(
                    mv[b][:, 0:2], st6[b][:].rearrange("p a s -> p (a s)")), [('bk_st6', b)], [('bk_mv', b)])
                e.act(mv[b][:, 2:3], mv[b][:, 1:2], AF.Sqrt, [('bk_mv', b)], [('bk_sd', b)], bias=LN_EPS)
                e.recip(mv[b][:, 3:4], mv[b][:, 2:3], [('bk_sd', b)], [('bk_rs', b)])
                e.ts('dve', zb[b][:], zb[b][:], mv[b][:, 0:1], mv[b][:, 3:4], ALU.subtract, ALU.mult,
                     [('bk_zb', b), ('bk_mv', b), ('bk_rs', b)], [('bk_zb', b)])
                e.tt('pool', zb[b][:], zb[b][:], lng[:], ALU.mult, [('bk_zb', b), 'bk_lng'], [('bk_zb', b)])
                e.tt('pool', zb[b][:], zb[b][:], lnb[:], ALU.add, [('bk_zb', b), 'bk_lnb'], [('bk_zb', b)])
                if last:
                    dst = x_next[(ti - 2) * 128:(ti - 1) * 128, :]
                else:
                    dst = x_next[ti * 128:(ti + 1) * 128, :]
                B.dma('pool', dst, zb[b][:], reads=[('bk_zb', b)], writes=['Xn'])
            B.barrier()
            B.last_w['X'] = B.last_w.get('Xn')


def group_cols(g):
    R0 = 1152
    rgx = np.arange(192 * g, 192 * g + 192)
    gq = 384 + np.arange(128 * g, 128 * g + 128)
    gk = 640 + np.arange(128 * g, 128 * g + 128)
    gv = 896 + np.arange(128 * g, 128 * g + 128)
    rgg = R0 + np.arange(192 * g, 192 * g + 192)
    naq = R0 + 384 + np.arange(192 * g, 192 * g + 192)
    nak = R0 + 768 + np.arange(192 * g, 192 * g + 192)
    nav = R0 + 1152 + np.arange(192 * g, 192 * g + 192)
    nag = R0 + 1536 + np.arange(192 * g, 192 * g + 192)
    gg = R0 + 1920 + np.arange(128 * g, 128 * g + 128)
    bf = R0 + 2176 + np.arange(2 * g, 2 * g + 2)
    bb = R0 + 2180 + np.arange(2 * g, 2 * g + 2)
    af = R0 + 2184 + np.arange(2 * g, 2 * g + 2)
    ab = R0 + 2188 + np.arange(2 * g, 2 * g + 2)
    cols = np.concatenate([rgx, rgg, naq, nak, nag, nav, gq, gk, gv, gg, bf, bb, af, ab])
    assert cols.shape[0] == GCOLS
    conv_ch = [rgx[:128], rgx[128:], gq, gk, gv]
    return cols, conv_ch


def make_consts():
    c = np.zeros((128, NCONST), np.float32)
    i = np.arange(128)
    c[:, C_ID:C_ID + 128] = np.eye(128)
    c[:, C_ONE:C_ONE + 128] = 1.0
    c[:64, C_BONE:C_BONE + 64] = 1.0
    c[64:, C_BONE + 64:C_BONE + 128] = 1.0
    R = np.zeros((128, 128), np.float32)
    for h in range(2):
        for half in range(2):
            o = 64 * h + 32 * half
            for t in range(16):
                R[o + t, o + t + 16] = -1.0
                R[o + t + 16, o + t] = 1.0
    c[:, C_RM:C_RM + 128] = R.T
    k = i[:, None]
    m = i[None, :]
    c[:, C_UF:C_UF + 128] = (k <= m)
    c[:, C_UB:C_UB + 128] = (k >= m)
    c[:, C_SF:C_SF + 128] = (m > k)
    c[:, C_SB:C_SB + 128] = (m < k)
    c[:, C_MF:C_MF + 128] = np.where(m >= k, 0.0, -BIG)
    c[:, C_MB:C_MB + 128] = np.where(m <= k, 0.0, -BIG)
    return c


def make_rope():
    p = np.arange(128)
    d = p % 64
    half = d // 32
    fi = d % 16
    inv_freq = (np.float32(10000.0) ** (-np.arange(16, dtype=np.float32) / np.float32(16))).astype(np.float32)
    t = np.arange(4096)
    row = (t // 64).astype(np.float32)
    col = (t % 64).astype(np.float32)
    pos = np.where(half[:, None] == 0, row[None, :], col[None, :]).astype(np.float32)
    ang = (pos * inv_freq[fi][:, None]).astype(np.float32)
    return np.cos(ang).astype(np.float32), np.sin(ang).astype(np.float32)


NA_CLASSES = [(0, 0), (1, 0), (2, 0), (30, 27), (31, 27)]


def make_nab(rpb_h):
    out = np.full((128, 5, 640), -BIG, np.float32)
    q = np.arange(128)
    key = np.arange(640)
    for ci, (qt, kt0) in enumerate(NA_CLASSES):
        r = 2 * qt + q // 64
        j = q % 64
        kr = 2 * kt0 + key // 64
        kcn = key % 64
        r0 = np.clip(r - 4, 0, 56)
        c0 = np.clip(j - 8, 0, 48)
        okr = (kr[None, :] >= r0[:, None]) & (kr[None, :] < r0[:, None] + 8)
        okc = (kcn[None, :] >= c0[:, None]) & (kcn[None, :] < c0[:, None] + 16)
        ro = np.clip(kr[None, :] - r[:, None] + 7, 0, 14)
        co = np.clip(kcn[None, :] - j[:, None] + 15, 0, 30)
        vals = rpb_h[ro, co]
        out[:, ci, :] = np.where(okr & okc, vals, np.float32(-BIG))
    return out


def prep_shared(inp, G):
    L = DEPTH
    sh = {}
    sh['consts'] = make_consts()
    cs, sn = make_rope()
    sh['rope_cos'], sh['rope_sin'] = cs, sn
    sh['w_mod'] = np.ascontiguousarray(inp['w_mod'])
    sh['bmod_rep'] = np.ascontiguousarray(np.broadcast_to(inp['b_mod'][:, None, :], (L, 128, 3 * D)))
    sh['gnw'] = np.ascontiguousarray(np.broadcast_to(np.tile(inp['gdn_nw'], (1, 2))[:, None, :], (L, 128, 128)))
    sh['lng_rep'] = np.ascontiguousarray(np.broadcast_to(inp['ln_g'][:, None, :], (L, 128, D)))
    sh['lnb_rep'] = np.ascontiguousarray(np.broadcast_to(inp['ln_b'][:, None, :], (L, 128, D)))
    rows = []
    for g in range(2):
        rows += list(range(192 * g, 192 * g + 192))
        rows += list(range(384 + 192 * g, 384 + 192 * g + 192))
        rows += list(range(768 + 128 * g, 768 + 128 * g + 128))
    sh['w_out'] = np.ascontiguousarray(inp['w_out'][:, np.array(rows), :])
    return sh


def prep_group(inp, g):
    L = DEPTH
    cols, conv_ch = group_cols(g)
    o = {}
    o['w_in'] = inp['w_in'][:, :, cols]
    cw = np.zeros((L, 128, 5, 4), np.float32)
    for ti, ch in enumerate(conv_ch):
        cw[:, :len(ch), ti, :] = np.transpose(inp['conv_w'][:, :, ch], (0, 2, 1))
    o['convw'] = cw
    rgw = np.zeros((L, 128, 2, 2, 192), np.float32)
    rgb = np.zeros((L, 128, 2, 2, 3), np.float32)
    for d in range(2):
        for ai, (wn, bn) in enumerate([('rg_wa', 'rg_ba'), ('rg_wx', 'rg_bx')]):
            W = inp[wn][:, d]
            rgw[:, 0:64, d, ai, 0:64] = W[:, 3 * g]
            rgw[:, 64:128, d, ai, 64:128] = W[:, 3 * g + 1]
            rgw[:, 0:64, d, ai, 128:192] = W[:, 3 * g + 2]
            bvec = inp[bn][:, d, 192 * g:192 * g + 192]
            rgb[:, :, 0, d, ai] = bvec[:, 0:128]
            rgb[:, 0:64, 1, d, ai] = bvec[:, 128:192]
        lam = inp['rg_lam'][:, d, 192 * g:192 * g + 192]
        rgb[:, :, 0, d, 2] = lam[:, 0:128]
        rgb[:, 0:64, 1, d, 2] = lam[:, 128:192]
    o['rgw'] = rgw
    o['rgb'] = rgb
    nab = np.zeros((L, 3, 128, 5, 640), np.float32)
    for l in range(L):
        for h in range(3):
            nab[l, h] = make_nab(inp['na_rpb'][l, 3 * g + h])
    o['nab'] = nab
    al = np.zeros((L, 128, NT, 4), np.float32)
    dt = np.zeros((L, 128, NT, 4), np.float32)
    for d in range(2):
        for h in range(2):
            al[:, :, :, 2 * d + h] = inp['gdn_alog'][:, d, 2 * g + h][:, None, None]
            dt[:, :, :, 2 * d + h] = inp['gdn_dtb'][:, d, 2 * g + h][:, None, None]
    o['galog'] = al
    o['gdtb'] = dt
    return o


def core_inputs(inp, sh, groups, b, x_rows):
    m = dict(sh)
    gs = [prep_group(inp, g) for g in groups]
    m['w_in'] = np.ascontiguousarray(np.concatenate([q['w_in'] for q in gs], axis=2))
    for k in ['convw', 'rgw', 'rgb', 'nab', 'galog', 'gdtb']:
        m[k] = np.ascontiguousarray(np.stack([q[k] for q in gs], axis=1))
    m['x_in'] = np.ascontiguousarray(x_rows)
    cc = np.stack([inp['c'][b], inp['c_ctx']], axis=-1)
    m['cc'] = np.ascontiguousarray(cc.reshape(KC, 128, 2).transpose(1, 0, 2))
    return m


MODE = 'B'
_PROG_CACHE = {}


def _prog(key, **kw):
    if key not in _PROG_CACHE:
        _PROG_CACHE[key] = Prog(**kw)
    return _PROG_CACHE[key]


def kernel_unfused(inp):
    nb = inp['x'].shape[0]
    sh = prep_shared(inp, 1)
    ncore = 2 * nb
    x_rows = [np.concatenate([inp['ctx'][b], inp['x'][b]], 0) for b in range(nb)]
    base = [core_inputs(inp, sh, [c % 2], c // 2, x_rows[c // 2]) for c in range(ncore)]
    yc_full = None
    out = None
    for k in range(DEPTH + 1):
        if k == 0:
            P = _prog(('A', 0), G=1, steps=[('front', 0)], x_ext_out=True, final=False)
        elif k < DEPTH:
            P = _prog(('A', k), G=1, steps=[('back', k - 1), ('front', k)], x_ext_out=True, final=False)
        else:
            P = _prog(('A', k), G=1, steps=[('back', DEPTH - 1)], x_ext_out=True, final=True)
        maps = []
        for c in range(ncore):
            m = dict(base[c])
            m['x_in'] = x_rows[c // 2]
            if k > 0:
                m['yc_in'] = yc_full[c // 2]
            maps.append(m)
        res = run_bass_kernel_spmd(P.nc, maps, core_ids=list(range(ncore)))
        rs = res.results
        if k < DEPTH:
            yc_full = [np.ascontiguousarray(np.concatenate([np.asarray(rs[2 * b]['yc_out']),
                                                            np.asarray(rs[2 * b + 1]['yc_out'])], 0))
                       for b in range(nb)]
        if 0 < k < DEPTH:
            x_rows = [np.asarray(rs[2 * b]['x_out']) for b in range(nb)]
        if k == DEPTH:
            out = np.stack([np.asarray(rs[2 * b]['out']) for b in range(nb)], 0)
    return out.astype(np.float32)


def kernel_fused(inp):
    nb = inp['x'].shape[0]
    sh = prep_shared(inp, 2)
    steps = []
    for l in range(DEPTH):
        steps += [('front', l), ('back', l)]
    P = _prog(('B',), G=2, steps=steps, x_ext_out=False, final=True)
    maps = []
    for c in range(8):
        b = c % nb
        x_rows = np.concatenate([inp['ctx'][b], inp['x'][b]], 0)
        maps.append(core_inputs(inp, sh, [0, 1], b, x_rows))
    res = run_bass_kernel_spmd(P.nc, maps, core_ids=list(range(8)))
    out = np.stack([np.asarray(res.results[b]['out']) for b in range(nb)], 0)
    return out.astype(np.float32)


def kernel(**inputs):
    inp = {k: np.asarray(v) for k, v in inputs.items()}
    if MODE == 'A':
        return kernel_unfused(inp)
    return kernel_fused(inp)
```

```python
from contextlib import ExitStack
import numpy as np
import concourse.bass as bass
import concourse.mybir as mybir
from concourse.bass_utils import run_bass_kernel_spmd

F32 = mybir.dt.float32
BF16 = mybir.dt.bfloat16
AF = mybir.ActivationFunctionType
ALU = mybir.AluOpType
AX = mybir.AxisListType

ENGS = ['pe', 'act', 'dve', 'pool', 'sp']
N_DMA_SEMS = 24


_UNIQ = [0]


def _sbt(nc, name, shape, dt):
    _UNIQ[0] += 1
    return nc.sbuf_tensor("%s_u%d" % (name, _UNIQ[0]), list(shape), dt)


class Builder:
    def __init__(self, nc):
        self.nc = nc
        self.stack = ExitStack()
        self.ops = {e: [] for e in ENGS}
        self.sem = {}
        self.cnt = {}
        self.waited = {e: {} for e in ENGS}
        self.last_w = {}
        self.readers = {}
        for e in ENGS:
            self.sem[e] = self.stack.enter_context(nc.semaphore('sem_' + e))
            self.cnt[e] = 0
        for k in range(N_DMA_SEMS):
            key = ('dma', k)
            self.sem[key] = self.stack.enter_context(nc.semaphore('sem_dma%d' % k))
            self.cnt[key] = 0
        self.dma_rr = 0
        self.n_ops = 0
        self.pending = {e: [] for e in ENGS}

    def sb(self, name, shape, dtype):
        return self.stack.enter_context(_sbt(self.nc, name, list(shape), dtype))

    def ps(self, name, shape, dtype):
        return self.stack.enter_context(self.nc.psum_tensor(name, list(shape), dtype))

    def _deps(self, eng, reads, writes):
        toks = []
        for r in reads:
            t = self.last_w.get(r)
            if t is not None:
                toks.append((t, False))
        for w in writes:
            t = self.last_w.get(w)
            if t is not None:
                toks.append((t, False))
            for sk, (val, peng) in self.readers.get(w, {}).items():
                toks.append(((sk, val, peng), True))
        waits = []
        for (sk, val, peng), is_war in toks:
            if peng == eng:
                if eng == 'pe' or is_war:
                    continue
            if self.waited[eng].get(sk, 0) >= val:
                continue
            self.waited[eng][sk] = val
            waits.append((sk, val))
        best = {}
        for sk, val in waits:
            best[sk] = max(best.get(sk, 0), val)
        return list(best.items())

    def _commit(self, tok, reads, writes):
        for w in writes:
            self.last_w[w] = tok
            self.readers[w] = {}
        for r in reads:
            d = self.readers.setdefault(r, {})
            sk, val, peng = tok
            if d.get(sk, (0, None))[0] < val:
                d[sk] = (val, peng)

    def barrier(self):
        for e in ENGS:
            for k, v in self.cnt.items():
                if v > 0 and self.waited[e].get(k, 0) < v:
                    self.waited[e][k] = v
                    self.pending[e].append((k, v))

    def _take_pending(self, eng, waits):
        if self.pending[eng]:
            best = dict(waits)
            for k, v in self.pending[eng]:
                best[k] = max(best.get(k, 0), v)
            self.pending[eng] = []
            return list(best.items())
        return waits

    def op(self, eng, fn, reads=(), writes=()):
        waits = self._take_pending(eng, self._deps(eng, reads, writes))
        self.cnt[eng] += 1
        tok = (eng, self.cnt[eng], eng)
        self.ops[eng].append((waits, fn, eng, 1))
        self._commit(tok, reads, writes)
        self.n_ops += 1

    def dma(self, q, out, in_, reads=(), writes=(), fn=None, **kw):
        k = self.dma_rr
        self.dma_rr = (self.dma_rr + 1) % N_DMA_SEMS
        key = ('dma', k)
        waits = self._deps(q, reads, writes)
        if self.cnt[key] > 0 and self.waited[q].get(key, 0) < self.cnt[key]:
            self.waited[q][key] = self.cnt[key]
            waits = [w for w in waits if w[0] != key] + [(key, self.cnt[key])]
        waits = self._take_pending(q, waits)
        self.cnt[key] += 16
        tok = (key, self.cnt[key], 'dma')
        if fn is None:
            fn = lambda e: e.dma_start(out=out, in_=in_, **kw)
        self.ops[q].append((waits, fn, key, 16))
        self._commit(tok, reads, writes)
        self.n_ops += 1

    def finish(self):
        nc = self.nc
        fin = [(k, v) for k, v in self.cnt.items() if v > 0]
        ops = self.ops
        sem = self.sem

        def replay(name):
            def run(e):
                for waits, fn, sk, inc in ops[name]:
                    for wk, wv in waits:
                        e.wait_ge(sem[wk], wv)
                    ins = fn(e)
                    ins.then_inc(sem[sk], inc)
                if name == 'sp':
                    for k, v in fin:
                        e.wait_ge(sem[k], v)
            return run

        with nc.Block() as block:
            block.tensor(replay('pe'))
            block.scalar(replay('act'))
            block.vector(replay('dve'))
            block.gpsimd(replay('pool'))
            block.sync(replay('sp'))
        self.stack.close()


DEPTH = 4
D = 1024
KC = 8
T = 4352
NT = 34
GCOLS = 1672
ALPHA = (2.0 * DEPTH) ** 0.25
LN_EPS = 1e-5
NORM_EPS = 1e-6
BIG = 30000.0
BLOCKS = [(0, 256)] + [(256 + 512 * i, 512) for i in range(8)]
PADW = 4360
C_ID, C_ONE, C_BONE, C_RM, C_UF, C_UB, C_SF, C_SB, C_MF, C_MB = [128 * i for i in range(10)]
NCONST = 1280
ONLY = {'rg', 'na', 'gdn'}


def pcol(t):
    return t + 2 if t < 256 else t + 6


class E:
    def __init__(self, B):
        self.B = B

    def mm(self, out, lhsT, rhs, st, sp, r, w):
        self.B.op('pe', lambda e: e.matmul(out, lhsT=lhsT, rhs=rhs, start=st, stop=sp), r, w)

    def tr(self, out, in_, ident, r, w):
        self.B.op('pe', lambda e: e.transpose(out=out, in_=in_, identity=ident), r, w)

    def act(self, out, in_, func, r, w, bias=None, scale=None, accum=None):
        kw = {}
        if bias is not None:
            kw['bias'] = bias
        if scale is not None:
            kw['scale'] = scale
        if accum is not None:
            kw['accum_out'] = accum
        self.B.op('act', lambda e: e.activation(out=out, in_=in_, func=func, **kw), r, w)

    def tt(self, eng, out, in0, in1, op, r, w):
        self.B.op(eng, lambda e: e.tensor_tensor(out=out, in0=in0, in1=in1, op=op), r, w)

    def ts(self, eng, out, in0, s1, s2, op0, op1, r, w):
        if op1 is None:
            self.B.op(eng, lambda e: e.tensor_scalar(out=out, in0=in0, scalar1=s1, scalar2=None, op0=op0), r, w)
        else:
            self.B.op(eng, lambda e: e.tensor_scalar(out=out, in0=in0, scalar1=s1, scalar2=s2, op0=op0, op1=op1), r, w)

    def stt(self, out, in0, sc, in1, op0, op1, r, w):
        self.B.op('dve', lambda e: e.scalar_tensor_tensor(out=out, in0=in0, scalar=sc, in1=in1, op0=op0, op1=op1), r, w)

    def cp(self, eng, out, in_, r, w):
        if eng == 'act':
            self.B.op('act', lambda e: e.activation(out=out, in_=in_, func=AF.Copy), r, w)
        else:
            self.B.op(eng, lambda e: e.tensor_copy(out=out, in_=in_), r, w)

    def red(self, out, in_, op, r, w, negate=False):
        self.B.op('dve', lambda e: e.tensor_reduce(out=out, in_=in_, axis=AX.X, op=op, negate=negate), r, w)

    def recip(self, out, in_, r, w):
        self.B.op('dve', lambda e: e.reciprocal(out=out, in_=in_), r, w)

    def memset(self, eng, ap, val, w):
        self.B.op(eng, lambda e: e.memset(ap, val), (), w)

    def scan(self, out, d0, d1, init, r, w):
        self.B.op('dve', lambda e: e.tensor_tensor_scan(out=out, data0=d0, data1=d1, initial=init,
                                                        op0=ALU.mult, op1=ALU.add), r, w)


def rev(ap):
    return ap[:, ::-1]


class Prog:
    def __init__(self, G, steps, x_ext_out, final, debug=False):
        self.G = G
        self.steps = steps
        self.final = final
        nc = self.nc = bass.Bass("TRN2", target_bir_lowering=False)
        self.B = B = Builder(nc)
        self.e = E(B)
        self.debug = debug
        self.dbg_outs = {}
        L = DEPTH

        def din(name, shape, dt=F32):
            return nc.dram_tensor(name, list(shape), dt, kind="ExternalInput").ap()

        def dout(name, shape, dt=F32):
            return nc.dram_tensor(name, list(shape), dt, kind="ExternalOutput").ap()

        def dint(name, shape, dt=F32):
            return nc.dram_tensor(name, list(shape), dt, kind="Internal").ap()

        self.dout = dout
        self.x_in = din("x_in", [T, D])
        self.cc = din("cc", [128, KC, 2])
        self.consts_d = din("consts", [128, NCONST])
        self.w_mod = din("w_mod", [L, D, 3 * D])
        self.bmod_rep = din("bmod_rep", [L, 128, 3 * D])
        self.w_in = din("w_in", [L, D, G * GCOLS])
        self.convw = din("convw", [L, G, 128, 5, 4])
        self.rgw = din("rgw", [L, G, 128, 2, 2, 192])
        self.rgb = din("rgb", [L, G, 128, 2, 2, 3])
        self.nab = din("nab", [L, G, 3, 128, 5, 640])
        self.galog = din("galog", [L, G, 128, NT, 4])
        self.gdtb = din("gdtb", [L, G, 128, NT, 4])
        self.gnw = din("gnw", [L, 128, 128])
        self.rope_cos = din("rope_cos", [128, 4096])
        self.rope_sin = din("rope_sin", [128, 4096])
        self.w_out = din("w_out", [L, D, D])
        self.lng_rep = din("lng_rep", [L, 128, D])
        self.lnb_rep = din("lnb_rep", [L, 128, D])
        has_front = any(s[0] == 'front' for s in steps)
        has_back = any(s[0] == 'back' for s in steps)
        self.uT_d = dint("uT_d", [128, KC, T], BF16)
        if steps[0][0] == 'back':
            self.yc_in = din("yc_in", [D, T], BF16)
        else:
            self.yc_in = None
        if has_front and x_ext_out:
            self.yc_out = dout("yc_out", [512 * G, T], BF16)
        else:
            self.yc_out = None
        self.yc_int = dint("yc_int", [D, T], BF16) if not x_ext_out else None
        self.x_ext_out = x_ext_out
        if x_ext_out and has_back and not final:
            self.x_out = dout("x_out", [T, D])
        else:
            self.x_out = None
        self.x_scr = [dint("x_scr0", [T, D]), dint("x_scr1", [T, D])] if not x_ext_out else None
        self.out_d = dout("out", [4096, D]) if final else None

        self.banks = B.ps("banks", [128, 8, 512], F32)
        self.cst = B.sb("cst", [128, NCONST], F32)
        self.identb = B.sb("identb", [128, 128], BF16)
        self.screp = B.sb("screp", [128, 2, KC, 128], F32)
        self.modcol = B.sb("modcol", [128, 16, 2], F32)
        self.gate_bc = B.sb("gate_bc", [128, 2, D], F32)
        self.build()

    def c(self, off, n=128, p0=0, p1=128):
        return self.cst[p0:p1, off:off + n]

    def bk(self, i):
        return ('bk', i)

    def dump(self, name, ap, shape, reads, dt=F32):
        if not self.debug:
            return
        d = self.nc.dram_tensor("dbg_" + name, list(shape), dt, kind="ExternalOutput").ap()
        self.dbg_outs[name] = d
        self.B.dma('pool', d, ap, reads=reads, writes=[('dbg', name)])

    def build(self):
        B, e = self.B, self.e
        B.dma('sp', self.cst[:], self.consts_d, writes=['cst'])
        e.cp('dve', self.identb[:], self.c(C_ID), ['cst'], ['identb'])
        with ExitStack() as ph:
            cct = ph.enter_context(_sbt(self.nc, "cct", [128, KC, 2], F32))
            sct = ph.enter_context(_sbt(self.nc, "sct", [128, KC, 2], F32))
            B.dma('sp', cct[:], self.cc, writes=['cct'])
            e.act(sct[:], cct[:], AF.Silu, ['cct'], ['sct'])
            for j in range(2):
                for kc in range(KC):
                    e.ts('dve', self.screp[:, j, kc, :], self.c(C_ONE), sct[:, kc, j:j + 1], None, ALU.mult, None,
                         ['cst', 'sct'], ['screp'])
            B.barrier()
        x_cur = self.x_in
        nxt = 0
        for kind, l in self.steps:
            last = (l == DEPTH - 1)
            if kind == 'front':
                yc = self.yc_out if self.x_ext_out else self.yc_int
                self.mod_phase(l)
                self.u_phase(l, x_cur)
                for g in range(self.G):
                    row0 = 512 * g
                    if 'rg' in ONLY:
                        self.rg_pass(l, g, yc, row0)
                    if 'na' in ONLY:
                        self.na_pass(l, g, yc, row0 + 192, ctx_out=not last)
                    if 'gdn' in ONLY:
                        self.gdn_pass(l, g, yc, row0 + 384)
            else:
                if self.yc_in is not None and (kind, l) == self.steps[0]:
                    yc = self.yc_in
                    self.mod_phase(l, gate_only=True)
                else:
                    yc = self.yc_int
                if last:
                    self.back_phase(l, x_cur, yc, self.out_d, True)
                else:
                    if self.x_ext_out:
                        xn = self.x_out
                    else:
                        xn = self.x_scr[nxt]
                        nxt ^= 1
                    self.back_phase(l, x_cur, yc, xn, False)
                    x_cur = xn
        B.finish()

    def mod_phase(self, l, gate_only=False):
        B, e, nc = self.B, self.e, self.nc
        with ExitStack() as ph:
            wm = [ph.enter_context(_sbt(nc, "wm%d" % i, [128, 3 * D], F32)) for i in range(2)]
            modbc = ph.enter_context(_sbt(nc, "modbc", [128, 3 * D], F32))
            bmod = ph.enter_context(_sbt(nc, "bmod", [128, 3 * D], F32))
            junk = ph.enter_context(_sbt(nc, "junk", [128, 128], F32))
            B.dma('sp', bmod[:], self.bmod_rep[l], writes=['bmod'])
            for j in range(2):
                for kc in range(KC):
                    b = kc % 2
                    B.dma('sp', wm[b][:], self.w_mod[l, kc * 128:(kc + 1) * 128, :], writes=[('wm', b)])
                    for nb in range(6):
                        e.mm(self.banks[:, nb, :], self.screp[:, j, kc, :], wm[b][:, nb * 512:(nb + 1) * 512],
                             kc == 0, kc == KC - 1, [('wm', b), 'screp'], [self.bk(nb)])
                for nb in range(6):
                    e.tt('dve', modbc[:, nb * 512:(nb + 1) * 512], self.banks[:, nb, :],
                         bmod[:, nb * 512:(nb + 1) * 512], ALU.add, ['bmod'], ['modbc', self.bk(nb)])
                if not gate_only:
                    for ch in range(16):
                        e.tt('dve', junk[:], modbc[:, ch * 128:(ch + 1) * 128], self.c(C_ID), ALU.mult,
                             ['modbc', 'cst'], ['junk'])
                        e.red(self.modcol[:, ch, j:j + 1], junk[:], ALU.add, ['junk'], ['modcol'])
                e.cp('pool', self.gate_bc[:, j, :], modbc[:, 2 * D:3 * D], ['modbc'], ['gate_bc'])
            if not gate_only:
                e.ts('dve', self.modcol[:, 8:16, :], self.modcol[:, 8:16, :], 1.0, None, ALU.add, None,
                     ['modcol'], ['modcol'])
            B.barrier()

    def u_phase(self, l, x_cur):
        B, e, nc = self.B, self.e, self.nc
        with ExitStack() as ph:
            xt = [ph.enter_context(_sbt(nc, "u_xt%d" % i, [128, D], F32)) for i in range(2)]
            ut = [ph.enter_context(_sbt(nc, "u_ut%d" % i, [128, KC, 128], BF16)) for i in range(2)]
            for ti in range(NT):
                j = 1 if ti < 2 else 0
                b = ti % 2
                B.dma('sp', xt[b][:], x_cur[ti * 128:(ti + 1) * 128, :], reads=['X'], writes=[('u_xt', b)])
                for kc in range(KC):
                    bank = 2 * b + kc // 4
                    e.tr(self.banks[:, bank, (kc % 4) * 128:(kc % 4 + 1) * 128], xt[b][:, kc * 128:(kc + 1) * 128],
                         self.c(C_ID), [('u_xt', b), 'cst'], [self.bk(bank)])
                for kc in range(KC):
                    bank = 2 * b + kc // 4
                    src = self.banks[:, bank, (kc % 4) * 128:(kc % 4 + 1) * 128]
                    if kc % 2 == 0:
                        e.ts('dve', ut[b][:, kc, :], src, self.modcol[:, 8 + kc, j:j + 1], self.modcol[:, kc, j:j + 1],
                             ALU.mult, ALU.add, ['modcol'], [('u_ut', b), self.bk(bank)])
                    else:
                        e.act(ut[b][:, kc, :], src, AF.Identity, ['modcol'], [('u_ut', b), self.bk(bank)],
                              bias=self.modcol[:, kc, j:j + 1], scale=self.modcol[:, 8 + kc, j:j + 1])
                B.dma('pool', self.uT_d[:, :, ti * 128:(ti + 1) * 128], ut[b][:], reads=[('u_ut', b)], writes=[('uT', ti)])
            B.barrier()

    def load_w(self, ph, l, g, col0, ncols, tag):
        B, e, nc = self.B, self.e, self.nc
        wsb = ph.enter_context(_sbt(nc, "wsb_" + tag, [128, KC, ncols], BF16))
        stg = [ph.enter_context(_sbt(nc, "wstg%d_%s" % (i, tag), [128, ncols], F32)) for i in range(2)]
        c0 = g * GCOLS + col0
        for kc in range(KC):
            b = kc % 2
            B.dma('sp', stg[b][:], self.w_in[l, kc * 128:(kc + 1) * 128, c0:c0 + ncols], writes=[('wstg', b)])
            e.cp('pool', wsb[:, kc, :], stg[b][:], [('wstg', b)], ['wsb'])
        return wsb

    def proj(self, ph, wsb, fm_tiles, tm_ranges, fm_sink, tm_sink):
        B, e, nc = self.B, self.e, self.nc
        ub = [ph.enter_context(_sbt(nc, "ub%d" % i, [128, KC, 512], BF16)) for i in range(2)]
        rr = 0
        for bi, (s, n) in enumerate(BLOCKS):
            b = bi % 2
            B.dma('sp', ub[b][:, :, 0:n], self.uT_d[:, :, s:s + n], reads=['uT'], writes=[('ub', b)])
            for idx, (c0, M) in enumerate(fm_tiles):
                bank = rr % 4
                rr += 1
                for kc in range(KC):
                    e.mm(self.banks[0:M, bank, 0:n], wsb[:, kc, c0:c0 + M], ub[b][:, kc, 0:n], kc == 0, kc == KC - 1,
                         ['wsb', ('ub', b)], [self.bk(bank)])
                fm_sink(idx, s, n, self.banks[0:M, bank, 0:n], self.bk(bank))
            for tl in range(n // 128):
                ti = s // 128 + tl
                for idx, (c0, W) in enumerate(tm_ranges):
                    bank = rr % 4
                    rr += 1
                    for kc in range(KC):
                        e.mm(self.banks[:, bank, 0:W], ub[b][:, kc, tl * 128:(tl + 1) * 128], wsb[:, kc, c0:c0 + W],
                             kc == 0, kc == KC - 1, ['wsb', ('ub', b)], [self.bk(bank)])
                    tm_sink(idx, ti, self.banks[:, bank, 0:W], self.bk(bank))

    def conv(self, P, out, cw, tile, np_, r, w):
        e = self.e
        for (s, n) in [(0, 256), (256, 4096)]:
            base = pcol(s) - 2
            o = out[0:np_, s:s + n]
            e.ts('dve', o, P[0:np_, base:base + n], cw[0:np_, tile, 0:1], None, ALU.mult, None, r, w)
            for j in range(1, 4):
                e.stt(o, P[0:np_, base + j:base + j + n], cw[0:np_, tile, j:j + 1], o, ALU.mult, ALU.add, r, w)

    def zero_pads(self, P, key):
        e = self.e
        for a, b in [(0, 2), (258, 262), (4358, 4360)]:
            e.memset('pool', P[:, a:b], 0.0, [key])

    def rg_pass(self, l, g, yc, row0):
        B, e, nc = self.B, self.e, self.nc
        with ExitStack() as p0:
            xa = [p0.enter_context(_sbt(nc, "rg_xa%d" % i, [128, T], F32)) for i in range(2)]
            zs = [p0.enter_context(_sbt(nc, "rg_zs%d" % i, [128, T], BF16)) for i in range(2)]
            cw = p0.enter_context(_sbt(nc, "rg_cw", [128, 5, 4], F32))
            B.dma('sp', cw[:], self.convw[l, g], writes=['cw'])
            NP = [128, 64]
            with ExitStack() as ph:
                wsb = self.load_w(ph, l, g, 0, 384, "rg")
                P = [ph.enter_context(_sbt(nc, "rg_P%d" % i, [128, PADW], F32)) for i in range(2)]
                for i in range(2):
                    self.zero_pads(P[i], ('rgP', i))

                def fm_sink(idx, s, n, ps, bkey):
                    if idx < 2:
                        e.cp('act', P[idx][0:NP[idx], pcol(s):pcol(s) + n], ps, [], [bkey, ('rgP', idx)])
                    else:
                        i = idx - 2
                        e.act(zs[i][0:NP[i], s:s + n], ps, AF.Silu, [], [bkey, ('rg_zs', i)])

                self.proj(ph, wsb, [(0, 128), (128, 64), (192, 128), (320, 64)], [], fm_sink, None)
                for i in range(2):
                    self.conv(P[i], xa[i], cw, i, NP[i], [('rgP', i), 'cw'], [('rg_xa', i)])
                B.barrier()
            if self.debug:
                self.dump("rg_xa0_%d_%d" % (l, g), xa[0][:], [128, T], [('rg_xa', 0)])
            with ExitStack() as ph:
                hh = [[ph.enter_context(_sbt(nc, "rg_h%d_%d" % (d, i), [128, T], F32)) for i in range(2)]
                      for d in range(2)]
                w = ph.enter_context(_sbt(nc, "rg_w", [128, 2, 2, 192], F32))
                bb = ph.enter_context(_sbt(nc, "rg_b", [128, 2, 2, 3], F32))
                c8 = ph.enter_context(_sbt(nc, "rg_c8", [128, 2, 2], F32))
                tmp = {}
                for d in range(2):
                    for i in range(2):
                        for nm in ['r', 'i', 's']:
                            tmp[(nm, d, i)] = ph.enter_context(_sbt(nc, "rg_t%s%d%d" % (nm, d, i), [128, 512], F32))
                B.dma('sp', w[:], self.rgw[l, g], writes=['rg_w'])
                B.dma('sp', bb[:], self.rgb[l, g], writes=['rg_b'])
                e.act(c8[:], bb[:, :, :, 2], AF.Exp, ['rg_b'], ['rg_c8'], scale=-1.0)
                e.act(c8[:], c8[:], AF.Ln, ['rg_c8'], ['rg_c8'], bias=1.0)
                e.ts('dve', c8[:], c8[:], -8.0, None, ALU.mult, None, ['rg_c8'], ['rg_c8'])
                wcols = [(0, 128), (128, 192)]
                orders = {0: BLOCKS, 1: [BLOCKS[0]] + BLOCKS[:0:-1]}

                def chain(d, i):
                    np_ = NP[i]
                    prev = None
                    ci = 2 * d + i
                    b0, b1 = 2 * ci, 2 * ci + 1
                    c0, c1 = wcols[i]
                    kh_ = ('rg_h', d, i)
                    for (s, n) in orders[d]:
                        tr_, ti_, ts_ = (tmp[(nm, d, i)][0:np_, 0:n] for nm in ['r', 'i', 's'])
                        kr, ki, ks = (('rg_t', nm, d, i) for nm in ['r', 'i', 's'])
                        xin = xa[i][0:np_, s:s + n]
                        e.mm(self.banks[0:np_, b0, 0:n], w[0:np_, d, 0, c0:c1], xin, True, True,
                             ['rg_w', ('rg_xa', i)], [self.bk(b0)])
                        e.mm(self.banks[0:np_, b1, 0:n], w[0:np_, d, 1, c0:c1], xin, True, True,
                             ['rg_w', ('rg_xa', i)], [self.bk(b1)])
                        yield
                        e.act(tr_, self.banks[0:np_, b0, 0:n], AF.Sigmoid, ['rg_b'], [kr, self.bk(b0)],
                              bias=bb[0:np_, i, d, 0:1])
                        e.act(ti_, self.banks[0:np_, b1, 0:n], AF.Sigmoid, ['rg_b'], [ki, self.bk(b1)],
                              bias=bb[0:np_, i, d, 1:2])
                        yield
                        e.act(tr_, tr_, AF.Exp, [kr, 'rg_c8'], [kr], scale=c8[0:np_, i, d:d + 1])
                        yield
                        e.tt('pool', ts_, tr_, tr_, ALU.mult, [kr], [ks])
                        yield
                        e.act(ts_, ts_, AF.Sqrt, [ks], [ks], bias=1.0, scale=-1.0)
                        yield
                        e.tt('dve', ti_, ts_, ti_, ALU.mult, [ks, ki], [ki])
                        yield
                        e.tt('pool', ti_, ti_, xin, ALU.mult, [ki, ('rg_xa', i)], [ki])
                        yield
                        init = prev if prev is not None else 0.0
                        dst = hh[d][i][0:np_, s:s + n]
                        if d == 1:
                            e.scan(rev(dst), rev(tr_), rev(ti_), init, [kr, ki, kh_], [kh_])
                            prev = hh[d][i][0:np_, s:s + 1]
                        else:
                            e.scan(dst, tr_, ti_, init, [kr, ki, kh_], [kh_])
                            prev = hh[d][i][0:np_, s + n - 1:s + n]
                        yield

                active = [chain(d, i) for d in range(2) for i in range(2)]
                while active:
                    nxt = []
                    for gnr in active:
                        try:
                            next(gnr)
                            nxt.append(gnr)
                        except StopIteration:
                            pass
                    active = nxt
                for i in range(2):
                    np_ = NP[i]
                    hf, hb = hh[0][i][0:np_, :], hh[1][i][0:np_, :]
                    for (s, n) in [(0, 2176), (2176, 2176)]:
                        e.tt('pool', hf[:, s:s + n], hf[:, s:s + n], hb[:, s:s + n], ALU.add,
                             [('rg_h', 0, i), ('rg_h', 1, i)], [('rg_h', 0, i)])
                        e.tt('dve', zs[i][0:np_, s:s + n], hf[:, s:s + n], zs[i][0:np_, s:s + n], ALU.mult,
                             [('rg_h', 0, i), ('rg_zs', i)], [('rg_zs', i)])
                    r0 = row0 + 128 * i
                    B.dma('pool', yc[r0:r0 + np_, :], zs[i][0:np_, :], reads=[('rg_zs', i)], writes=[('YC', self.B.n_ops)])
                B.barrier()

    def na_pass(self, l, g, yc, row0, ctx_out):
        B, e, nc = self.B, self.e, self.nc
        bnk = self.banks
        NP = [128, 64]
        with ExitStack() as p0:
            qT = [p0.enter_context(_sbt(nc, "na_q%d" % i, [128, T], BF16)) for i in range(2)]
            kT = [p0.enter_context(_sbt(nc, "na_k%d" % i, [128, T], BF16)) for i in range(2)]
            zn = p0.enter_context(_sbt(nc, "na_zn", [128, NT, 192], BF16))
            v_tm = p0.enter_context(_sbt(nc, "na_v", [128, NT, 192], BF16))
            with ExitStack() as ph:
                wsb = self.load_w(ph, l, g, 384, 768, "na")

                def fm_sink(idx, s, n, ps, bkey):
                    i = idx % 2
                    if idx < 2:
                        e.act(qT[i][0:NP[i], s:s + n], ps, AF.Copy, [], [bkey, 'na_q'], scale=0.125)
                    else:
                        e.cp('dve', kT[i][0:NP[i], s:s + n], ps, [], [bkey, 'na_k'])

                def tm_sink(idx, ti, ps, bkey):
                    e.act(zn[:, ti, :], ps[:, 0:192], AF.Silu, [], [bkey, 'na_zn'])
                    e.cp('dve', v_tm[:, ti, :], ps[:, 192:384], [], [bkey, 'na_v'])

                self.proj(ph, wsb, [(0, 128), (128, 64), (192, 128), (320, 64)], [(384, 384)], fm_sink, tm_sink)
                B.barrier()
            with ExitStack() as ph:
                def sbt(name, shape, dt=F32):
                    return ph.enter_context(_sbt(nc, name, shape, dt))
                nab = sbt("na_bias", [128, 3, 5, 640])
                NB = 2
                S = [[sbt("na_S%d_%d" % (p, h), [128, 896]) for h in range(3)] for p in range(NB)]
                Pb = [[sbt("na_Pb%d_%d" % (p, h), [128, 896], BF16) for h in range(3)] for p in range(NB)]
                PT = [[sbt("na_PT%d_%d" % (p, h), [128, 896], BF16) for h in range(3)] for p in range(NB)]
                st = [[sbt("na_st%d_%d" % (p, h), [128, 4]) for h in range(3)] for p in range(NB)]
                ytm = [sbt("na_ytm%d" % p, [128, 192], BF16) for p in range(NB)]
                ybT = [sbt("na_ybT%d" % p, [128, 256], BF16) for p in range(NB)]
                for h in range(3):
                    B.dma('sp', nab[:, h], self.nab[l, g, h], writes=['na_bias'])

                def qt_gen(it, qt, lat):
                    p = it % NB
                    if lat:
                        cls = {0: 0, 1: 1, 30: 3, 31: 4}.get(qt, 2)
                        kt0 = min(max(qt - 2, 0), 27)
                        tq = 2 + qt
                        kcol = (2 + kt0) * 128
                        ktiles = [2 + kt0 + c for c in range(5)] + [0, 1]
                        nk = 896
                    else:
                        tq = qt
                        ktiles = [0, 1]
                        nk = 256
                    qcol = tq * 128
                    nch = nk // 128
                    hs = range(3)
                    HR = [(0, 0), (0, 64), (1, 0)]
                    kS = [('na_S', p, h) for h in hs]
                    kP = [('na_Pb', p, h) for h in hs]
                    kPT = [('na_PT', p, h) for h in hs]
                    kst = [('na_st', p, h) for h in hs]
                    for h in hs:
                        ti_, hr0 = HR[h]
                        qa = qT[ti_][hr0:hr0 + 64, qcol:qcol + 128]
                        k_ = kT[ti_]
                        b0, b1 = 2 * h, 2 * h + 1
                        if lat:
                            e.mm(bnk[:, b0, 0:512], qa, k_[hr0:hr0 + 64, kcol:kcol + 512], True, True,
                                 ['na_q', 'na_k'], [self.bk(b0)])
                            e.mm(bnk[:, b1, 0:128], qa, k_[hr0:hr0 + 64, kcol + 512:kcol + 640], True, True,
                                 ['na_q', 'na_k'], [self.bk(b1)])
                            e.mm(bnk[:, b1, 128:384], qa, k_[hr0:hr0 + 64, 0:256], True, True,
                                 ['na_q', 'na_k'], [self.bk(b1)])
                        else:
                            e.mm(bnk[:, b1, 128:384], qa, k_[hr0:hr0 + 64, 0:256], True, True,
                                 ['na_q', 'na_k'], [self.bk(b1)])
                    yield
                    for h in hs:
                        b0, b1 = 2 * h, 2 * h + 1
                        if lat:
                            e.tt('dve', S[p][h][:, 0:512], bnk[:, b0, :], nab[:, h, cls, 0:512], ALU.add,
                                 ['na_bias'], [kS[h], self.bk(b0)])
                            e.tt('dve', S[p][h][:, 512:640], bnk[:, b1, 0:128], nab[:, h, cls, 512:640], ALU.add,
                                 ['na_bias'], [kS[h], self.bk(b1)])
                            e.cp('act', S[p][h][:, 640:896], bnk[:, b1, 128:384], [], [kS[h], self.bk(b1)])
                        else:
                            e.cp('act', S[p][h][:, 0:256], bnk[:, b1, 128:384], [], [kS[h], self.bk(b1)])
                    yield
                    for h in hs:
                        e.red(st[p][h][:, 0:1], S[p][h][:, 0:nk], ALU.max, [kS[h]], [kst[h]], negate=True)
                    yield
                    for h in hs:
                        e.act(Pb[p][h][:, 0:nk], S[p][h][:, 0:nk], AF.Exp, [kS[h], kst[h]], [kP[h], (kst[h], 'sum')],
                              bias=st[p][h][:, 0:1], scale=1.0, accum=st[p][h][:, 1:2])
                    yield
                    for h in hs:
                        pbf = bnk[:, 2 * h, :].bitcast(BF16)
                        for c in range(nch):
                            e.tr(pbf[:, c * 128:(c + 1) * 128], Pb[p][h][:, c * 128:(c + 1) * 128], self.identb[:],
                                 [kP[h], 'identb'], [self.bk(2 * h)])
                        e.recip(st[p][h][:, 2:3], st[p][h][:, 1:2], [(kst[h], 'sum')], [(kst[h], 'ri')])
                    yield
                    for h in hs:
                        pbf = bnk[:, 2 * h, :].bitcast(BF16)
                        e.cp('act' if h != 1 else 'dve', PT[p][h][:, 0:nk], pbf[:, 0:nk], [], [kPT[h], self.bk(2 * h)])
                    yield
                    for h in hs:
                        for c in range(nch):
                            e.mm(bnk[:, 6, h * 64:(h + 1) * 64], PT[p][h][:, c * 128:(c + 1) * 128],
                                 v_tm[:, ktiles[c], h * 64:(h + 1) * 64], c == 0, c == nch - 1,
                                 ['na_v', kPT[h]], [self.bk(6)])
                    yield
                    for h in hs:
                        e.stt(ytm[p][:, h * 64:(h + 1) * 64], bnk[:, 6, h * 64:(h + 1) * 64], st[p][h][:, 2:3],
                              zn[:, tq, h * 64:(h + 1) * 64], ALU.mult, ALU.mult, [(kst[h], 'ri'), 'na_zn'],
                              [('na_ytm', p), self.bk(6)])
                    yield
                    pb7 = bnk[:, 7, :].bitcast(BF16)
                    e.tr(pb7[:, 0:128], ytm[p][:, 0:128], self.identb[:], [('na_ytm', p), 'identb'], [self.bk(7)])
                    e.tr(pb7[0:64, 128:256], ytm[p][:, 128:192], self.identb[:], [('na_ytm', p), 'identb'], [self.bk(7)])
                    yield
                    e.cp('act', ybT[p][:, 0:128], pb7[:, 0:128], [], [('na_ybT', p), self.bk(7)])
                    e.cp('dve', ybT[p][0:64, 128:256], pb7[0:64, 128:256], [], [('na_ybT', p), self.bk(7)])
                    yield
                    B.dma('pool', yc[row0:row0 + 128, qcol:qcol + 128], ybT[p][:, 0:128], reads=[('na_ybT', p)],
                          writes=[('YC', self.B.n_ops)])
                    B.dma('pool', yc[row0 + 128:row0 + 192, qcol:qcol + 128], ybT[p][0:64, 128:256],
                          reads=[('na_ybT', p)], writes=[('YC', self.B.n_ops)])
                    yield

                qts = [(qt, True) for qt in range(32)]
                if ctx_out:
                    qts += [(0, False), (1, False)]
                pending = [qt_gen(i, qt, lat) for i, (qt, lat) in enumerate(qts)]
                active = []
                rounds = 0
                while pending or active:
                    if pending and (rounds % 5 == 0) and len(active) < NB:
                        active.append(pending.pop(0))
                    nxt = []
                    for gnr in active:
                        try:
                            next(gnr)
                            nxt.append(gnr)
                        except StopIteration:
                            pass
                    active = nxt
                    rounds += 1
                B.barrier()

    def gdn_pass(self, l, g, yc, row0):
        B, e, nc = self.B, self.e, self.nc
        bnk = self.banks
        with ExitStack() as p0:
            qT = p0.enter_context(_sbt(nc, "gd_q", [128, T], F32))
            kT = p0.enter_context(_sbt(nc, "gd_k", [128, T], F32))
            zg = p0.enter_context(_sbt(nc, "gd_zg", [128, NT, 128], BF16))
            ba = p0.enter_context(_sbt(nc, "gd_ba", [128, NT, 8], F32))
            vT = p0.enter_context(_sbt(nc, "gd_v", [128, T], F32))
            cw = p0.enter_context(_sbt(nc, "gd_cw", [128, 5, 4], F32))
            with ExitStack() as p1:
                B.dma('sp', cw[:], self.convw[l, g], writes=['cw'])
                with ExitStack() as ph:
                    wsb = self.load_w(ph, l, g, 1152, 520, "gd")
                    P = [ph.enter_context(_sbt(nc, "gd_P%d" % i, [128, PADW], F32)) for i in range(3)]
                    for i in range(3):
                        self.zero_pads(P[i], ('gdP', i))

                    def fm_sink(idx, s, n, ps, bkey):
                        e.cp('act', P[idx][:, pcol(s):pcol(s) + n], ps, [], [bkey, ('gdP', idx)])

                    def tm_sink(idx, ti, ps, bkey):
                        e.act(zg[:, ti, :], ps[:, 0:128], AF.Silu, [], [bkey, 'gd_zg'])
                        e.cp('dve', ba[:, ti, :], ps[:, 128:136], [], [bkey, 'gd_ba'])

                    self.proj(ph, wsb, [(0, 128), (128, 128), (256, 128)], [(384, 136)], fm_sink, tm_sink)
                    for i, dst in enumerate([qT, kT, vT]):
                        self.conv(P[i], dst, cw, 2 + i, 128, [('gdP', i), 'cw'], [('gd_c', i)])
                    B.barrier()
                kv = p1.enter_context(_sbt(nc, "gd_kv", [128, NT, 256], F32))
                with ExitStack() as ph:
                    sq = [[ph.enter_context(_sbt(nc, "gd_sq%d%d" % (i, k), [128, 512], F32)) for k in range(2)] for i in range(2)]
                    nr = [[ph.enter_context(_sbt(nc, "gd_nr%d%d" % (i, k), [128, 512], F32)) for k in range(2)] for i in range(2)]
                    t1 = [[ph.enter_context(_sbt(nc, "gd_t1%d%d" % (i, k), [128, 512], F32)) for k in range(2)] for i in range(2)]
                    cs = [ph.enter_context(_sbt(nc, "gd_cs%d" % i, [128, 512], F32)) for i in range(2)]
                    sn = [ph.enter_context(_sbt(nc, "gd_sn%d" % i, [128, 512], F32)) for i in range(2)]

                    def blk_gen(bi, s, n):
                        lat = s >= 256
                        pb = bi % 2
                        if lat:
                            B.dma('sp', cs[pb][:], self.rope_cos[:, s - 256:s - 256 + 512], writes=[('gd_cs', pb)])
                            B.dma('sp', sn[pb][:], self.rope_sin[:, s - 256:s - 256 + 512], writes=[('gd_sn', pb)])
                        Xs = [qT[:, s:s + n], kT[:, s:s + n]]
                        kx = [('gd_cb', 0, bi), ('gd_cb', 1, bi)]
                        ksq = [('gd_sq', wi, pb) for wi in range(2)]
                        knr = [('gd_nr', wi, pb) for wi in range(2)]
                        kt1 = [('gd_t1', wi, pb) for wi in range(2)]
                        bb0 = [2 * (2 * pb + wi) for wi in range(2)]
                        for wi in range(2):
                            e.act(Xs[wi], Xs[wi], AF.Silu, [kx[wi]], [kx[wi]])
                        e.act(vT[:, s:s + n], vT[:, s:s + n], AF.Silu, [('gd_vb', bi)], [('gd_vb', bi)])
                        yield
                        for wi in range(2):
                            e.tt('pool', sq[wi][pb][:, 0:n], Xs[wi], Xs[wi], ALU.mult, [kx[wi]], [ksq[wi]])
                        yield
                        for wi in range(2):
                            e.mm(bnk[:, bb0[wi], 0:n], self.c(C_BONE), sq[wi][pb][:, 0:n], True, True, [ksq[wi], 'cst'],
                                 [self.bk(bb0[wi])])
                        yield
                        for wi in range(2):
                            e.act(nr[wi][pb][:, 0:n], bnk[:, bb0[wi], 0:n], AF.Sqrt, [], [knr[wi], self.bk(bb0[wi])],
                                  bias=NORM_EPS)
                        yield
                        for wi in range(2):
                            e.recip(nr[wi][pb][:, 0:n], nr[wi][pb][:, 0:n], [knr[wi]], [knr[wi]])
                        yield
                        e.stt(Xs[0], Xs[0], 0.125, nr[0][pb][:, 0:n], ALU.mult, ALU.mult, [kx[0], knr[0]], [kx[0]])
                        e.tt('dve', Xs[1], Xs[1], nr[1][pb][:, 0:n], ALU.mult, [kx[1], knr[1]], [kx[1]])
                        yield
                        if lat:
                            for wi in range(2):
                                b1 = bb0[wi] + 1
                                e.mm(bnk[:, b1, 0:n], self.c(C_RM), Xs[wi], True, True, [kx[wi], 'cst'], [self.bk(b1)])
                                e.tt('pool', t1[wi][pb][:, 0:n], Xs[wi], cs[pb][:, 0:n], ALU.mult, [kx[wi], ('gd_cs', pb)],
                                     [kt1[wi]])
                            yield
                            for wi in range(2):
                                b1 = bb0[wi] + 1
                                e.tt('dve', Xs[wi], bnk[:, b1, 0:n], sn[pb][:, 0:n], ALU.mult, [('gd_sn', pb)],
                                     [kx[wi], self.bk(b1)])
                            yield
                            for wi in range(2):
                                e.tt('pool', Xs[wi], Xs[wi], t1[wi][pb][:, 0:n], ALU.add, [kx[wi], kt1[wi]], [kx[wi]])
                            yield

                    pend = [blk_gen(bi, s, n) for bi, (s, n) in enumerate(BLOCKS)]
                    active = []
                    rounds = 0
                    while pend or active:
                        if pend and rounds % 4 == 0 and len(active) < 2:
                            active.append(pend.pop(0))
                        nxt = []
                        for gnr in active:
                            try:
                                next(gnr)
                                nxt.append(gnr)
                            except StopIteration:
                                pass
                        active = nxt
                        rounds += 1
                    for ti in range(NT):
                        b0 = ti % 4
                        blk = 0 if ti < 2 else 1 + (ti - 2) // 4
                        e.tr(bnk[:, b0, 0:128], kT[:, ti * 128:(ti + 1) * 128], self.c(C_ID), [('gd_cb', 1, blk), 'cst'],
                             [self.bk(b0)])
                        e.tr(bnk[:, b0, 128:256], vT[:, ti * 128:(ti + 1) * 128], self.c(C_ID), [('gd_vb', blk), 'cst'],
                             [self.bk(b0)])
                        e.cp('act' if ti % 2 else 'dve', kv[:, ti, :], bnk[:, b0, 0:256], [], ['gd_kv', self.bk(b0)])
                    B.barrier()
                if self.debug:
                    self.dump("gd_q_%d_%d" % (l, g), qT[:], [128, T], [('gd_c', 0)])
                    self.dump("gd_k_%d_%d" % (l, g), kT[:], [128, T], [('gd_c', 1)])
                    self.dump("gd_kv_%d_%d" % (l, g), kv[:], [128, NT, 256], ['gd_kv'])
                with ExitStack() as ph:
                    def sbt(name, shape, dt=F32):
                        return ph.enter_context(_sbt(nc, name, shape, dt))
                    o_tm = [sbt("gd_o0", [128, NT, 128]), vT[:].rearrange("p (t c) -> p t c", c=128)]
                    al = sbt("gd_al", [128, NT, 4])
                    dtb = sbt("gd_dtb", [128, NT, 4])
                    g_all = sbt("gd_g", [128, NT, 4])
                    beta = sbt("gd_beta", [128, NT, 4])
                    nbeta = sbt("gd_nbeta", [128, NT, 4])
                    tg = sbt("gd_tg", [128, NT, 4])
                    B.dma('sp', al[:], self.galog[l, g], writes=['gd_al'])
                    B.dma('sp', dtb[:], self.gdtb[l, g], writes=['gd_dtb'])
                    e.act(beta[:], ba[:, :, 0:4], AF.Sigmoid, ['gd_ba'], ['gd_beta'])
                    e.ts('dve', nbeta[:], beta[:], -1.0, None, ALU.mult, None, ['gd_beta'], ['gd_nbeta'])
                    e.tt('dve', tg[:], ba[:, :, 4:8], dtb[:], ALU.add, ['gd_ba', 'gd_dtb'], ['gd_tg'])
                    e.act(tg[:], tg[:], AF.Exp, ['gd_tg'], ['gd_tg'])
                    e.act(tg[:], tg[:], AF.Ln, ['gd_tg'], ['gd_tg'], bias=1.0)
                    e.act(al[:], al[:], AF.Exp, ['gd_al'], ['gd_al'])
                    e.stt(g_all[:], tg[:], -1.0, al[:], ALU.mult, ALU.mult, ['gd_tg', 'gd_al'], ['gd_g'])
                    if self.debug:
                        self.dump("gd_g_%d_%d" % (l, g), g_all[:], [128, NT, 4], ['gd_g'])
                        self.dump("gd_beta_%d_%d" % (l, g), beta[:], [128, NT, 4], ['gd_beta'])
                    gc = sbt("gd_gc", [128, NT, 4])
                    gl = sbt("gd_gl", [128, NT, 4])
                    eg = sbt("gd_eg", [128, NT, 4])
                    neg_eg = sbt("gd_negeg", [128, NT, 4])
                    negc = sbt("gd_negc", [128, NT, 4])
                    egl = sbt("gd_egl", [128, NT, 4])
                    eglast = sbt("gd_eglast", [128, NT, 4])
                    gflat = g_all[:].rearrange("p t c -> p (t c)")
                    NV = NT * 4
                    e.mm(bnk[:, 0, 0:NV], self.c(C_UF), gflat, True, True, ['cst', 'gd_g'], [self.bk(0)])
                    e.mm(bnk[:, 0, NV:2 * NV], self.c(C_UB), gflat, True, True, ['cst', 'gd_g'], [self.bk(0)])
                    e.mm(bnk[:, 0, 2 * NV:3 * NV], self.c(C_ONE), gflat, True, True, ['cst', 'gd_g'], [self.bk(0)])
                    pv0 = bnk[:, 0, 0:NV].rearrange("p (t c) -> p t c", c=4)
                    pv1 = bnk[:, 0, NV:2 * NV].rearrange("p (t c) -> p t c", c=4)
                    pv2 = bnk[:, 0, 2 * NV:3 * NV].rearrange("p (t c) -> p t c", c=4)
                    e.cp('dve', gc[:, :, 0:2], pv0[:, :, 0:2], [], ['gd_gc', self.bk(0)])
                    e.cp('dve', gc[:, :, 2:4], pv1[:, :, 2:4], [], ['gd_gc', self.bk(0)])
                    e.cp('dve', gl[:], pv2, [], ['gd_gl', self.bk(0)])
                    e.act(eg[:], gc[:], AF.Exp, ['gd_gc'], ['gd_vecs'])
                    e.ts('pool', neg_eg[:], eg[:], -1.0, None, ALU.mult, None, ['gd_vecs'], ['gd_vecs2'])
                    e.ts('pool', negc[:], gc[:], -1.0, None, ALU.mult, None, ['gd_gc'], ['gd_negc'])
                    e.tt('pool', egl[:], gl[:], gc[:], ALU.subtract, ['gd_gc', 'gd_gl'], ['gd_egl'])
                    e.act(egl[:], egl[:], AF.Exp, ['gd_egl'], ['gd_egl'])
                    e.act(eglast[:], gl[:], AF.Exp, ['gd_gl'], ['gd_eglast'])
                    R = 3
                    chains = [(d, h) for d in range(2) for h in range(2)]
                    T2t = {c: [sbt("gd_T2t%d%d_%d" % (c[0], c[1], r), [128, 128]) for r in range(R)] for c in chains}
                    QKt = {c: [sbt("gd_QKt%d%d_%d" % (c[0], c[1], r), [128, 128]) for r in range(R)] for c in chains}
                    vec = {c: [sbt("gd_vec%d%d_%d" % (c[0], c[1], r), [128, 4]) for r in range(R)] for c in chains}
                    Sst = {c: sbt("gd_S%d%d" % c, [128, 64]) for c in chains}
                    Rp = {c: sbt("gd_Rp%d%d" % c, [128, 64]) for c in chains}
                    qs = {c: sbt("gd_qs%d%d" % c, [128, 64]) for c in chains}
                    vnw = {c: sbt("gd_vn%d%d" % c, [128, 64]) for c in chains}
                    kh = {c: sbt("gd_kh%d%d" % c, [128, 64]) for c in chains}
                    Gm = [sbt("gd_G%d" % i, [128, 128]) for i in range(4)]
                    nG = [sbt("gd_nG%d" % i, [128, 128]) for i in range(4)]
                    Et = [sbt("gd_Et%d" % i, [128, 128]) for i in range(4)]
                    tv = [sbt("gd_tv%d" % i, [128, 2]) for i in range(4)]
                    YT = [[sbt("gd_YT%d_%d" % (i, k), [128, 256]) for k in range(2)] for i in range(4)]
                    Yt = [[sbt("gd_Yt%d_%d" % (i, k), [128, 128]) for k in range(2)] for i in range(4)]
                    for c in chains:
                        e.memset('pool', Sst[c][:], 0.0, [('gd_S', c)])
                    ident = self.c(C_ID)
                    ones = self.c(C_ONE)
                    orders = {0: list(range(NT)), 1: [1, 0] + list(range(NT - 1, 1, -1))}

                    def prep(c, ti, slot):
                        d, h = c
                        col = 2 * d + h
                        pp = chains.index(c)
                        U = self.c(C_UF if d == 0 else C_UB)
                        Sm = self.c(C_SF if d == 0 else C_SB)
                        Mk = self.c(C_MF if d == 0 else C_MB)
                        hr0 = 64 * h
                        tc0 = ti * 128
                        gcol = g_all[:, ti, col:col + 1]
                        kG, knG, kEt, ktv = ('gd_G', pp), ('gd_nG', pp), ('gd_Et', pp), ('gd_tv', pp)
                        kslot = ('gd_slot', c, slot)
                        bP = pp
                        kb = self.bk(bP)
                        pD = bnk[:, bP, 0:128]
                        pV = bnk[:, bP, 128:130]
                        pKK = bnk[:, bP, 132:260]
                        pQK = bnk[:, bP, 260:388]
                        pI = bnk[:, bP, 0:256]
                        pJ = bnk[:, bP, 256:384]
                        e.ts('pool', Gm[pp][:], U, gcol, None, ALU.mult, None, ['cst', 'gd_g'], [kG])
                        yield
                        e.mm(pD, ones, Gm[pp][:], True, True, ['cst', kG], [kb])
                        ka = kT[hr0:hr0 + 64, tc0:tc0 + 128]
                        qa = qT[hr0:hr0 + 64, tc0:tc0 + 128]
                        e.mm(pKK, ka, ka, True, True, [('gd_c', 1)], [kb])
                        e.mm(pQK, ka, qa, True, True, [('gd_c', 1), ('gd_c', 0)], [kb])
                        yield
                        e.stt(nG[pp][:], pD, negc[:, ti, col:col + 1], Mk, ALU.add, ALU.add, ['gd_negc', 'cst'], [knG, kb])
                        yield
                        e.act(Et[pp][:], nG[pp][:], AF.Exp, [knG], [kEt])
                        yield
                        vc = vec[c][slot]
                        Y0 = YT[pp][0]
                        kYT = [('gd_YT', pp, 0), ('gd_YT', pp, 1)]
                        kYt = [('gd_Yt', pp, 0), ('gd_Yt', pp, 1)]
                        e.stt(Y0[:, 0:128], pKK, nbeta[:, ti, col:col + 1], Et[pp][:], ALU.mult, ALU.mult,
                              ['gd_nbeta', kEt], [kYT[0], kb])
                        e.tt('dve', QKt[c][slot][:], pQK, Et[pp][:], ALU.mult, [kEt], [(kslot, 'QK'), kb])
                        yield
                        e.tt('pool', Y0[:, 0:128], Y0[:, 0:128], Sm, ALU.mult, [kYT[0], 'cst'], [kYT[0]])
                        yield
                        e.tr(pJ, Y0[:, 0:128], ident, [kYT[0], 'cst'], [kb])
                        e.tt('pool', Y0[:, 128:256], Y0[:, 0:128], ident, ALU.add, [kYT[0], 'cst'], [kYT[0]])
                        yield
                        e.cp('act', Yt[pp][0][:], pJ, [], [kYt[0], kb])
                        yield
                        e.mm(bnk[:, bP, 0:128], Yt[pp][0][:], Y0[:, 0:128], True, True, [kYt[0], kYT[0]], [kb])
                        e.mm(pJ, Y0[:, 0:128], Yt[pp][0][:], True, True, [kYt[0], kYT[0]], [kb])
                        yield
                        e.cp('act', YT[pp][1][:, 0:128], bnk[:, bP, 0:128], [], [kYT[1], kb])
                        e.cp('dve', Yt[pp][1][:], pJ, [], [kYt[1], kb])
                        e.cp('pool', YT[pp][1][:, 128:256], Y0[:, 128:256], [kYT[0]], [kYT[1]])
                        yield
                        for q in range(1, 7):
                            a = q % 2
                            nb_ = 1 - a
                            cur, curt = YT[pp][a], Yt[pp][a]
                            nx, nxt_ = YT[pp][nb_], Yt[pp][nb_]
                            if q < 6:
                                e.mm(pI, curt[:], cur[:, 0:256], True, True, [kYt[a], kYT[a]], [kb])
                                e.mm(pJ, cur[:, 0:128], curt[:], True, True, [kYt[a], kYT[a]], [kb])
                                yield
                                e.cp('act', nx[:, 0:128], bnk[:, bP, 0:128], [], [kYT[nb_], kb])
                                e.tt('dve', nx[:, 128:256], bnk[:, bP, 128:256], cur[:, 128:256], ALU.add, [kYT[a]],
                                     [kYT[nb_], kb])
                                e.cp('act' if q % 2 else 'dve', nxt_[:], pJ, [], [kYt[nb_], kb])
                                yield
                            else:
                                e.mm(bnk[:, bP, 0:128], curt[:], cur[:, 128:256], True, True, [kYt[a], kYT[a]], [kb])
                                yield
                                e.tt('dve', T2t[c][slot][:], bnk[:, bP, 0:128], cur[:, 128:256], ALU.add, [kYT[a]],
                                     [(kslot, 'T'), kb])
                                yield

                    def seq_stage(stage, c, ti, slot):
                        d, h = c
                        col = 2 * d + h
                        hr0 = 64 * h
                        tc0 = ti * 128
                        ci = chains.index(c)
                        bC = 4 + ci
                        kslot = ('gd_slot', c, slot)
                        vc = vec[c][slot]
                        kS = ('gd_S', c)
                        if stage == 0:
                            e.mm(bnk[:, bC, 0:64], kT[hr0:hr0 + 64, tc0:tc0 + 128], Sst[c][hr0:hr0 + 64, :], True, True,
                                 [('gd_c', 1), kS], [self.bk(bC)])
                            e.mm(bnk[:, bC, 64:128], qT[hr0:hr0 + 64, tc0:tc0 + 128], Sst[c][hr0:hr0 + 64, :], True, True,
                                 [('gd_c', 0), kS], [self.bk(bC)])
                            e.ts('pool', kh[c][:], kv[:, ti, hr0:hr0 + 64], egl[:, ti, col:col + 1], None, ALU.mult, None,
                                 ['gd_kv', 'gd_egl'], [('gd_kh', c)])
                        elif stage == 1:
                            e.stt(Rp[c][:], bnk[:, bC, 0:64], neg_eg[:, ti, col:col + 1], kv[:, ti, 128 + hr0:128 + hr0 + 64],
                                  ALU.mult, ALU.add, ['gd_vecs2', 'gd_kv'], [('gd_Rp', c), self.bk(bC)])
                            e.act(qs[c][:], bnk[:, bC, 64:128], AF.Identity, ['gd_vecs'], [('gd_qs', c), self.bk(bC)],
                                  scale=eg[:, ti, col:col + 1])
                        elif stage == 2:
                            e.mm(bnk[:, bC, 128:192], T2t[c][slot][:], Rp[c][:], True, True, [(kslot, 'T'), ('gd_Rp', c)],
                                 [self.bk(bC)])
                        elif stage == 3:
                            e.act(vnw[c][:], bnk[:, bC, 128:192], AF.Identity, ['gd_beta'], [('gd_vn', c), self.bk(bC)],
                                  scale=beta[:, ti, col:col + 1])
                        elif stage == 4:
                            e.mm(bnk[:, bC, 192:256], QKt[c][slot][:], vnw[c][:], True, True, [(kslot, 'QK'), ('gd_vn', c)],
                                 [self.bk(bC)])
                            e.mm(bnk[hr0:hr0 + 64, bC, 256:320], kh[c][:], vnw[c][:], True, True,
                                 [('gd_kh', c), ('gd_vn', c)], [self.bk(bC)])
                        else:
                            e.tt('dve', o_tm[d][:, ti, hr0:hr0 + 64], bnk[:, bC, 192:256], qs[c][:], ALU.add,
                                 [('gd_qs', c)], [('gd_o', d), self.bk(bC)])
                            e.stt(Sst[c][hr0:hr0 + 64, :], Sst[c][hr0:hr0 + 64, :], eglast[hr0:hr0 + 64, ti, col:col + 1],
                                  bnk[hr0:hr0 + 64, bC, 256:320], ALU.mult, ALU.add, [kS, 'gd_eglast'], [kS, self.bk(bC)])

                    def run_wave(preps, seqs, ratio=4):
                        active = list(preps)
                        sq = list(seqs)
                        k = 0
                        while active or sq:
                            nxt_active = []
                            for gnr in active:
                                try:
                                    next(gnr)
                                    nxt_active.append(gnr)
                                except StopIteration:
                                    pass
                            active = nxt_active
                            k += 1
                            if sq and (k % ratio == 0 or not active):
                                nsq = []
                                for gnr in sq:
                                    try:
                                        next(gnr)
                                        nsq.append(gnr)
                                    except StopIteration:
                                        pass
                                sq = nsq

                    def seq_gen(c, ti, slot):
                        for stage in range(6):
                            seq_stage(stage, c, ti, slot)
                            yield

                    run_wave([prep(c, orders[c[0]][0], 0) for c in chains], [])
                    for si in range(NT):
                        preps = []
                        if si + 1 < NT:
                            preps = [prep(c, orders[c[0]][si + 1], (si + 1) % R) for c in chains]
                        seqs = [seq_gen(c, orders[c[0]][si], si % R) for c in chains]
                        run_wave(preps, seqs)
                    if self.debug:
                        self.dump("gd_o0_%d_%d" % (l, g), o_tm[0][:], [128, NT, 128], [('gd_o', 0)])
                        self.dump("gd_o1_%d_%d" % (l, g), o_tm[1][:], [128, NT, 128], [('gd_o', 1)])
                    nw = sbt("gd_nw", [128, 128])
                    ss = sbt("gd_ss", [128, NT * 2])
                    yb = [sbt("gd_yb%d" % i, [128, 512], BF16) for i in range(2)]
                    B.dma('sp', nw[:], self.gnw[l], writes=['gd_nw'])
                    O = o_tm[0]
                    O1 = o_tm[1]
                    Of = O[:].rearrange("p t c -> p (t c)")
                    O1f = O1[:].rearrange("p t c -> p (t c)")
                    e.tt('pool', Of, Of, O1f, ALU.add, [('gd_o', 0), ('gd_o', 1)], [('gd_o', 0)])
                    e.tt('dve', O1f, Of, Of, ALU.mult, [('gd_o', 0)], [('gd_o', 1)])
                    e.red(ss[:], O1[:].rearrange("p t (h c) -> p (t h) c", c=64), ALU.add, [('gd_o', 1)], ['gd_ss'])
                    e.ts('dve', ss[:], ss[:], 1.0 / 64.0, NORM_EPS, ALU.mult, ALU.add, ['gd_ss'], ['gd_ss'])
                    e.act(ss[:], ss[:], AF.Sqrt, ['gd_ss'], ['gd_ss'])
                    e.recip(ss[:], ss[:], ['gd_ss'], ['gd_ss'])
                    O3 = O[:].rearrange("p t (h c) -> p (t h) c", c=64)
                    e.tt('dve', O3, O3, ss[:].unsqueeze(2).to_broadcast([128, NT * 2, 64]), ALU.mult,
                         [('gd_o', 0), 'gd_ss'], [('gd_o', 0)])
                    e.tt('pool', O[:], O[:], nw[:].unsqueeze(1).to_broadcast([128, NT, 128]), ALU.mult,
                         [('gd_o', 0), 'gd_nw'], [('gd_o', 0)])
                    e.tt('dve', O[:], O[:], zg[:], ALU.mult, [('gd_o', 0), 'gd_zg'], [('gd_o', 0)])
                    groups = [[0, 1]] + [list(range(2 + 4 * i, 6 + 4 * i)) for i in range(8)]
                    for gi, tl in enumerate(groups):
                        pb = gi % 2
                        bO = gi % 4
                        for k, ti in enumerate(tl):
                            e.tr(bnk[:, bO, k * 128:(k + 1) * 128], O[:, ti, :], ident, [('gd_o', 0), 'cst'], [self.bk(bO)])
                        n = 128 * len(tl)
                        e.cp('act' if gi % 2 else 'dve', yb[pb][:, 0:n], bnk[:, bO, 0:n], [], [('gd_yb', pb), self.bk(bO)])
                        B.dma('pool', yc[row0:row0 + 128, tl[0] * 128:tl[0] * 128 + n], yb[pb][:, 0:n],
                              reads=[('gd_yb', pb)], writes=[('YC', self.B.n_ops)])
                    B.barrier()

    def back_phase(self, l, x_cur, yc, x_next, last):
        B, e, nc = self.B, self.e, self.nc
        bnk = self.banks
        with ExitStack() as ph:
            def sbt(name, shape, dt=F32):
                return ph.enter_context(_sbt(nc, name, shape, dt))
            wo = sbt("bk_wo", [128, KC, D], BF16)
            stg = [sbt("bk_stg%d" % i, [128, D]) for i in range(2)]
            ycb = [sbt("bk_yc%d" % i, [128, KC, 128], BF16) for i in range(2)]
            xt = [sbt("bk_xt%d" % i, [128, D]) for i in range(2)]
            tb = [sbt("bk_tb%d" % i, [128, D]) for i in range(2)]
            zb = [sbt("bk_zb%d" % i, [128, D]) for i in range(2)]
            lng = sbt("bk_lng", [128, D])
            lnb = sbt("bk_lnb", [128, D])
            st6 = [sbt("bk_st6%d" % i, [128, 2, 6]) for i in range(2)]
            mv = [sbt("bk_mv%d" % i, [128, 4]) for i in range(2)]
            B.dma('sp', lng[:], self.lng_rep[l], writes=['bk_lng'])
            B.dma('sp', lnb[:], self.lnb_rep[l], writes=['bk_lnb'])
            for kc in range(KC):
                b = kc % 2
                B.dma('sp', stg[b][:], self.w_out[l, kc * 128:(kc + 1) * 128, :], writes=[('bk_stg', b)])
                e.cp('pool', wo[:, kc, :], stg[b][:], [('bk_stg', b)], ['bk_wo'])
            ycv = yc.rearrange("(c p) t -> p c t", p=128)
            tiles = list(range(2, NT)) if last else list(range(NT))
            for n_, ti in enumerate(tiles):
                j = 1 if ti < 2 else 0
                b = n_ % 2
                b0 = 2 * b
                B.dma('sp', ycb[b][:], ycv[:, :, ti * 128:(ti + 1) * 128], reads=['YC'], writes=[('bk_yc', b)])
                B.dma('sp', xt[b][:], x_cur[ti * 128:(ti + 1) * 128, :], reads=['X'], writes=[('bk_xt', b)])
                for nb in range(2):
                    for c in range(KC):
                        e.mm(bnk[:, b0 + nb, :], ycb[b][:, c, :], wo[:, c, nb * 512:(nb + 1) * 512], c == 0, c == KC - 1,
                             [('bk_yc', b), 'bk_wo'], [self.bk(b0 + nb)])
                for nb in range(2):
                    e.tt('dve', tb[b][:, nb * 512:(nb + 1) * 512], bnk[:, b0 + nb, :],
                         self.gate_bc[:, j, nb * 512:(nb + 1) * 512], ALU.mult, ['gate_bc'],
                         [('bk_tb', b), self.bk(b0 + nb)])
                e.stt(zb[b][:], xt[b][:], ALPHA, tb[b][:], ALU.mult, ALU.add, [('bk_xt', b), ('bk_tb', b)], [('bk_zb', b)])
                for nb in range(2):
                    B.op('dve', (lambda o_, i_: (lambda en: en.bn_stats(out=o_, in_=i_)))(
                        st6[b][:, nb, :], zb[b][:, nb * 512:(nb + 1) * 512]), [('bk_zb', b)], [('bk_st6', b)])
                B.op('dve', (lambda o_, i_: (lambda en: en.bn_aggr(out=o_, in_=i_)))(
                    mv[b][:, 0:2], st6[b][:].rearrange("p a s -> p (a s)")), [('bk_st6', b)], [('bk_mv', b)])
                e.act(mv[b][:, 2:3], mv[b][:, 1:2], AF.Sqrt, [('bk_mv', b)], [('bk_sd', b)], bias=LN_EPS)
                e.recip(mv[b][:, 3:4], mv[b][:, 2:3], [('bk_sd', b)], [('bk_rs', b)])
                e.ts('dve', zb[b][:], zb[b][:], mv[b][:, 0:1], mv[b][:, 3:4], ALU.subtract, ALU.mult,
                     [('bk_zb', b), ('bk_mv', b), ('bk_rs', b)], [('bk_zb', b)])
                e.tt('pool', zb[b][:], zb[b][:], lng[:], ALU.mult, [('bk_zb', b), 'bk_lng'], [('bk_zb', b)])
                e.tt('pool', zb[b][:], zb[b][:], lnb[:], ALU.add, [('bk_zb', b), 'bk_lnb'], [('bk_zb', b)])
                if last:
                    dst = x_next[(ti - 2) * 128:(ti - 1) * 128, :]
                else:
                    dst = x_next[ti * 128:(ti + 1) * 128, :]
                B.dma('pool', dst, zb[b][:], reads=[('bk_zb', b)], writes=[('Xn', ti)])
            B.barrier()
            B.last_w['X'] = B.last_w.get('Xn')


def group_cols(g):
    R0 = 1152
    rgx = np.arange(192 * g, 192 * g + 192)
    gq = 384 + np.arange(128 * g, 128 * g + 128)
    gk = 640 + np.arange(128 * g, 128 * g + 128)
    gv = 896 + np.arange(128 * g, 128 * g + 128)
    rgg = R0 + np.arange(192 * g, 192 * g + 192)
    naq = R0 + 384 + np.arange(192 * g, 192 * g + 192)
    nak = R0 + 768 + np.arange(192 * g, 192 * g + 192)
    nav = R0 + 1152 + np.arange(192 * g, 192 * g + 192)
    nag = R0 + 1536 + np.arange(192 * g, 192 * g + 192)
    gg = R0 + 1920 + np.arange(128 * g, 128 * g + 128)
    bf = R0 + 2176 + np.arange(2 * g, 2 * g + 2)
    bb = R0 + 2180 + np.arange(2 * g, 2 * g + 2)
    af = R0 + 2184 + np.arange(2 * g, 2 * g + 2)
    ab = R0 + 2188 + np.arange(2 * g, 2 * g + 2)
    cols = np.concatenate([rgx, rgg, naq, nak, nag, nav, gq, gk, gv, gg, bf, bb, af, ab])
    assert cols.shape[0] == GCOLS
    conv_ch = [rgx[:128], rgx[128:], gq, gk, gv]
    return cols, conv_ch


def make_consts():
    c = np.zeros((128, NCONST), np.float32)
    i = np.arange(128)
    c[:, C_ID:C_ID + 128] = np.eye(128)
    c[:, C_ONE:C_ONE + 128] = 1.0
    c[:64, C_BONE:C_BONE + 64] = 1.0
    c[64:, C_BONE + 64:C_BONE + 128] = 1.0
    R = np.zeros((128, 128), np.float32)
    for h in range(2):
        for half in range(2):
            o = 64 * h + 32 * half
            for t in range(16):
                R[o + t, o + t + 16] = -1.0
                R[o + t + 16, o + t] = 1.0
    c[:, C_RM:C_RM + 128] = R.T
    k = i[:, None]
    m = i[None, :]
    c[:, C_UF:C_UF + 128] = (k <= m)
    c[:, C_UB:C_UB + 128] = (k >= m)
    c[:, C_SF:C_SF + 128] = (m > k)
    c[:, C_SB:C_SB + 128] = (m < k)
    c[:, C_MF:C_MF + 128] = np.where(m >= k, 0.0, -BIG)
    c[:, C_MB:C_MB + 128] = np.where(m <= k, 0.0, -BIG)
    return c


def make_rope():
    p = np.arange(128)
    d = p % 64
    half = d // 32
    fi = d % 16
    inv_freq = (np.float32(10000.0) ** (-np.arange(16, dtype=np.float32) / np.float32(16))).astype(np.float32)
    t = np.arange(4096)
    row = (t // 64).astype(np.float32)
    col = (t % 64).astype(np.float32)
    pos = np.where(half[:, None] == 0, row[None, :], col[None, :]).astype(np.float32)
    ang = (pos * inv_freq[fi][:, None]).astype(np.float32)
    return np.cos(ang).astype(np.float32), np.sin(ang).astype(np.float32)


NA_CLASSES = [(0, 0), (1, 0), (2, 0), (30, 27), (31, 27)]


def make_nab(rpb_h):
    out = np.full((128, 5, 640), -BIG, np.float32)
    q = np.arange(128)
    key = np.arange(640)
    for ci, (qt, kt0) in enumerate(NA_CLASSES):
        r = 2 * qt + q // 64
        j = q % 64
        kr = 2 * kt0 + key // 64
        kcn = key % 64
        r0 = np.clip(r - 4, 0, 56)
        c0 = np.clip(j - 8, 0, 48)
        okr = (kr[None, :] >= r0[:, None]) & (kr[None, :] < r0[:, None] + 8)
        okc = (kcn[None, :] >= c0[:, None]) & (kcn[None, :] < c0[:, None] + 16)
        ro = np.clip(kr[None, :] - r[:, None] + 7, 0, 14)
        co = np.clip(kcn[None, :] - j[:, None] + 15, 0, 30)
        vals = rpb_h[ro, co]
        out[:, ci, :] = np.where(okr & okc, vals, np.float32(-BIG))
    return out


def prep_shared(inp, G):
    L = DEPTH
    sh = {}
    sh['consts'] = make_consts()
    cs, sn = make_rope()
    sh['rope_cos'], sh['rope_sin'] = cs, sn
    sh['w_mod'] = np.ascontiguousarray(inp['w_mod'])
    sh['bmod_rep'] = np.ascontiguousarray(np.broadcast_to(inp['b_mod'][:, None, :], (L, 128, 3 * D)))
    sh['gnw'] = np.ascontiguousarray(np.broadcast_to(np.tile(inp['gdn_nw'], (1, 2))[:, None, :], (L, 128, 128)))
    sh['lng_rep'] = np.ascontiguousarray(np.broadcast_to(inp['ln_g'][:, None, :], (L, 128, D)))
    sh['lnb_rep'] = np.ascontiguousarray(np.broadcast_to(inp['ln_b'][:, None, :], (L, 128, D)))
    rows = []
    for g in range(2):
        rows += list(range(192 * g, 192 * g + 192))
        rows += list(range(384 + 192 * g, 384 + 192 * g + 192))
        rows += list(range(768 + 128 * g, 768 + 128 * g + 128))
    sh['w_out'] = np.ascontiguousarray(inp['w_out'][:, np.array(rows), :])
    return sh


def prep_group(inp, g):
    L = DEPTH
    cols, conv_ch = group_cols(g)
    o = {}
    o['w_in'] = inp['w_in'][:, :, cols]
    cw = np.zeros((L, 128, 5, 4), np.float32)
    for ti, ch in enumerate(conv_ch):
        cw[:, :len(ch), ti, :] = np.transpose(inp['conv_w'][:, :, ch], (0, 2, 1))
    o['convw'] = cw
    rgw = np.zeros((L, 128, 2, 2, 192), np.float32)
    rgb = np.zeros((L, 128, 2, 2, 3), np.float32)
    for d in range(2):
        for ai, (wn, bn) in enumerate([('rg_wa', 'rg_ba'), ('rg_wx', 'rg_bx')]):
            W = inp[wn][:, d]
            rgw[:, 0:64, d, ai, 0:64] = W[:, 3 * g]
            rgw[:, 64:128, d, ai, 64:128] = W[:, 3 * g + 1]
            rgw[:, 0:64, d, ai, 128:192] = W[:, 3 * g + 2]
            bvec = inp[bn][:, d, 192 * g:192 * g + 192]
            rgb[:, :, 0, d, ai] = bvec[:, 0:128]
            rgb[:, 0:64, 1, d, ai] = bvec[:, 128:192]
        lam = inp['rg_lam'][:, d, 192 * g:192 * g + 192]
        rgb[:, :, 0, d, 2] = lam[:, 0:128]
        rgb[:, 0:64, 1, d, 2] = lam[:, 128:192]
    o['rgw'] = rgw
    o['rgb'] = rgb
    nab = np.zeros((L, 3, 128, 5, 640), np.float32)
    for l in range(L):
        for h in range(3):
            nab[l, h] = make_nab(inp['na_rpb'][l, 3 * g + h])
    o['nab'] = nab
    al = np.zeros((L, 128, NT, 4), np.float32)
    dt = np.zeros((L, 128, NT, 4), np.float32)
    for d in range(2):
        for h in range(2):
            al[:, :, :, 2 * d + h] = inp['gdn_alog'][:, d, 2 * g + h][:, None, None]
            dt[:, :, :, 2 * d + h] = inp['gdn_dtb'][:, d, 2 * g + h][:, None, None]
    o['galog'] = al
    o['gdtb'] = dt
    return o


def core_inputs(inp, sh, groups, b, x_rows):
    m = dict(sh)
    gs = [prep_group(inp, g) for g in groups]
    m['w_in'] = np.ascontiguousarray(np.concatenate([q['w_in'] for q in gs], axis=2))
    for k in ['convw', 'rgw', 'rgb', 'nab', 'galog', 'gdtb']:
        m[k] = np.ascontiguousarray(np.stack([q[k] for q in gs], axis=1))
    m['x_in'] = np.ascontiguousarray(x_rows)
    cc = np.stack([inp['c'][b], inp['c_ctx']], axis=-1)
    m['cc'] = np.ascontiguousarray(cc.reshape(KC, 128, 2).transpose(1, 0, 2))
    return m


MODE = 'B'
_PROG_CACHE = {}


def _prog(key, **kw):
    if key not in _PROG_CACHE:
        _PROG_CACHE[key] = Prog(**kw)
    return _PROG_CACHE[key]


def kernel_unfused(inp):
    nb = inp['x'].shape[0]
    sh = prep_shared(inp, 1)
    ncore = 2 * nb
    x_rows = [np.concatenate([inp['ctx'][b], inp['x'][b]], 0) for b in range(nb)]
    base = [core_inputs(inp, sh, [c % 2], c // 2, x_rows[c // 2]) for c in range(ncore)]
    yc_full = None
    out = None
    for k in range(DEPTH + 1):
        if k == 0:
            P = _prog(('A', 0), G=1, steps=[('front', 0)], x_ext_out=True, final=False)
        elif k < DEPTH:
            P = _prog(('A', k), G=1, steps=[('back', k - 1), ('front', k)], x_ext_out=True, final=False)
        else:
            P = _prog(('A', k), G=1, steps=[('back', DEPTH - 1)], x_ext_out=True, final=True)
        maps = []
        for c in range(ncore):
            m = dict(base[c])
            m['x_in'] = x_rows[c // 2]
            if k > 0:
                m['yc_in'] = yc_full[c // 2]
            maps.append(m)
        res = run_bass_kernel_spmd(P.nc, maps, core_ids=list(range(ncore)))
        rs = res.results
        if k < DEPTH:
            yc_full = [np.ascontiguousarray(np.concatenate([np.asarray(rs[2 * b]['yc_out']),
                                                            np.asarray(rs[2 * b + 1]['yc_out'])], 0))
                       for b in range(nb)]
        if 0 < k < DEPTH:
            x_rows = [np.asarray(rs[2 * b]['x_out']) for b in range(nb)]
        if k == DEPTH:
            out = np.stack([np.asarray(rs[2 * b]['out']) for b in range(nb)], 0)
    return out.astype(np.float32)


def kernel_fused(inp):
    nb = inp['x'].shape[0]
    sh = prep_shared(inp, 2)
    steps = []
    for l in range(DEPTH):
        steps += [('front', l), ('back', l)]
    P = _prog(('B',), G=2, steps=steps, x_ext_out=False, final=True)
    maps = []
    for c in range(8):
        b = c % nb
        x_rows = np.concatenate([inp['ctx'][b], inp['x'][b]], 0)
        maps.append(core_inputs(inp, sh, [0, 1], b, x_rows))
    res = run_bass_kernel_spmd(P.nc, maps, core_ids=list(range(8)))
    out = np.stack([np.asarray(res.results[b]['out']) for b in range(nb)], 0)
    return out.astype(np.float32)


def kernel(**inputs):
    inp = {k: np.asarray(v) for k, v in inputs.items()}
    if MODE == 'A':
        return kernel_unfused(inp)
    return kernel_fused(inp)
```
